# Optimizing a Trainium2 kernel written in Bass

```python
import jax, jax.numpy as jnp
from jax import lax
import numpy as np

D_MODEL = 2048
BATCH = 16
SEQ = 256
DEPTH = 1
DEC_BATCH = 8
DEC_SEQ = 2048
PAST_LEN = 256

GRID_W = 64
MLSTM_WIDTH = D_MODEL // 2
N_HEADS = 8
HEAD_DIM = MLSTM_WIDTH // N_HEADS
POOL_WIDTH = D_MODEL - MLSTM_WIDTH
POOL_WINDOWS = (2, 4, 8, 16)
N_POOL_GROUPS = len(POOL_WINDOWS)
POOL_GROUP_DIM = POOL_WIDTH // N_POOL_GROUPS
N_GATES = 4
PROJ_WIDTH = 4 * MLSTM_WIDTH + N_GATES * N_HEADS + POOL_WIDTH
D_FF = 4 * D_MODEL
CHUNK = 128
N_MOD = 6
EPS = 1e-6

kernel_name = "hymba_mlstm_pool_flow_step"


def _rmsnorm(x, w):
    xf = x.astype(jnp.float32)
    xf = xf * lax.rsqrt(jnp.mean(xf * xf, axis=-1, keepdims=True) + EPS)
    return (xf * w.astype(jnp.float32)).astype(x.dtype)


def _adaln(cond, ada_w, ada_b):
    mod = jax.nn.silu(cond) @ ada_w + ada_b
    return mod.reshape(cond.shape[0], N_MOD, D_MODEL)


def _mlstm_scan(q, k, v, li, lf, C0, n0, m0):
    B, T, H, Dh = q.shape
    nc = T // CHUNK

    def to_chunks(a):
        a = a.reshape((B, nc, CHUNK, H) + a.shape[3:])
        return jnp.moveaxis(a, (1, 3), (0, 2))

    mask = jnp.tril(jnp.ones((CHUNK, CHUNK), dtype=bool))

    def step(carry, xs):
        C, n, m = carry
        qc, kc, vc, lic, lfc = xs
        b = jnp.cumsum(lfc, axis=-1)
        dmat = jnp.where(mask, b[..., :, None] - b[..., None, :] + lic[..., None, :], -jnp.inf)
        m_inter = b + m[..., None]
        m_t = jnp.maximum(m_inter, jnp.max(dmat, axis=-1))
        s = jnp.einsum('bhtd,bhsd->bhts', qc, kc) * jnp.exp(dmat - m_t[..., None])
        g = jnp.exp(m_inter - m_t)
        num = g[..., None] * jnp.einsum('bhtd,bhde->bhte', qc, C) + jnp.einsum('bhts,bhse->bhte', s, vc)
        den = g * jnp.einsum('bhtd,bhd->bht', qc, n) + jnp.sum(s, axis=-1)
        h = num / jnp.maximum(jnp.abs(den), jnp.exp(-m_t))[..., None]
        b_last = b[..., -1]
        decay = b_last[..., None] - b + lic
        m_new = jnp.maximum(b_last + m, jnp.max(decay, axis=-1))
        wk = kc * jnp.exp(decay - m_new[..., None])[..., None]
        g_state = jnp.exp(b_last + m - m_new)
        C_new = g_state[..., None, None] * C + jnp.einsum('bhsd,bhse->bhde', wk, vc)
        n_new = g_state[..., None] * n + jnp.sum(wk, axis=2)
        return (C_new, n_new, m_new), h

    (C, n, m), h = lax.scan(step, (C0, n0, m0),
                            (to_chunks(q), to_chunks(k), to_chunks(v), to_chunks(li), to_chunks(lf)))
    h = jnp.moveaxis(h, (0, 2), (1, 3)).reshape(B, T, H, Dh)
    return h, C, n, m


def _mlstm_bidir(q, k, v, gates, C0, n0, m0):
    li_f = gates[:, :, 0]
    lf_f = jax.nn.log_sigmoid(gates[:, :, 1])
    li_b = gates[:, :, 2]
    lf_b = jax.nn.log_sigmoid(gates[:, :, 3])
    rev = lambda a: jnp.flip(a, axis=1)
    h_f, Cf, nf, mf = _mlstm_scan(q, k, v, li_f, lf_f, C0[:, 0], n0[:, 0], m0[:, 0])
    h_b, Cb, nb, mb = _mlstm_scan(rev(q), rev(k), rev(v), rev(li_b), rev(lf_b), C0[:, 1], n0[:, 1], m0[:, 1])
    h = h_f + rev(h_b)
    return h, jnp.stack([Cf, Cb], axis=1), jnp.stack([nf, nb], axis=1), jnp.stack([mf, mb], axis=1)


def _centred_pool_minus_self(x, window):
    L = x.shape[-2]
    xf = x.astype(jnp.float32)
    cs = jnp.cumsum(xf, axis=-2)
    cs = jnp.concatenate([jnp.zeros_like(cs[..., :1, :]), cs], axis=-2)
    t = jnp.arange(L)
    lo = jnp.maximum(t - window // 2, 0)
    hi = jnp.minimum(t + window // 2, L)
    total = jnp.take(cs, hi, axis=-2) - jnp.take(cs, lo, axis=-2)
    mean = total / (hi - lo).astype(jnp.float32)[:, None]
    return (mean - xf).astype(x.dtype)


def _pool_mixer(u, pool_w, pool_scale, grid_w):
    B, L, _ = u.shape
    if grid_w is not None:
        rows = L // grid_w
        u = u.reshape(B, rows, grid_w, POOL_WIDTH)
    groups = [_centred_pool_minus_self(u[..., gi * POOL_GROUP_DIM:(gi + 1) * POOL_GROUP_DIM], w)
              for gi, w in enumerate(POOL_WINDOWS)]
    p = jnp.stack(groups, axis=-2)
    y = jnp.einsum('...gc,gcd->...gd', p, pool_w).reshape(B, L, POOL_WIDTH)
    return y * pool_scale


def _layer(x, mod, C0, n0, m0, w_in, gate_bias, mlstm_norm_w, pool_w, pool_scale, w_out, norm_w, w1, w2, grid_w):
    B, T, _ = x.shape
    shift_a, scale_a, gate_a, shift_f, scale_f, gate_f = [mod[:, i][:, None, :] for i in range(N_MOD)]
    h = _rmsnorm(x, norm_w[0]) * (1 + scale_a) + shift_a
    proj = h @ w_in
    M = MLSTM_WIDTH
    q, k, v, o, g, u = jnp.split(proj, [M, 2 * M, 3 * M, 4 * M, 4 * M + N_GATES * N_HEADS], axis=-1)
    f32 = jnp.float32
    q = q.reshape(B, T, N_HEADS, HEAD_DIM).astype(f32) * (HEAD_DIM ** -0.5)
    k = k.reshape(B, T, N_HEADS, HEAD_DIM).astype(f32)
    v = v.reshape(B, T, N_HEADS, HEAD_DIM).astype(f32)
    gates = g.reshape(B, T, N_GATES, N_HEADS).astype(f32) + gate_bias.astype(f32)
    hm, C, n, m = _mlstm_bidir(q, k, v, gates, C0.astype(f32), n0.astype(f32), m0.astype(f32))
    hm = _rmsnorm(hm, mlstm_norm_w.reshape(N_HEADS, HEAD_DIM)).astype(x.dtype)
    hm = hm.reshape(B, T, M) * jax.nn.sigmoid(o)
    hp = _pool_mixer(u, pool_w, pool_scale, grid_w)
    mix = jnp.concatenate([hm, hp], axis=-1) @ w_out
    x = x + gate_a * _rmsnorm(mix, norm_w[1])
    h = _rmsnorm(x, norm_w[2]) * (1 + scale_f) + shift_f
    y = jnp.square(jax.nn.relu(h @ w1)) @ w2
    x = x + gate_f * _rmsnorm(y, norm_w[3])
    return x, C, n, m


def setup_inputs(seed: int = 0) -> dict:
    key = jax.random.key(seed)
    ks = jax.random.split(key, 20)
    nrm = jax.random.normal
    f32 = jnp.float32
    x_prompt = nrm(ks[0], (BATCH, SEQ, D_MODEL), f32)
    x_sample = nrm(ks[1], (DEC_BATCH, DEC_SEQ, D_MODEL), f32)
    c = nrm(ks[2], (DEC_BATCH, D_MODEL), f32)
    state_C = 0.5 * nrm(ks[3], (DEC_BATCH, DEPTH, 2, N_HEADS, HEAD_DIM, HEAD_DIM), f32)
    state_n = 0.5 * nrm(ks[4], (DEC_BATCH, DEPTH, 2, N_HEADS, HEAD_DIM), f32)
    state_m = 0.5 * nrm(ks[5], (DEC_BATCH, DEPTH, 2, N_HEADS), f32)
    c_ctx = nrm(ks[6], (D_MODEL,), f32)
    w_in = nrm(ks[7], (DEPTH, D_MODEL, PROJ_WIDTH), f32) * D_MODEL ** -0.5
    ib = 0.1 * nrm(ks[8], (DEPTH, 2, N_HEADS), f32)
    fb = jnp.linspace(3.0, 6.0, N_HEADS, dtype=f32) + 0.1 * nrm(ks[9], (DEPTH, 2, N_HEADS), f32)
    gate_bias = jnp.stack([ib[:, 0], fb[:, 0], ib[:, 1], fb[:, 1]], axis=1)
    mlstm_norm_w = 1.0 + 0.1 * nrm(ks[10], (DEPTH, MLSTM_WIDTH), f32)
    pool_w = nrm(ks[11], (DEPTH, N_POOL_GROUPS, POOL_GROUP_DIM, POOL_GROUP_DIM), f32) * POOL_GROUP_DIM ** -0.5
    pool_scale = 1.0 + 0.1 * nrm(ks[12], (DEPTH, POOL_WIDTH), f32)
    w_out = nrm(ks[13], (DEPTH, MLSTM_WIDTH + POOL_WIDTH, D_MODEL), f32) * (MLSTM_WIDTH + POOL_WIDTH) ** -0.5
    ada_w = nrm(ks[14], (DEPTH, D_MODEL, N_MOD * D_MODEL), f32) * (0.5 * D_MODEL ** -0.5)
    ada_b = 0.02 * nrm(ks[15], (DEPTH, N_MOD * D_MODEL), f32)
    norm_w = 1.0 + 0.1 * nrm(ks[16], (DEPTH, 4, D_MODEL), f32)
    w1 = nrm(ks[17], (DEPTH, D_MODEL, D_FF), f32) * D_MODEL ** -0.5
    w2 = nrm(ks[18], (DEPTH, D_FF, D_MODEL), f32) * D_FF ** -0.5
    return {"x_prompt": x_prompt, "x_sample": x_sample, "c": c,
            "state_C": state_C, "state_n": state_n, "state_m": state_m,
            "c_ctx": c_ctx, "w_in": w_in, "gate_bias": gate_bias, "mlstm_norm_w": mlstm_norm_w,
            "pool_w": pool_w, "pool_scale": pool_scale, "w_out": w_out,
            "ada_w": ada_w, "ada_b": ada_b, "norm_w": norm_w, "w1": w1, "w2": w2}


def reference(x_prompt, x_sample, c, state_C, state_n, state_m, c_ctx, w_in, gate_bias, mlstm_norm_w,
              pool_w, pool_scale, w_out, ada_w, ada_b, norm_w, w1, w2):
    B = x_prompt.shape[0]
    yp = x_prompt
    ys = x_sample
    new_C, new_n, new_m = [], [], []
    for layer in range(DEPTH):
        params = (w_in[layer], gate_bias[layer], mlstm_norm_w[layer], pool_w[layer], pool_scale[layer],
                  w_out[layer], norm_w[layer], w1[layer], w2[layer])
        mod_ctx = _adaln(c_ctx[None, :], ada_w[layer], ada_b[layer])
        C0 = jnp.zeros((B, 2, N_HEADS, HEAD_DIM, HEAD_DIM), jnp.float32)
        n0 = jnp.zeros((B, 2, N_HEADS, HEAD_DIM), jnp.float32)
        m0 = jnp.zeros((B, 2, N_HEADS), jnp.float32)
        yp, Cc, nc_, mc = _layer(yp, mod_ctx, C0, n0, m0, *params, None)
        new_C.append(Cc)
        new_n.append(nc_)
        new_m.append(mc)
        mod_lat = _adaln(c, ada_w[layer], ada_b[layer])
        ys, _, _, _ = _layer(ys, mod_lat, state_C[:, layer], state_n[:, layer], state_m[:, layer], *params, GRID_W)
    new_state_C = jnp.stack(new_C, axis=1)
    new_state_n = jnp.stack(new_n, axis=1)
    new_state_m = jnp.stack(new_m, axis=1)
    return (yp, ys, new_state_C, new_state_n, new_state_m)
```

```python
import os
from contextlib import ExitStack

import numpy as np
import ml_dtypes

import concourse.bass as bass
import concourse.mybir as mybir
from concourse.bass_utils import run_bass_kernel_spmd

F32 = mybir.dt.float32
BF16 = mybir.dt.bfloat16
AF = mybir.ActivationFunctionType
ALU = mybir.AluOpType
AX = mybir.AxisListType

D = 2048
NH = 8
DH = 128
M = 1024
DFF = 8192
EPS = 1e-6
NTOK = 2560
ENGS = ("pe", "act", "dve", "pool", "sp")


class _Rec:
    def __init__(self):
        self.call = None

    def __getattr__(self, name):
        def f(*a, **k):
            self.call = (name, a, k)
            return self
        return f


class Prog:
    def __init__(self, nc, es):
        self.nc = nc
        self.es = es
        self.ops = []
        self.last_writer = {}
        self.readers = {}
        self.chan_last = {}
        self.region_fence = {}

    @staticmethod
    def _psum_banks(ap):
        esz = 4 if ap.dtype == F32 else 2
        start = ap.offset * esz
        ext = sum((cnt - 1) * abs(st) for st, cnt in ap.ap[1:]) * esz + esz
        return range(start // 2048, (start + ext - 1) // 2048 + 1)

    def _add(self, eng, fn, reads, writes, kind="c", chan=None):
        idx = len(self.ops)
        deps = set()
        rec = _Rec()
        fn(rec)
        assert rec.call is not None
        reads = list(reads)
        writes = list(writes)
        name_, a_, k_ = rec.call
        operands = [("out" if i == 0 else "in", v) for i, v in enumerate(a_)] + list(k_.items())
        for key, v in operands:
            if hasattr(v, "space") and str(v.space) == "PSUM":
                for b in self._psum_banks(v):
                    if key == "out":
                        if f"ps{b}" not in writes:
                            print("WARN: undeclared psum write", eng, name_, b)
                            writes.append(f"ps{b}")
                    else:
                        if f"ps{b}" not in reads:
                            print("WARN: undeclared psum read", eng, name_, b)
                            reads.append(f"ps{b}")
                    if eng in ("act", "dve"):
                        if f"psx{b}" not in writes:
                            writes.append(f"psx{b}")

        def lastw(r):
            w = self.last_writer.get(r)
            if w is None and "/" in r:
                w = self.region_fence.get(r.split("/")[0])
            return w

        for r in reads:
            w = lastw(r)
            if w is not None:
                deps.add(w)
        for w_ in writes:
            w = lastw(w_)
            if w is not None:
                deps.add(w)
            for rd in self.readers.get(w_, ()):
                deps.add(rd)
        if kind == "d":
            prev = self.chan_last.get(chan)
            if prev is not None:
                deps.add(prev)
            self.chan_last[chan] = idx
        deps.discard(idx)
        for r in reads:
            self.readers.setdefault(r, []).append(idx)
        for w_ in writes:
            self.last_writer[w_] = idx
            self.readers[w_] = []
        self.ops.append(dict(eng=eng, call=rec.call, deps=deps, kind=kind, chan=chan))
        return idx

    def pe(self, fn, reads=(), writes=()):
        return self._add("pe", fn, reads, writes)

    def act(self, fn, reads=(), writes=()):
        return self._add("act", fn, reads, writes)

    def dve(self, fn, reads=(), writes=()):
        return self._add("dve", fn, reads, writes)

    def pool(self, fn, reads=(), writes=()):
        return self._add("pool", fn, reads, writes)

    def dma(self, queue, chan, fn, reads=(), writes=()):
        return self._add(queue, fn, reads, writes, kind="d", chan=chan)

    def fence(self, region, fn):
        pref = region + "/"
        names = [k for k in set(self.last_writer) | set(self.readers) if k.startswith(pref)]
        idx = self._add("dve", fn, reads=(), writes=names + ["SCRcell"])
        for k in names:
            self.last_writer.pop(k, None)
            self.readers.pop(k, None)
        self.region_fence[region] = idx
        return idx

    def emit(self):
        nc, es = self.nc, self.es
        ops = self.ops
        n = len(ops)
        needs_signal = [False] * n
        for i, op in enumerate(ops):
            keep = set()
            for d in op["deps"]:
                dop = ops[d]
                if dop["kind"] == "c" and op["kind"] == "c" and dop["eng"] == "pe" and op["eng"] == "pe":
                    continue
                keep.add(d)
            op["deps"] = keep
            for d in keep:
                needs_signal[d] = True
        eng_sem = {e: es.enter_context(nc.semaphore(f"sem_{e}")) for e in ENGS}
        chan_sem = {}
        for op in ops:
            if op["kind"] == "d" and op["chan"] not in chan_sem:
                chan_sem[op["chan"]] = es.enter_context(nc.semaphore(f"semd_{len(chan_sem)}"))
        eng_cnt = {e: 0 for e in ENGS}
        chan_cnt = {c: 0 for c in chan_sem}
        sig = [None] * n
        for i, op in enumerate(ops):
            if op["kind"] == "d":
                chan_cnt[op["chan"]] += 16
                sig[i] = (chan_sem[op["chan"]], chan_cnt[op["chan"]])
            elif needs_signal[i]:
                eng_cnt[op["eng"]] += 1
                sig[i] = (eng_sem[op["eng"]], eng_cnt[op["eng"]])
        per_eng = {e: [] for e in ENGS}
        for i, op in enumerate(ops):
            per_eng[op["eng"]].append(i)
        final_waits = [(chan_sem[c], chan_cnt[c]) for c in chan_sem]
        block = es.enter_context(nc.Block())

        def make(ename):
            def body(eng):
                known = {}
                for i in per_eng[ename]:
                    op = ops[i]
                    waits = {}
                    for d in op["deps"]:
                        s, v = sig[d]
                        key = id(s)
                        if known.get(key, 0) >= v:
                            continue
                        if key not in waits or waits[key][1] < v:
                            waits[key] = (s, v)
                    for key, (s, v) in waits.items():
                        eng.wait_ge(s, v)
                        known[key] = v
                    name_, a_, k_ = op["call"]
                    ins = getattr(eng, name_)(*a_, **k_)
                    if sig[i] is not None:
                        s, v = sig[i]
                        ins.then_inc(s, 16 if op["kind"] == "d" else 1)
                if ename == "sp":
                    for s, v in final_waits:
                        if v > 0:
                            eng.wait_ge(s, v)
            return body

        block.tensor(make("pe"))
        block.scalar(make("act"))
        block.vector(make("dve"))
        block.gpsimd(make("pool"))
        block.sync(make("sp"))
        self.stats = {e: len(per_eng[e]) for e in ENGS}


def build_program(debug=False):
    nc = bass.Bass("TRN2", target_bir_lowering=False)
    es = ExitStack()

    def din(name, shape, dt=F32):
        return nc.dram_tensor(name, list(shape), dt, kind="ExternalInput").ap()

    def dout(name, shape, dt=F32):
        return nc.dram_tensor(name, list(shape), dt, kind="ExternalOutput").ap()

    xs = din("xs", [NTOK, D])
    sC = din("sC", [2, NH, DH, DH])
    sn = din("sn", [2, NH, DH])
    sm = din("sm", [1, 16])
    w_in = din("w_in", [D, 5152])
    gb = din("gb", [1, 32])
    pool_w = din("pool_w", [4, 256, 256])
    w_out = din("w_out", [D, D])
    ada_w = din("ada_w", [D, 6 * D])
    w1 = din("w1", [D, DFF])
    w2 = din("w2", [DFF, D])
    smallpl = din("smallpl", [128, 208])
    cb16 = din("cb16", [128, 384], BF16)
    cf32 = din("cf32", [128, 512])
    invc_s = din("invc_s", [1, 256])
    invc_p = din("invc_p", [1, 1024])
    ys = dout("ys", [NTOK, D])
    nC = dout("nC", [2, 2, NH, DH, DH])
    nn = dout("nn", [2, 2, NH, DH])
    nm = dout("nm", [2, 16])

    scr = nc.dram_tensor("scr_w", [97, 128, 4096], BF16, kind="Internal").ap()

    def sb(name, shape, dt=F32):
        return es.enter_context(nc.sbuf_tensor(name, list(shape), dt))

    R1 = sb("R1", [128, 32768], BF16)
    R2 = sb("R2", [128, 16, 2048], BF16)
    RX = sb("RX", [128, 8192], F32)
    RING = [sb(f"ring{i}", [128, 4096], BF16) for i in range(2)]
    GG = sb("GG", [128, 4096], F32)
    TS = sb("TS", [128, 1536], F32)
    CB = sb("CB", [128, 384], BF16)
    CF = sb("CF", [128, 512], F32)
    SPL = sb("SPL", [128, 208], F32)
    GBI = sb("GBI", [128, 32], F32)
    SMB = sb("SMB", [128, 16], F32)
    SCB = sb("SCB", [128, 32], BF16)
    MOD = sb("MOD", [128, 192], F32)
    DER = sb("DER", [128, 192], F32)
    WG = sb("WG", [128, 512], BF16)
    PW = sb("PW", [128, 512], BF16)
    INVC = sb("INVC", [128, 256], F32)
    STAT = sb("STAT", [128, 64], F32)
    DIAG = sb("DIAG", [128, 256], F32)
    MFIN = sb("MFIN", [128, 32], F32)
    SCR = sb("SCR", [128, 8], F32)
    FST = sb("FST", [128, 24], F32)
    PS = es.enter_context(nc.psum_tensor("PS", [128, 4096], F32))
    PSb = PS.bitcast(BF16)

    identb = CB[:, 0:128]
    maskU = CB[:, 128:256]
    maskL = CB[:, 256:384]
    identf = CF[:, 0:128]
    triU = CF[:, 128:256]
    triL = CF[:, 256:384]
    onesf = CF[:, 384:512]

    P = Prog(nc, es)
    dbg = {}
    STOP = os.environ.get("MK_STOP", "")
    DUMPS = os.environ.get("MK_DUMP", "").split(",")

    class _Stop(Exception):
        pass

    def checkpoint(name):
        if STOP == name:
            raise _Stop()

    def dump(name, ap, shape, reads, dt=F32):
        if name not in DUMPS:
            return
        o = nc.dram_tensor("dbg_" + name, list(shape), dt, kind="ExternalOutput").ap()
        P.dma("sp", "dbg_" + name, lambda e: e.dma_start(out=o, in_=ap), reads=reads)

    def bank(b):
        return PS[:, b * 512:(b + 1) * 512]

    def bankb(b):
        return PSb[:, b * 1024:(b + 1) * 1024]

    rr = {"ev": 0, "stat": 0, "ps": 0, "chan": 0}

    def evac_eng():
        rr["ev"] ^= 1
        if os.environ.get("MK_EV"):
            return os.environ["MK_EV"]
        return "act" if rr["ev"] else "dve"

    def stat_col():
        c = rr["stat"]
        rr["stat"] = (c + 1) % 64
        return c

    tasks = []

    def add_task(load_fn, compute_fn):
        tasks.append((load_fn, compute_fn))

    def ring_w(slot, a, b):
        return RING[slot][:, 0:a * b].rearrange("p (a b) -> p a b", a=a)

    def ring_f(slot):
        return RING[slot].bitcast(F32)

    def run_tasks():
        nslot = [0]
        pending = []
        loaded = []

        def do_load(t):
            lf, cf = t
            if lf is None:
                loaded.append((None, cf))
            else:
                s = nslot[0] % 2
                nslot[0] += 1
                lf(s)
                loaded.append((s, cf))

        i = 0
        nt = len(tasks)
        outstanding = 0
        qi = 0
        while qi < nt or loaded:
            while qi < nt and (tasks[qi][0] is None or outstanding < 2):
                if tasks[qi][0] is not None:
                    outstanding += 1
                do_load(tasks[qi])
                qi += 1
                if len(loaded) >= 3:
                    break
            s, cf = loaded.pop(0)
            cf(s)
            if rr.get("conv_rate"):
                emit_conv(rr["conv_rate"])
            if s is not None:
                outstanding -= 1
        tasks.clear()

    def wchan():
        rr["chan"] = (rr["chan"] + 1) % 4
        return f"w{rr['chan']}"

    rr["hchan"] = 0
    rr["cchan"] = 0

    def hchan():
        rr["hchan"] = (rr["hchan"] + 1) % 4
        return f"h{rr['hchan']}"

    def cchan():
        rr["cchan"] = (rr["cchan"] + 1) % 3
        return f"cv{rr['cchan']}"

    w_outv_ = w_out.rearrange("(k p) n -> p k n", p=128)
    w1v_ = w1.rearrange("(k p) n -> p k n", p=128)
    w2v_ = w2.rearrange("(j p) n -> p j n", p=128)
    conv_jobs = []
    for hf in range(2):
        for u4 in range(4):
            i = hf * 4 + u4
            conv_jobs.append((i, scr[i].rearrange("p (a b) -> p a b", a=4), w_outv_[:, u4 * 4:(u4 + 1) * 4, hf * 1024:(hf + 1) * 1024]))
    for u in range(32):
        conv_jobs.append((8 + u, scr[8 + u].rearrange("p (a b) -> p a b", a=16), w1v_[:, :, u * 256:(u + 1) * 256]))
    for hf in range(2):
        for u in range(16):
            i = 40 + hf * 16 + u
            conv_jobs.append((i, scr[i].rearrange("p (a b) -> p a b", a=4), w2v_[:, u * 4:(u + 1) * 4, hf * 1024:(hf + 1) * 1024]))

    w_inv_ = w_in.rearrange("(k p) n -> p k n", p=128)
    win_jobs = []
    for h_ in range(NH):
        for kind in range(2):
            c0_ = (0 if kind == 0 else 2 * M) + h_ * 128
            c1_ = (M if kind == 0 else 3 * M) + h_ * 128
            i = 72 + 8 * kind + h_
            dv = scr[i].rearrange("p (a b) -> p a b", a=16)
            win_jobs.append((i, dv[:, :, 0:128], w_inv_[:, :, c0_:c0_ + 128]))
            win_jobs.append((i, dv[:, :, 128:256], w_inv_[:, :, c1_:c1_ + 128]))
    for g_ in range(4):
        i = 88 + g_
        win_jobs.append((i, scr[i].rearrange("p (a b) -> p a b", a=16), w_inv_[:, :, 4128 + g_ * 256:4128 + (g_ + 1) * 256]))
    for g_ in range(4):
        i = 93 + g_
        win_jobs.append((i, scr[i][:, 0:512].rearrange("p (c n) -> p c n", c=2), pool_w[g_].rearrange("(c p) n -> p c n", p=128)))
    win_jobs.sort(key=lambda t: (t[0] - 72) % 8 if t[0] < 88 else 100)

    def emit_conv(n):
        for _ in range(n):
            if not conv_jobs:
                return
            i, o_ap, i_ap = conv_jobs.pop(0)
            rr["cvn"] = (rr.get("cvn", 0) + 1) % 20
            P.dma("pool", f"cvu{rr['cvn']}", lambda e: e.dma_start(out=o_ap, in_=i_ap), writes=[f"scr{i}"])

    def scr_load(slot, i):
        P.dma("sp", hchan(), lambda e: e.dma_start(out=RING[slot][:], in_=scr[i]), reads=[f"scr{i}"], writes=[f"ring{slot}"])

    def scr_store(slot, i):
        P.dma("sp", hchan(), lambda e: e.dma_start(out=scr[i], in_=RING[slot][:]), reads=[f"ring{slot}"], writes=[f"scr{i}"])

    def rstd_from(ss_ap, ss_res, n, out_col):
        tmpc = stat_col()
        P.act(lambda e, a=ss_ap, t=tmpc: e.activation(out=STAT[:, t:t + 1], in_=a, func=AF.Sqrt, scale=1.0 / n, bias=EPS),
              reads=ss_res, writes=[f"st{tmpc}"])
        P.dve(lambda e, t=tmpc, o=out_col: e.reciprocal(out=STAT[:, o:o + 1], in_=STAT[:, t:t + 1]),
              reads=[f"st{tmpc}"], writes=[f"st{out_col}"])

    def _body():
        P.dma("sp", "c0", lambda e: e.dma_start(out=CB[:], in_=cb16[:, :]), writes=["CB"])
        P.dma("sp", "c1", lambda e: e.dma_start(out=CF[:], in_=cf32[:, :]), writes=["CF"])
        P.dma("sp", "c2", lambda e: e.dma_start(out=SPL[:], in_=smallpl[:, :]), writes=["SPL"])
        P.dma("sp", "c3", lambda e: e.dma_start(out=GBI[:], in_=gb.partition_broadcast(128)), writes=["GBI"])
        P.dma("sp", "c4", lambda e: e.dma_start(out=SMB[:], in_=sm.partition_broadcast(128)), writes=["SMB"])
        P.dma("pool", "c5", lambda e: e.dma_start(out=WG[:].rearrange("p (k n) -> p k n", k=16),
                                                  in_=w_in.rearrange("(k p) n -> p k n", p=128)[:, :, 4096:4128]),
              writes=["WG"])
        condv = SPL[:, 0:32]
        adab = SPL[:, 32:128]
        nwv = SPL[:, 128:192]
        mnw = SPL[:, 192:200]
        psc = SPL[:, 200:208]
        P.act(lambda e: e.activation(out=SCB[:], in_=condv, func=AF.Silu), reads=["SPL"], writes=["SCB"])
        SCBv = SCB[:].rearrange("p (k c) -> p k c", c=2)
        ada_v = ada_w.rearrange("(k p) n -> p k n", p=128)
        conv_jobs[0:0] = win_jobs
        emit_conv(len(win_jobs))
        MODB = 7
        for u in range(48):
            def lf(slot, u=u):
                P.dma("pool", wchan(), lambda e, s=slot, u=u: e.dma_start(out=ring_w(s, 16, 256), in_=ada_v[:, :, u * 256:(u + 1) * 256]),
                      writes=[f"ring{slot}"])

            def cf(slot, u=u):
                wv = ring_w(slot, 16, 256)
                for c2 in range(2):
                    ch = u * 2 + c2
                    for k in range(16):
                        P.pe(lambda e, wv=wv, c2=c2, k=k, ch=ch: e.matmul(bank(MODB)[:, ch * 2:ch * 2 + 2], lhsT=wv[:, k, c2 * 128:(c2 + 1) * 128],
                                                                     rhs=SCBv[:, k, :], start=(k == 0), stop=(k == 15)),
                             reads=[f"ring{slot}", "SCB"], writes=[f"ps{MODB}"])
            add_task(lf, cf)
        run_tasks()
        emit_conv(1000)
        P.dve(lambda e: e.tensor_tensor(out=MOD[:].rearrange("p (a c) -> p a c", c=2), in0=bank(MODB)[:, 0:192].rearrange("p (a c) -> p a c", c=2),
                                        in1=adab.unsqueeze(2).to_broadcast([128, 96, 2]), op=ALU.add),
              reads=[f"ps{MODB}", "SPL"], writes=["MOD"])
        MODv = MOD[:].rearrange("p (a c) -> p a c", c=2)
        DERv = DER[:].rearrange("p (c i k) -> p c i k", c=2, i=6)

        def modv(i, c):
            return MODv[:, i * 16:(i + 1) * 16, c]

        def nw(i):
            return nwv[:, i * 16:(i + 1) * 16]

        for c in range(2):
            P.dve(lambda e, c=c: e.scalar_tensor_tensor(out=DERv[:, c, 0, :], in0=modv(1, c), scalar=1.0, in1=nw(0), op0=ALU.add, op1=ALU.mult),
                  reads=["MOD", "SPL"], writes=["DER"])
            P.dve(lambda e, c=c: e.tensor_copy(out=DERv[:, c, 1, :], in_=modv(0, c)), reads=["MOD"], writes=["DER"])
            P.dve(lambda e, c=c: e.scalar_tensor_tensor(out=DERv[:, c, 2, :], in0=modv(4, c), scalar=1.0, in1=nw(2), op0=ALU.add, op1=ALU.mult),
                  reads=["MOD", "SPL"], writes=["DER"])
            P.dve(lambda e, c=c: e.tensor_copy(out=DERv[:, c, 3, :], in_=modv(3, c)), reads=["MOD"], writes=["DER"])
            P.dve(lambda e, c=c: e.tensor_tensor(out=DERv[:, c, 4, :], in0=modv(2, c), in1=nw(1), op=ALU.mult), reads=["MOD", "SPL"], writes=["DER"])
            P.dve(lambda e, c=c: e.tensor_tensor(out=DERv[:, c, 5, :], in0=modv(5, c), in1=nw(3), op=ALU.mult), reads=["MOD", "SPL"], writes=["DER"])

        dump("mod", MOD[:], [128, 192], ["MOD"])
        dump("der", DER[:], [128, 192], ["DER"])
        checkpoint("setup")
        xnb = TS.bitcast(BF16)[:, 0:2048]
        xnf = TS[:, 0:1024]
        rtmp = [TS.bitcast(BF16)[:, 2048 + i * 512:2048 + (i + 1) * 512] for i in range(2)]

        def norm_transpose(src_ap, src_res, cond, gi, si, dst_fn, dst_res_fn, tbanks, xn_on_act=False, junk=None):
            ssc = stat_col()
            jap, jres = junk if junk is not None else (xnb, ["xnp0", "xnp1"])
            P.act(lambda e, c=ssc: e.activation(out=jap, in_=src_ap, func=AF.Square, accum_out=STAT[:, c:c + 1]),
                  reads=src_res, writes=list(jres) + [f"st{ssc}"])
            rc = stat_col()
            rstd_from(STAT[:, ssc:ssc + 1], [f"st{ssc}"], D, rc)
            for hb in range(2):
                xh = xnb[:, hb * 1024:(hb + 1) * 1024]
                sh_ = src_ap[:, hb * 1024:(hb + 1) * 1024]
                xr = f"xnp{hb}"
                if xn_on_act:
                    P.act(lambda e, c=rc: e.activation(out=xh, in_=sh_, func=AF.Identity, scale=STAT[:, c:c + 1]),
                          reads=src_res + [f"st{rc}"], writes=[xr])
                else:
                    P.dve(lambda e, c=rc: e.tensor_scalar(out=xh, in0=sh_, scalar1=STAT[:, c:c + 1], scalar2=None, op0=ALU.mult),
                          reads=src_res + [f"st{rc}"], writes=[xr])
                b = tbanks[hb]
                for kk in range(8):
                    k = hb * 8 + kk
                    P.pe(lambda e, b=b, kk=kk, k=k: e.transpose(out=bankb(b)[:, kk * 128:(kk + 1) * 128], in_=xnb[:, k * 128:(k + 1) * 128], identity=identb),
                         reads=[xr, "CB"], writes=[f"ps{b}"])
                for kk in range(8):
                    k = hb * 8 + kk
                    en = "act" if hb == 1 else "dve"
                    src = bankb(b)[:, kk * 128:(kk + 1) * 128]
                    g_ap = DERv[:, cond, gi, k:k + 1]
                    s_ap = DERv[:, cond, si, k:k + 1]
                    if en == "act":
                        P.act(lambda e, src=src, k=k, g=g_ap, s=s_ap: e.activation(out=dst_fn(k), in_=src, func=AF.Identity, scale=g, bias=s),
                              reads=[f"ps{b}", "DER"], writes=[dst_res_fn(k)])
                    else:
                        P.dve(lambda e, src=src, k=k, g=g_ap, s=s_ap: e.tensor_scalar(out=dst_fn(k), in0=src, scalar1=g, scalar2=s, op0=ALU.mult, op1=ALU.add),
                              reads=[f"ps{b}", "DER"], writes=[dst_res_fn(k)])

        groups = [
            dict(name="P", base=2048, NT=512, nch=4, cond=1, seqs=[(0, 2, 0), (2, 2, 1)], L=256, invc=invc_p),
            dict(name="S", base=0, NT=2048, nch=16, cond=0, seqs=[(0, 16, None)], L=64, invc=invc_s),
        ]
        if os.environ.get("MK_ONLY_P"):
            groups = groups[:1]
        RXb = RX.bitcast(BF16)
        GGb = None

        for gi_, G in enumerate(groups):
            base, NT, nch, cond, L = G["base"], G["NT"], G["nch"], G["cond"], G["L"]
            ntt = NT // 512
            hT = R1[:, 0:16 * NT].rearrange("p (k t) -> p k t", k=16)

            def hres(k, c):
                return f"R1/h{k}c{c}"

            def mres(k, c):
                return f"R2/k{k}c{c}"

            P.fence("R1", lambda e: e.memset(SCR[:, 0:1], 0.0))
            rr["conv_rate"] = 0
            first_pass = False


            R2f32 = R2.bitcast(F32)
            NXS = 4
            xslots = [R2f32[:, 2 * i:2 * i + 2, :] for i in range(NXS)]

            def a1_load(c):
                sl = c % NXS
                P.dma("sp", f"xa{sl}", lambda e: e.dma_start(out=xslots[sl], in_=xs[base + c * 128: base + (c + 1) * 128, :].rearrange("p (a b) -> p a b", a=2)),
                      writes=[f"R2/xs{sl}"])

            for c in range(min(NXS - 1, nch)):
                a1_load(c)
            for c in range(nch):
                if c + NXS - 1 < nch:
                    a1_load(c + NXS - 1)
                sl = c % NXS
                norm_transpose(xslots[sl].rearrange("p a b -> p (a b)"), [f"R2/xs{sl}"], cond, 0, 1,
                               lambda k, c=c: hT[:, k, c * 128:(c + 1) * 128], lambda k, c=c: hres(k, c), (0, 1),
                               junk=(R2[:, 15, 0:2048], [mres(15, c_) for c_ in range(16)]))

            if G["name"] == "P":
                dump("hT", R1[:, 0:8192], [128, 8192], [hres(k, c) for k in range(16) for c in range(4)], BF16)
                checkpoint("A1")
            P.fence("R2", lambda e: e.memset(SCR[:, 1:2], 0.0))
            GB = 2
            Gt = GG[:, 0:nch * 32].rearrange("p (c x) -> p c x", x=32)
            G4 = GG[:, 0:nch * 32].rearrange("p (c d t h) -> p c d t h", d=2, t=2, h=8)
            li = G4[:, :, :, 0, :]
            lfp = G4[:, :, :, 1, :]

            def g16(off):
                return GG[:, off:off + nch * 16]

            def g4(off):
                return GG[:, off:off + nch * 16].rearrange("p (c d h) -> p c d h", d=2, h=8)

            def g3(off):
                return GG[:, off:off + nch * 16].rearrange("p (c x) -> p c x", x=16)

            O_LF, O_T1, O_T2, O_B, O_TOT, O_A, O_AM, O_MC, O_MP, O_W, O_GC, O_DL = [512 + 256 * i for i in range(12)]
            O_AMT = 3600
            for c in range(nch):
                for k in range(16):
                    P.pe(lambda e, c=c, k=k: e.matmul(bank(GB)[:, c * 32:(c + 1) * 32], lhsT=hT[:, k, c * 128:(c + 1) * 128],
                                                      rhs=WG[:].rearrange("p (k n) -> p k n", k=16)[:, k, :], start=(k == 0), stop=(k == 15)),
                         reads=[hres(k, c), "WG"], writes=[f"ps{GB}"])
            P.dve(lambda e: e.tensor_tensor(out=Gt, in0=bank(GB)[:, 0:nch * 32].rearrange("p (c x) -> p c x", x=32),
                                            in1=GBI[:].unsqueeze(1).to_broadcast([128, nch, 32]), op=ALU.add),
                  reads=[f"ps{GB}", "GBI"], writes=["GG/G"])
            checkpoint("g1")
            P.act(lambda e: e.activation(out=g4(O_T1), in_=lfp, func=AF.Abs), reads=["GG/G"], writes=["GG/T1"])
            P.act(lambda e: e.activation(out=g16(O_T2), in_=g16(O_T1), func=AF.Exp, scale=-1.0), reads=["GG/T1"], writes=["GG/T2"])
            P.act(lambda e: e.activation(out=g16(O_T1), in_=g16(O_T2), func=AF.Ln, bias=1.0), reads=["GG/T2"], writes=["GG/T1"])
            P.dve(lambda e: e.scalar_tensor_tensor(out=g4(O_LF), in0=lfp, scalar=0.0, in1=g4(O_T1), op0=ALU.min, op1=ALU.subtract),
                  reads=["GG/G", "GG/T1"], writes=["GG/LF"])
            checkpoint("g2")
            LF4 = g4(O_LF)
            for c in range(nch):
                P.pe(lambda e, c=c: e.matmul(bank(GB)[:, c * 32:c * 32 + 8], lhsT=triU, rhs=LF4[:, c, 0, :], start=True, stop=True),
                     reads=["GG/LF", "CF", "GG/G"], writes=[f"ps{GB}"])
                P.pe(lambda e, c=c: e.matmul(bank(GB)[:, c * 32 + 8:c * 32 + 16], lhsT=triL, rhs=LF4[:, c, 1, :], start=True, stop=True),
                     reads=["GG/LF", "CF"], writes=[f"ps{GB}"])
                P.pe(lambda e, c=c: e.matmul(bank(GB)[:, c * 32 + 16:c * 32 + 32], lhsT=onesf, rhs=g3(O_LF)[:, c, :], start=True, stop=True),
                     reads=["GG/LF", "CF"], writes=[f"ps{GB}"])
            pg = bank(GB)[:, 0:nch * 32].rearrange("p (c x) -> p c x", x=32)
            checkpoint("g3a")
            P.dve(lambda e: e.tensor_copy(out=g3(O_B), in_=pg[:, :, 0:16]), reads=[f"ps{GB}"], writes=["GG/B"])
            if os.environ.get("MK_X") == "a":
                P.dve(lambda e: e.tensor_copy(out=g3(O_TOT), in_=pg[:, :, 16:32]), reads=[f"ps{GB}"], writes=["GG/TOT"])
            elif os.environ.get("MK_X") == "c":
                P.act(lambda e: e.activation(out=g3(O_TOT), in_=pg[:, :, 16:32], func=AF.Identity), reads=[f"ps{GB}"], writes=["GG/TOT"])
            elif os.environ.get("MK_X") == "s":
                P.act(lambda e: e.copy(out=g3(O_TOT), in_=pg[:, :, 16:32]), reads=[f"ps{GB}", "GG/B"], writes=["GG/TOT"])
            else:
                P.act(lambda e: e.copy(out=g3(O_TOT), in_=pg[:, :, 16:32]), reads=[f"ps{GB}"], writes=["GG/TOT"])
            checkpoint("g3b")
            P.dve(lambda e: e.tensor_tensor(out=g4(O_A), in0=li, in1=g4(O_B), op=ALU.subtract), reads=["GG/G", "GG/B"], writes=["GG/A"])
            checkpoint("g3")
            ncol = nch * 16
            nhalf = (ncol + 127) // 128
            AMB = 3
            for hf in range(nhalf):
                w_ = min(128, ncol - hf * 128)
                P.pe(lambda e, hf=hf, w_=w_: e.transpose(out=bank(AMB)[0:w_, hf * 128:hf * 128 + 128], in_=g16(O_A)[:, hf * 128:hf * 128 + w_], identity=identf),
                     reads=["GG/A", "CF"], writes=[f"ps{AMB}"])
                P.dve(lambda e, hf=hf, w_=w_: e.reduce_max(out=GG[0:w_, O_AMT + hf:O_AMT + hf + 1], in_=bank(AMB)[0:w_, hf * 128:hf * 128 + 128], axis=AX.X),
                      reads=[f"ps{AMB}"], writes=["GG/AMT"])
                P.dve(lambda e, hf=hf, w_=w_: e.tensor_scalar(out=DIAG[0:w_, hf * 128:hf * 128 + w_], in0=identf[0:w_, 0:w_], scalar1=GG[0:w_, O_AMT + hf:O_AMT + hf + 1],
                                                            scalar2=None, op0=ALU.mult),
                      reads=["GG/AMT", "CF"], writes=["DIAG"])
                P.pe(lambda e, hf=hf, w_=w_: e.matmul(bank(AMB)[:, 256 + hf * 128:256 + hf * 128 + w_], lhsT=onesf[0:w_, :], rhs=DIAG[0:w_, hf * 128:hf * 128 + w_],
                                                      start=True, stop=True),
                     reads=["DIAG", "CF"], writes=[f"ps{AMB}"])
            P.dve(lambda e: e.tensor_copy(out=g16(O_AM), in_=bank(AMB)[:, 256:256 + ncol]), reads=[f"ps{AMB}"], writes=["GG/AM"])
            checkpoint("g4")
            MFv = MFIN[:].rearrange("p (s l) -> p s l", s=2)
            for (c0, n, pidx) in G["seqs"]:
                for d in range(2):
                    order = list(range(c0, c0 + n)) if d == 0 else list(range(c0 + n - 1, c0 - 1, -1))
                    ls = slice(d * 8, d * 8 + 8)
                    first = order[0]
                    if pidx is None:
                        P.dve(lambda e, first=first, ls=ls: e.tensor_copy(out=g3(O_MP)[:, first, ls], in_=SMB[:, ls]), reads=["SMB"], writes=["GG/MP"])
                    else:
                        P.dve(lambda e, first=first, ls=ls: e.memset(g3(O_MP)[:, first, ls], 0.0), writes=["GG/MP"])
                    for i, c in enumerate(order):
                        P.dve(lambda e, c=c, ls=ls: e.tensor_tensor(out=g3(O_MC)[:, c, ls], in0=g3(O_MP)[:, c, ls], in1=g3(O_AM)[:, c, ls], op=ALU.max),
                              reads=["GG/MP", "GG/AM"], writes=["GG/MC"])
                        if i + 1 < n:
                            dst = g3(O_MP)[:, order[i + 1], ls]
                            dres = "GG/MP"
                        else:
                            dst = MFv[:, (pidx or 0), ls]
                            dres = "MFIN"
                        P.dve(lambda e, c=c, ls=ls, dst=dst: e.tensor_tensor(out=dst, in0=g3(O_TOT)[:, c, ls], in1=g3(O_MC)[:, c, ls], op=ALU.add),
                              reads=["GG/TOT", "GG/MC"], writes=[dres])
                if pidx is not None:
                    P.dma("sp", "om", lambda e, pidx=pidx: e.dma_start(out=nm[pidx:pidx + 1, :], in_=MFv[0:1, pidx, :]), reads=["MFIN"])
            checkpoint("g5")
            P.dve(lambda e: e.tensor_tensor(out=g16(O_T1), in0=g16(O_A), in1=g16(O_MC), op=ALU.subtract), reads=["GG/A", "GG/MC"], writes=["GG/T1"])
            P.act(lambda e: e.activation(out=g16(O_W), in_=g16(O_T1), func=AF.Exp), reads=["GG/T1"], writes=["GG/W"])
            P.dve(lambda e: e.tensor_tensor(out=g16(O_T2), in0=g16(O_MP), in1=g16(O_MC), op=ALU.subtract), reads=["GG/MP", "GG/MC"], writes=["GG/T2"])
            P.act(lambda e: e.activation(out=g16(O_GC), in_=g16(O_T2), func=AF.Exp), reads=["GG/T2"], writes=["GG/GC"])
            P.dve(lambda e: e.tensor_tensor(out=g16(O_T1), in0=g16(O_B), in1=g16(O_MC), op=ALU.add), reads=["GG/B", "GG/MC", "GG/W"], writes=["GG/T1"])
            P.act(lambda e: e.activation(out=g16(O_DL), in_=g16(O_T1), func=AF.Exp, scale=-1.0), reads=["GG/T1"], writes=["GG/DL"])
            Wv, GCv, DLv = g3(O_W), g3(O_GC), g3(O_DL)
            if G["name"] == "P":
                dump("gates", GG[:], [128, 4096], ["GG/W", "GG/GC", "GG/DL", "GG/MC", "GG/B", "GG/A", "GG/G", "GG/AM", "GG/MP", "GG/TOT", "GG/LF"])
                checkpoint("gates")

            R2flat = R2[:].rearrange("p k t -> p (k t)")
            SETB = [RXb, R2flat[:, 16384:32768]]
            SETP = ["RX/s0", "R2/s1"]

            def sets(si):
                Bf = SETB[si]
                return dict(
                    qT=Bf[:, 0:NT], kT=Bf[:, 2048:2048 + NT],
                    ktok=Bf[:, 4096:4096 + nch * 128].rearrange("p (c d) -> p c d", d=128),
                    vtok=Bf[:, 6144:6144 + nch * 130].rearrange("p (c d) -> p c d", d=130),
                    sigo=Bf[:, 8256:8256 + nch * 128].rearrange("p (c d) -> p c d", d=128),
                    pre=SETP[si])
            T0 = 10304
            sTm = [RXb[:, T0 + i * 128:T0 + (i + 1) * 128] for i in range(4)]
            vwb = [RXb[:, T0 + 512 + i * 130:T0 + 512 + (i + 1) * 130] for i in range(6)]
            hnb = [RXb[:, T0 + 1292 + i * 128:T0 + 1292 + (i + 1) * 128] for i in range(2)]
            F0 = (T0 + 1548) // 2
            hs8 = [RX[:, F0 + i * 128:F0 + (i + 1) * 128] for i in range(8)]
            Cst = [[RX[:, F0 + 1024 + (d * 2 + j) * 130:F0 + 1024 + (d * 2 + j + 1) * 130] for j in range(2)] for d in range(2)]
            assert F0 + 1024 + 4 * 130 <= 8192
            CSB = R2flat[:, 16384 + 10304:16384 + 10304 + 32 * 130].rearrange("p (d c x) -> p d c x", d=2, x=130)
            w_inv = w_in.rearrange("(k p) n -> p k n", p=128)
            heads = list(range(NH))

            def load_unit(slot, h, kind):
                wv = ring_w(slot, 16, 256)
                si = 72 + 8 * kind + h
                if not first_pass:
                    scr_load(slot, si)
                    return
                c0_ = (0 if kind == 0 else 2 * M) + h * 128
                c1_ = (M if kind == 0 else 3 * M) + h * 128
                P.dma("pool", wchan(), lambda e: e.dma_start(out=wv[:, :, 0:128], in_=w_inv[:, :, c0_:c0_ + 128]), writes=[f"ring{slot}"])
                P.dma("pool", wchan(), lambda e: e.dma_start(out=wv[:, :, 128:256], in_=w_inv[:, :, c1_:c1_ + 128]), writes=[f"ring{slot}"])
                scr_store(slot, si)

            def gen_proj(h):
                S = sets(h % 2)
                pre = S["pre"]
                wv = ring_w(0, 16, 256)
                for tt in range(ntt):
                    for which in range(2):
                        b = which
                        for k in range(16):
                            P.pe(lambda e: e.matmul(bank(b), lhsT=wv[:, k, which * 128:(which + 1) * 128], rhs=hT[:, k, tt * 512:(tt + 1) * 512],
                                                    start=(k == 0), stop=(k == 15)),
                                 reads=["ring0"] + [hres(k, tt * 4 + j) for j in range(4)], writes=[f"ps{b}"])
                        if which == 0:
                            P.act(lambda e: e.activation(out=S["qT"][:, tt * 512:(tt + 1) * 512], in_=bank(b), func=AF.Identity, scale=float(DH ** -0.5)),
                                  reads=[f"ps{b}"], writes=[f"{pre}q{tt}"])
                        else:
                            P.dve(lambda e: e.tensor_copy(out=S["kT"][:, tt * 512:(tt + 1) * 512], in_=bank(b)),
                                  reads=[f"ps{b}"], writes=[f"{pre}k{tt}"])
                        yield
                if h + 1 < NH:
                    load_unit(0, h + 1, 0)
                for c in range(nch):
                    b = c % 2
                    P.pe(lambda e: e.transpose(out=bankb(b)[:, 0:128], in_=S["kT"][:, c * 128:(c + 1) * 128], identity=identb),
                         reads=[f"{pre}k{c // 4}", "CB"], writes=[f"ps{b}"])
                    P.dve(lambda e: e.tensor_copy(out=S["ktok"][:, c, :], in_=bankb(b)[:, 0:128]), reads=[f"ps{b}"], writes=[f"{pre}kt{c}"])
                    if c % 2 == 1:
                        yield
                wv1 = ring_w(1, 16, 256)
                P.dve(lambda e: e.memset(S["vtok"][:, :, 128:130], 1.0), writes=[f"{pre}vones"])
                for c in range(nch):
                    b = c % 2
                    for k in range(16):
                        P.pe(lambda e: e.matmul(bank(b)[:, 0:256], lhsT=hT[:, k, c * 128:(c + 1) * 128], rhs=wv1[:, k, :], start=(k == 0), stop=(k == 15)),
                             reads=["ring1", hres(k, c)], writes=[f"ps{b}"])
                    P.dve(lambda e: e.tensor_copy(out=S["vtok"][:, c, 0:128], in_=bank(b)[:, 0:128]), reads=[f"ps{b}"], writes=[f"{pre}v{c}"])
                    P.act(lambda e: e.activation(out=S["sigo"][:, c, :], in_=bank(b)[:, 128:256], func=AF.Sigmoid), reads=[f"ps{b}"], writes=[f"{pre}o{c}"])
                    yield
                if h + 1 < NH:
                    load_unit(1, h + 1, 1)

            def run_scan(h, filler):
                S = sets(h % 2)
                pre = S["pre"]
                qT, kT, ktok, vtok, sigo = S["qT"], S["kT"], S["ktok"], S["vtok"], S["sigo"]
                for (c0, n, pidx) in G["seqs"]:
                    for d in range(2):
                        if pidx is None:
                            P.dma("sp", f"st{d}", lambda e: e.dma_start(out=Cst[d][0][:, 0:128], in_=sC[d, h, :, :]), writes=[f"RX/C{d}0"])
                            P.dma("sp", f"st{d}", lambda e: e.dma_start(out=Cst[d][0][:, 128:129], in_=sn[d, h, :].rearrange("(p o) -> p o", o=1)),
                                  writes=[f"RX/C{d}0"])
                        else:
                            P.dve(lambda e: e.memset(Cst[d][0][:, 0:130], 0.0), writes=[f"RX/C{d}0"])

                    def chunk_of(d, i):
                        return c0 + i if d == 0 else c0 + n - 1 - i

                    def prefetch(d, i):
                        c = chunk_of(d, i)
                        lane = d * 8 + h
                        vi = d * 2 + (i % 2)
                        cb_ = 2 + d * 2 + (i % 2)
                        P.act(lambda e: e.activation(out=vwb[vi][:, 0:129], in_=vtok[:, c, 0:129], func=AF.Identity, scale=Wv[:, c, lane:lane + 1]),
                              reads=[f"{pre}v{c}", f"{pre}vones", "GG/W"], writes=[f"RX/vw{vi}"])
                        P.pe(lambda e: e.matmul(bank(cb_)[:, 0:129], lhsT=ktok[:, c, :], rhs=vwb[vi][:, 0:129], start=True, stop=True),
                             reads=[f"{pre}kt{c}", f"RX/vw{vi}"], writes=[f"ps{cb_}"])

                    for i in range(min(2, n)):
                        for d in range(2):
                            prefetch(d, i)
                    for i in range(n):
                        for d in range(2):
                            c = chunk_of(d, i)
                            lane = d * 8 + h
                            cur, nx = i % 2, (i + 1) % 2
                            cb_ = 2 + d * 2 + (i % 2)
                            P.act(lambda e: e.activation(out=CSB[:, d, c, 0:129], in_=Cst[d][cur][:, 0:129], func=AF.Identity, scale=GCv[:, c, lane:lane + 1]),
                                  reads=[f"RX/C{d}{cur}", "GG/GC"], writes=[f"R2/cs{d}_{c}"])
                            P.dve(lambda e: e.scalar_tensor_tensor(out=Cst[d][nx][:, 0:129], in0=Cst[d][cur][:, 0:129], scalar=GCv[:, c, lane:lane + 1],
                                                                   in1=bank(cb_)[:, 0:129], op0=ALU.mult, op1=ALU.add),
                                  reads=[f"RX/C{d}{cur}", f"ps{cb_}", "GG/GC"], writes=[f"RX/C{d}{nx}"])
                        if i + 2 < n:
                            for d in range(2):
                                prefetch(d, i + 2)

                    if pidx is not None:
                        fin = n % 2
                        for d in range(2):
                            P.dma("sp", f"oc{d}", lambda e: e.dma_start(out=nC[pidx, d, h, :, :], in_=Cst[d][fin][:, 0:128]), reads=[f"RX/C{d}{fin}"])
                            P.dma("sp", f"on{d}", lambda e: e.dma_start(out=nn[pidx, d, h, :].rearrange("(p o) -> p o", o=1), in_=Cst[d][fin][:, 128:129]),
                                  reads=[f"RX/C{d}{fin}"])
                    batch = []

                    def flush():
                        nb = len(batch)
                        if nb == 0:
                            return
                        P.act(lambda e: e.activation(out=FST[:, 8:8 + nb], in_=FST[:, 0:nb], func=AF.Sqrt, scale=1.0 / DH, bias=EPS),
                              reads=[f"fs{j}" for j in range(nb)], writes=["fstd"])
                        P.dve(lambda e: e.reciprocal(out=FST[:, 16:16 + nb], in_=FST[:, 8:8 + nb]), reads=["fstd"], writes=["frs"])
                        def stt_(j, c):
                            hb_ = j % 2
                            P.dve(lambda e: e.scalar_tensor_tensor(out=hnb[hb_], in0=hs8[j], scalar=FST[:, 16 + j:17 + j], in1=sigo[:, c, :], op0=ALU.mult, op1=ALU.mult),
                                  reads=[f"RX/hs{j}", "frs", f"{pre}o{c}"], writes=[f"RX/hn{hb_}"])
                        stt_(0, batch[0])
                        for j, c in enumerate(batch):
                            hb_ = j % 2
                            if j % 2 == 1:
                                filler(1)
                            tb = 0 + (j % 2)
                            P.pe(lambda e: e.transpose(out=bankb(tb)[:, 0:128], in_=hnb[hb_], identity=identb), reads=[f"RX/hn{hb_}", "CB"], writes=[f"ps{tb}"])
                            if j + 1 < nb:
                                stt_(j + 1, batch[j + 1])
                            P.dve(lambda e: e.tensor_scalar(out=R2[:, h, c * 128:(c + 1) * 128], in0=bankb(tb)[:, 0:128], scalar1=mnw[:, h:h + 1], scalar2=None, op0=ALU.mult),
                                  reads=[f"ps{tb}", "SPL"], writes=[mres(h, c)])
                        batch.clear()

                    def emit_sT(c):
                        sb_ = 2 + (c % 2)
                        P.pe(lambda e: e.matmul(bank(sb_)[:, 0:128], lhsT=kT[:, c * 128:(c + 1) * 128], rhs=qT[:, c * 128:(c + 1) * 128], start=True, stop=True),
                             reads=[f"{pre}k{c // 4}", f"{pre}q{c // 4}"], writes=[f"ps{sb_}"])

                    def emit_masks(c):
                        sb_ = 2 + (c % 2)
                        for d in range(2):
                            lane = d * 8 + h
                            msk = maskU if d == 0 else maskL
                            si_ = (c % 2) * 2 + d
                            P.dve(lambda e: e.scalar_tensor_tensor(out=sTm[si_], in0=bank(sb_)[:, 0:128], scalar=Wv[:, c, lane:lane + 1], in1=msk, op0=ALU.mult, op1=ALU.mult),
                                  reads=[f"ps{sb_}", "CB", "GG/W"], writes=[f"RX/sTm{si_}"])

                    emit_sT(c0)
                    emit_masks(c0)
                    for c in range(c0, c0 + n):
                        j = len(batch)
                        batch.append(c)
                        if c + 1 < c0 + n:
                            emit_sT(c + 1)
                            emit_masks(c + 1)
                        filler(1)
                        nbase = 4 + 2 * (c % 2)
                        for d in range(2):
                            si_ = (c % 2) * 2 + d
                            nb_ = nbase + d
                            P.pe(lambda e: e.matmul(bank(nb_)[:, 0:129], lhsT=sTm[si_], rhs=vtok[:, c, 0:129], start=True, stop=False),
                                 reads=[f"RX/sTm{si_}", f"{pre}v{c}", f"{pre}vones"], writes=[f"ps{nb_}"])
                            P.pe(lambda e: e.matmul(bank(nb_)[:, 0:129], lhsT=qT[:, c * 128:(c + 1) * 128], rhs=CSB[:, d, c, 0:129], start=False, stop=True),
                                 reads=[f"{pre}q{c // 4}", f"R2/cs{d}_{c}"], writes=[f"ps{nb_}"])
                        dc_ = stat_col()
                        if dc_ == 63:
                            dc_ = stat_col()
                        stat_col()
                        den2 = PS[:, nbase * 512:(nbase + 2) * 512].rearrange("p (b x) -> p b x", b=2)[:, :, 128:129]
                        P.act(lambda e: e.activation(out=STAT[:, dc_:dc_ + 2].unsqueeze(2), in_=den2, func=AF.Abs),
                              reads=[f"ps{nbase}", f"ps{nbase + 1}"], writes=[f"st{dc_}", f"st{dc_ + 1}"])
                        P.dve(lambda e: e.tensor_tensor(out=STAT[:, dc_:dc_ + 2], in0=STAT[:, dc_:dc_ + 2], in1=DLv[:, c, h:16:8], op=ALU.max),
                              reads=[f"st{dc_}", f"st{dc_ + 1}", "GG/DL"], writes=[f"st{dc_}", f"st{dc_ + 1}"])
                        P.dve(lambda e: e.reciprocal(out=STAT[:, dc_:dc_ + 2], in_=STAT[:, dc_:dc_ + 2]), reads=[f"st{dc_}", f"st{dc_ + 1}"], writes=[f"st{dc_}", f"st{dc_ + 1}"])
                        P.act(lambda e: e.activation(out=hs8[j], in_=bank(nbase)[:, 0:128], func=AF.Identity, scale=STAT[:, dc_:dc_ + 1]),
                              reads=[f"ps{nbase}", f"st{dc_}"], writes=[f"RX/hs{j}"])
                        P.dve(lambda e: e.scalar_tensor_tensor(out=hs8[j], in0=bank(nbase + 1)[:, 0:128], scalar=STAT[:, dc_ + 1:dc_ + 2], in1=hs8[j], op0=ALU.mult, op1=ALU.add),
                              reads=[f"ps{nbase + 1}", f"st{dc_ + 1}", f"RX/hs{j}"], writes=[f"RX/hs{j}"])
                        P.act(lambda e: e.activation(out=hnb[j % 2], in_=hs8[j], func=AF.Square, accum_out=FST[:, j:j + 1]),
                              reads=[f"RX/hs{j}"], writes=[f"RX/hn{j % 2}", f"fs{j}"])
                        filler(1)
                        if len(batch) == 8:
                            flush()
                    flush()

            def drain(g_, nsteps=None):
                k_ = 0
                for _ in g_:
                    k_ += 1
                    if nsteps is not None and k_ >= nsteps:
                        return False
                return True

            load_unit(0, heads[0], 0)
            load_unit(1, heads[0], 1)
            drain(gen_proj(heads[0]))
            for h in heads:
                gp = gen_proj(h + 1) if h + 1 < NH else iter(())
                run_scan(h, lambda k, gp=gp: drain(gp, k))
                drain(gp)
                if rr.get("conv_rate"):
                    emit_conv(rr["conv_rate"] * 3)
            P.fence("R2", lambda e: e.memset(SCR[:, 1:2], 0.0))

            if G["name"] == "P":
                dump("mixh", R2[:, 0:8, 0:512], [128, 8, 512], [mres(k, c) for k in range(8) for c in range(4)], BF16)
                checkpoint("heads")
            P.fence("RX", lambda e: e.memset(SCR[:, 1:2], 0.0))
            rows = 512 // L
            LP = L + 16
            PBUFS = [[RX[:, (s_ * 3 + i) * 640:(s_ * 3 + i) * 640 + rows * LP].rearrange("p (r l) -> p r l", l=LP) for i in range(3)] for s_ in range(2)]
            ppT4 = [[RXb[:, 7680 + (p_ * 2 + i) * 512:7680 + (p_ * 2 + i + 1) * 512] for i in range(2)] for p_ in range(2)]
            for i in range(6):
                P.dve(lambda e, i=i: e.memset(RX[:, i * 640:(i + 1) * 640], 0.0), writes=[f"RX/pb{i // 3}_{i % 3}"])
            for g in range(4):
                win = (2, 4, 8, 16)[g]
                half = win // 2

                def lf_u(slot, g=g):
                    if not first_pass:
                        scr_load(slot, 88 + g)
                        return
                    P.dma("pool", wchan(), lambda e, s=slot: e.dma_start(out=ring_w(s, 16, 256), in_=w_inv[:, :, 4128 + g * 256:4128 + (g + 1) * 256]),
                          writes=[f"ring{slot}"])
                    scr_store(slot, 88 + g)

                def cf_u(slot, g=g, win=win, half=half):
                    wv = ring_w(slot, 16, 256)
                    P.dma("sp", "pw", lambda e: e.dma_start(out=PW[:], in_=scr[93 + g][:, 0:512]), reads=[f"scr{93 + g}"], writes=["PW"])
                    P.dma("sp", "ic", lambda e: e.dma_start(out=INVC[:, 0:L], in_=G["invc"][:, g * L:(g + 1) * L].partition_broadcast(128)), writes=["INVC"])
                    PWv = PW[:].rearrange("p (c n) -> p c n", c=2)
                    def emit_hp(tt):
                        ppT = ppT4[tt % 2]
                        for dc in range(2):
                            b = 2 + dc
                            for cc in range(2):
                                P.pe(lambda e, b=b, dc=dc, cc=cc: e.matmul(bank(b), lhsT=PWv[:, cc, dc * 128:(dc + 1) * 128], rhs=ppT[cc], start=(cc == 0), stop=(cc == 1)),
                                     reads=["PW", f"RX/pp{tt % 2}_{cc}"], writes=[f"ps{b}"])
                            kk = 8 + g * 2 + dc
                            P.act(lambda e, b=b, kk=kk, tt=tt: e.activation(out=R2[:, kk, tt * 512:(tt + 1) * 512], in_=bank(b), func=AF.Identity, scale=psc[:, kk - 8:kk - 7]),
                                  reads=[f"ps{b}", "SPL"], writes=[mres(kk, tt * 4 + j) for j in range(4)])

                    for tt in range(ntt):
                        ppT = ppT4[tt % 2]
                        for cc in range(2):
                            b = (tt * 2 + cc) % 2
                            for k in range(16):
                                P.pe(lambda e, b=b, k=k, cc=cc, tt=tt: e.matmul(bank(b), lhsT=wv[:, k, cc * 128:(cc + 1) * 128], rhs=hT[:, k, tt * 512:(tt + 1) * 512],
                                                                            start=(k == 0), stop=(k == 15)),
                                     reads=[f"ring{slot}"] + [hres(k, tt * 4 + j) for j in range(4)], writes=[f"ps{b}"])
                            PBUF = PBUFS[cc]
                            pbn = lambda i_: f"RX/pb{cc}_{i_}"
                            P.act(lambda e, b=b: e.copy(out=PBUF[0][:, :, 8:8 + L], in_=bank(b).rearrange("p (r l) -> p r l", l=L)),
                                  reads=[f"ps{b}"], writes=[pbn(0)])
                            cur = 0
                            sh = 1
                            nxt = 1
                            while sh < win:
                                src = PBUF[cur]
                                dst = PBUF[nxt]
                                P.dve(lambda e, src=src, dst=dst, sh=sh: e.tensor_tensor(out=dst[:, :, 8:LP], in0=src[:, :, 8:LP], in1=src[:, :, 8 - sh:LP - sh], op=ALU.add),
                                      reads=[pbn(cur)], writes=[pbn(nxt)])
                                cur = nxt
                                nxt = 2 if cur == 1 else 1
                                sh *= 2
                            src = PBUF[cur]
                            dst = PBUF[nxt]
                            P.dve(lambda e, src=src, dst=dst, half=half: e.tensor_tensor(out=dst[:, :, 8:8 + L], in0=src[:, :, 8 + half - 1:8 + half - 1 + L],
                                                                                     in1=INVC[:, 0:L].unsqueeze(1).to_broadcast([128, rows, L]), op=ALU.mult),
                                  reads=[pbn(cur), "INVC"], writes=[pbn(nxt)])
                            P.dve(lambda e, dst=dst, cc=cc: e.tensor_tensor(out=ppT[cc].rearrange("p (r l) -> p r l", l=L), in0=dst[:, :, 8:8 + L], in1=PBUF[0][:, :, 8:8 + L], op=ALU.subtract),
                                  reads=[pbn(nxt), pbn(0)], writes=[f"RX/pp{tt % 2}_{cc}"])
                        if tt >= 1:
                            emit_hp(tt - 1)
                    emit_hp(ntt - 1)
                add_task(lf_u, cf_u)
            run_tasks()
            if G["name"] == "P":
                dump("mixp", R2[:, 8:16, 0:512], [128, 8, 512], [mres(k, c) for k in range(8, 16) for c in range(4)], BF16)
                checkpoint("pool")

            rr["conv_rate"] = 0
            P.fence("R1", lambda e: e.memset(SCR[:, 2:3], 0.0))
            P.fence("RX", lambda e: e.memset(SCR[:, 3:4], 0.0))
            P.fence("GG", lambda e: e.memset(SCR[:, 6:7], 0.0))
            for which in range(2):
                for k in range(16):
                    dsl = (which * 16 + k) % 2
                    P.dve(lambda e, which=which, k=k, dsl=dsl: e.tensor_scalar(out=DIAG[:, dsl * 128:(dsl + 1) * 128], in0=identf, scalar1=DERv[:, cond, 4 + which, k:k + 1],
                                                                             scalar2=None, op0=ALU.mult),
                          reads=["DER", "CF"], writes=[f"DIAG{dsl}"])
                    b = dsl
                    P.pe(lambda e, dsl=dsl, b=b: e.matmul(bank(b)[:, 0:128], lhsT=onesf, rhs=DIAG[:, dsl * 128:(dsl + 1) * 128], start=True, stop=True),
                         reads=[f"DIAG{dsl}", "CF"], writes=[f"ps{b}"])
                    P.act(lambda e, which=which, k=k, b=b: e.copy(out=GG[:, which * 2048 + k * 128:which * 2048 + (k + 1) * 128], in_=bank(b)[:, 0:128]),
                          reads=[f"ps{b}"], writes=["GG/GA" if which == 0 else "GG/GF"])
            GA = GG[:, 0:2048]
            GF = GG[:, 2048:4096]
            X1 = RX[:].rearrange("p (s d) -> p s d", s=4)
            R2f = R2.bitcast(F32)
            aT = R1[:].rearrange("p (j t) -> p j t", j=64)
            w_outv = w_out.rearrange("(k p) n -> p k n", p=128)
            w1v = w1.rearrange("(k p) n -> p k n", p=128)
            w2v = w2.rearrange("(j p) n -> p j n", p=128)
            for tt in range(ntt):
                r0 = base + tt * 512
                ssB = {}

                def tokc(sub, tt=tt):
                    return tt * 4 + sub

                for hf in range(2):
                    for u4 in range(4):
                        def lf_wo(slot, hf=hf, u4=u4):
                            if True:
                                scr_load(slot, hf * 4 + u4)
                            else:
                                P.dma("pool", wchan(), lambda e, s=slot: e.dma_start(out=ring_w(s, 4, 1024), in_=w_outv[:, u4 * 4:(u4 + 1) * 4, hf * 1024:(hf + 1) * 1024]),
                                      writes=[f"ring{slot}"])

                        def cf_wo(slot, hf=hf, u4=u4, tt=tt):
                            wv = ring_w(slot, 4, 1024)
                            for sub in range(4):
                                for kk in range(4):
                                    k = u4 * 4 + kk
                                    for cbb in range(2):
                                        b = sub * 2 + cbb
                                        P.pe(lambda e, b=b, k=k, kk=kk, cbb=cbb, sub=sub: e.matmul(bank(b), lhsT=R2[:, k, (tt * 4 + sub) * 128:(tt * 4 + sub + 1) * 128],
                                                                                            rhs=wv[:, kk, cbb * 512:(cbb + 1) * 512], start=(k == 0), stop=(k == 15)),
                                             reads=[f"ring{slot}", mres(k, tt * 4 + sub)], writes=[f"ps{b}"])
                            if u4 == 3:
                                for sub in range(4):
                                    pss = PS[:, sub * 1024:(sub + 1) * 1024]
                                    sc_ = stat_col()
                                    ssB[(sub, hf)] = sc_
                                    P.act(lambda e, pss=pss, sc_=sc_: e.activation(out=TS.bitcast(BF16)[:, 2048:3072], in_=pss, func=AF.Square, accum_out=STAT[:, sc_:sc_ + 1]),
                                          reads=[f"ps{sub * 2}", f"ps{sub * 2 + 1}"], writes=["rt0", "rt1", f"st{sc_}"])
                                    if hf == 0:
                                        P.dve(lambda e, pss=pss, sub=sub: e.tensor_copy(out=X1[:, sub, 0:1024], in_=pss),
                                              reads=[f"ps{sub * 2}", f"ps{sub * 2 + 1}"], writes=[f"RX/x{sub}a"])
                        add_task(lf_wo, cf_wo)
                for sub in range(4):
                    def lf_x(slot, sub=sub, r0=r0):
                        P.dma("sp", f"x{slot}", lambda e, s=slot: e.dma_start(out=ring_f(s)[:, 0:2048], in_=xs[r0 + sub * 128:r0 + (sub + 1) * 128, :]),
                              writes=[f"ring{slot}"])

                    def cf_x(slot, sub=sub, tt=tt):
                        a0, a1 = ssB[(sub, 0)], ssB[(sub, 1)]
                        P.dve(lambda e: e.tensor_tensor(out=STAT[:, a0:a0 + 1], in0=STAT[:, a0:a0 + 1], in1=STAT[:, a1:a1 + 1], op=ALU.add),
                              reads=[f"st{a0}", f"st{a1}"], writes=[f"st{a0}"])
                        rc_ = stat_col()
                        rstd_from(STAT[:, a0:a0 + 1], [f"st{a0}"], D, rc_)
                        pss = PS[:, sub * 1024:(sub + 1) * 1024]
                        P.dve(lambda e: e.scalar_tensor_tensor(out=X1[:, sub, 0:1024], in0=X1[:, sub, 0:1024], scalar=STAT[:, rc_:rc_ + 1], in1=GA[:, 0:1024],
                                                               op0=ALU.mult, op1=ALU.mult),
                              reads=[f"RX/x{sub}a", f"st{rc_}", "GG/GA"], writes=[f"RX/x{sub}a"])
                        P.dve(lambda e: e.scalar_tensor_tensor(out=X1[:, sub, 1024:2048], in0=pss, scalar=STAT[:, rc_:rc_ + 1], in1=GA[:, 1024:2048],
                                                               op0=ALU.mult, op1=ALU.mult),
                              reads=[f"ps{sub * 2}", f"ps{sub * 2 + 1}", f"st{rc_}", "GG/GA"], writes=[f"RX/x{sub}b"])
                        P.dve(lambda e: e.tensor_tensor(out=X1[:, sub, 0:1024], in0=X1[:, sub, 0:1024], in1=ring_f(slot)[:, 0:1024], op=ALU.add),
                              reads=[f"RX/x{sub}a", f"ring{slot}"], writes=[f"RX/x{sub}a"])
                        P.pool(lambda e: e.tensor_tensor(out=X1[:, sub, 1024:2048], in0=X1[:, sub, 1024:2048], in1=ring_f(slot)[:, 1024:2048], op=ALU.add),
                               reads=[f"RX/x{sub}b", f"ring{slot}"], writes=[f"RX/x{sub}b"])

                        def back(sb2):
                            c = tt * 4 + sb2
                            norm_transpose(X1[:, sb2, :], [f"RX/x{sb2}a", f"RX/x{sb2}b"], cond, 2, 3,
                                           lambda k, c=c: R2[:, k, c * 128:(c + 1) * 128], lambda k, c=c: mres(k, c), (sb2 * 2, sb2 * 2 + 1), xn_on_act=True,
                                           junk=(R1[:, 0:2048], [f"R1/a{j_}" for j_ in range(4)]))
                        if sub >= 1:
                            back(sub - 1)
                        if sub == 3:
                            back(3)
                    add_task(lf_x, cf_x)
                for u in range(32):
                    def lf_w1(slot, u=u):
                        if True:
                            scr_load(slot, 8 + u)
                        else:
                            P.dma("pool", wchan(), lambda e, s=slot: e.dma_start(out=ring_w(s, 16, 256), in_=w1v[:, :, u * 256:(u + 1) * 256]), writes=[f"ring{slot}"])

                    def cf_w1(slot, u=u, tt=tt):
                        wv = ring_w(slot, 16, 256)
                        for jj in range(2):
                            j = u * 2 + jj
                            b = j % 4
                            for k in range(16):
                                P.pe(lambda e, b=b, k=k, jj=jj: e.matmul(bank(b), lhsT=wv[:, k, jj * 128:(jj + 1) * 128], rhs=R2[:, k, tt * 512:(tt + 1) * 512],
                                                                         start=(k == 0), stop=(k == 15)),
                                     reads=[f"ring{slot}"] + [mres(k, tt * 4 + s_) for s_ in range(4)], writes=[f"ps{b}"])
                            rt = j % 2
                            P.act(lambda e, b=b, rt=rt: e.activation(out=rtmp[rt], in_=bank(b), func=AF.Relu), reads=[f"ps{b}"], writes=[f"rt{rt}"])
                            P.dve(lambda e, j=j, rt=rt: e.tensor_tensor(out=aT[:, j, :], in0=rtmp[rt], in1=rtmp[rt], op=ALU.mult), reads=[f"rt{rt}"], writes=[f"R1/a{j}"])
                    add_task(lf_w1, cf_w1)
                ss3 = {}
                for hf in range(2):
                    for u in range(16):
                        def lf_w2(slot, hf=hf, u=u):
                            if True:
                                scr_load(slot, 40 + hf * 16 + u)
                            else:
                                P.dma("pool", wchan(), lambda e, s=slot: e.dma_start(out=ring_w(s, 4, 1024), in_=w2v[:, u * 4:(u + 1) * 4, hf * 1024:(hf + 1) * 1024]),
                                      writes=[f"ring{slot}"])

                        def cf_w2(slot, hf=hf, u=u, tt=tt, r0=r0):
                            wv = ring_w(slot, 4, 1024)
                            for sub in range(4):
                                for jj in range(4):
                                    j = u * 4 + jj
                                    for cbb in range(2):
                                        b = sub * 2 + cbb
                                        P.pe(lambda e, b=b, j=j, jj=jj, cbb=cbb, sub=sub: e.matmul(bank(b), lhsT=aT[:, j, sub * 128:(sub + 1) * 128], rhs=wv[:, jj, cbb * 512:(cbb + 1) * 512],
                                                                                            start=(j == 0), stop=(j == 63)),
                                             reads=[f"ring{slot}", f"R1/a{j}"], writes=[f"ps{b}"])
                            if u == 15:
                                tmpp = [TS[:, 0:512], TS[:, 512:1024]]
                                YR1 = R1.bitcast(F32)[:, 0:4096].rearrange("p (s d) -> p s d", s=4)
                                if hf == 1:
                                    for sub in range(4):
                                        pss = PS[:, sub * 1024:(sub + 1) * 1024]
                                        P.act(lambda e: e.copy(out=YR1[:, sub, :], in_=pss), reads=[f"ps{sub * 2}", f"ps{sub * 2 + 1}"],
                                              writes=[f"R1/a{j_}" for j_ in range(sub * 4, sub * 4 + 4)])
                                for sub in range(4):
                                    pss = PS[:, sub * 1024:(sub + 1) * 1024] if hf == 0 else YR1[:, sub, :]
                                    sres = [f"ps{sub * 2}", f"ps{sub * 2 + 1}"] if hf == 0 else [f"R1/a{j_}" for j_ in range(sub * 4, sub * 4 + 4)]
                                    sc_ = stat_col()
                                    ss3[(sub, hf)] = sc_
                                    P.act(lambda e: e.activation(out=TS.bitcast(BF16)[:, 2048:3072], in_=pss, func=AF.Square, accum_out=STAT[:, sc_:sc_ + 1]),
                                          reads=sres, writes=["rt0", "rt1", f"st{sc_}"])
                                    if hf == 0:
                                        yr = R2f[:, sub * 4:(sub + 1) * 4, tt * 256:(tt + 1) * 256]
                                        yres = [mres(k, tt * 4 + s_) for k in range(sub * 4, sub * 4 + 4) for s_ in range(4)]
                                        P.dve(lambda e: e.tensor_copy(out=yr, in_=pss.rearrange("p (a b) -> p a b", a=4)),
                                              reads=[f"ps{sub * 2}", f"ps{sub * 2 + 1}"], writes=yres)
                                if hf == 1:
                                    rcs = {}
                                    for sub in range(4):
                                        a0, a1 = ss3[(sub, 0)], ss3[(sub, 1)]
                                        P.dve(lambda e: e.tensor_tensor(out=STAT[:, a0:a0 + 1], in0=STAT[:, a0:a0 + 1], in1=STAT[:, a1:a1 + 1], op=ALU.add),
                                              reads=[f"st{a0}", f"st{a1}"], writes=[f"st{a0}"])
                                        rcs[sub] = stat_col()
                                        rstd_from(STAT[:, a0:a0 + 1], [f"st{a0}"], D, rcs[sub])
                                    for sub in range(4):
                                        rc_ = rcs[sub]
                                        y1 = YR1[:, sub, :]
                                        y1res = [f"R1/a{j_}" for j_ in range(sub * 4, sub * 4 + 4)]
                                        yr = R2f[:, sub * 4:(sub + 1) * 4, tt * 256:(tt + 1) * 256]
                                        yres = [mres(k, tt * 4 + s_) for k in range(sub * 4, sub * 4 + 4) for s_ in range(4)]
                                        P.dve(lambda e: e.scalar_tensor_tensor(out=y1, in0=y1, scalar=STAT[:, rc_:rc_ + 1], in1=GF[:, 1024:2048], op0=ALU.mult, op1=ALU.mult),
                                              reads=y1res + [f"st{rc_}", "GG/GF"], writes=y1res)
                                        P.dve(lambda e: e.tensor_tensor(out=X1[:, sub, 1024:2048], in0=X1[:, sub, 1024:2048], in1=y1, op=ALU.add),
                                              reads=y1res + [f"RX/x{sub}b"], writes=[f"RX/x{sub}b"])
                                        P.dve(lambda e: e.scalar_tensor_tensor(out=yr, in0=yr, scalar=STAT[:, rc_:rc_ + 1], in1=GF[:, 0:1024].rearrange("p (a b) -> p a b", a=4),
                                                                               op0=ALU.mult, op1=ALU.mult),
                                              reads=yres + [f"st{rc_}", "GG/GF"], writes=yres)
                                        P.pool(lambda e: e.tensor_tensor(out=X1[:, sub, 0:1024].rearrange("p (a b) -> p a b", a=4), in0=X1[:, sub, 0:1024].rearrange("p (a b) -> p a b", a=4),
                                                                         in1=yr, op=ALU.add),
                                               reads=yres + [f"RX/x{sub}a"], writes=[f"RX/x{sub}a"])
                                        P.dma("pool", f"oy{sub % 2}", lambda e: e.dma_start(out=ys[r0 + sub * 128:r0 + (sub + 1) * 128, :], in_=X1[:, sub, :]),
                                              reads=[f"RX/x{sub}a", f"RX/x{sub}b"])
                        add_task(lf_w2, cf_w2)
                run_tasks()
            P.fence("RX", lambda e: e.memset(SCR[:, 4:5], 0.0))
            P.fence("R2", lambda e: e.memset(SCR[:, 5:6], 0.0))
            if G["name"] == "P":
                checkpoint("BP")
            P.fence("GG", lambda e: e.memset(SCR[:, 7:8], 0.0))


    try:
        _body()
    except _Stop:
        pass
    P.emit()
    return nc, es, P


def _invcnt(L):
    out = np.zeros((4, L), np.float32)
    t = np.arange(L)
    for g, w in enumerate((2, 4, 8, 16)):
        lo = np.maximum(t - w // 2, 0)
        hi = np.minimum(t + w // 2, L)
        out[g] = 1.0 / (hi - lo).astype(np.float32)
    return out.reshape(1, 4 * L)


_CACHE = {}


def kernel(x_prompt, x_sample, c, state_C, state_n, state_m, c_ctx, w_in, gate_bias, mlstm_norm_w,
           pool_w, pool_scale, w_out, ada_w, ada_b, norm_w, w1, w2):
    f = np.float32
    x_prompt = np.asarray(x_prompt, f); x_sample = np.asarray(x_sample, f)
    c = np.asarray(c, f); c_ctx = np.asarray(c_ctx, f)
    state_C = np.asarray(state_C, f); state_n = np.asarray(state_n, f); state_m = np.asarray(state_m, f)
    w_in2 = np.ascontiguousarray(np.asarray(w_in, f)[0])
    w_out2 = np.ascontiguousarray(np.asarray(w_out, f)[0])
    ada_w2 = np.ascontiguousarray(np.asarray(ada_w, f)[0])
    w1_2 = np.ascontiguousarray(np.asarray(w1, f)[0])
    w2_2 = np.ascontiguousarray(np.asarray(w2, f)[0])
    pool_w2 = np.ascontiguousarray(np.asarray(pool_w, f)[0])
    gb = np.ascontiguousarray(np.asarray(gate_bias, f)[0].reshape(1, 32))
    adab_pl = np.asarray(ada_b, f)[0].reshape(96, 128).T
    nw_pl = np.asarray(norm_w, f)[0].reshape(4, 16, 128).transpose(2, 0, 1).reshape(128, 64)
    mnw_pl = np.asarray(mlstm_norm_w, f)[0].reshape(8, 128).T
    psc_pl = np.asarray(pool_scale, f)[0].reshape(8, 128).T
    eye = np.eye(128, dtype=f)
    triU = np.triu(np.ones((128, 128), f))
    triL = np.tril(np.ones((128, 128), f))
    cb16 = np.concatenate([eye, triU, triL], axis=1).astype(ml_dtypes.bfloat16)
    cf32 = np.concatenate([eye, triU, triL, np.ones((128, 128), f)], axis=1)
    invc_s = _invcnt(64)
    invc_p = _invcnt(256)
    in_maps = []
    for b in range(8):
        xs = np.concatenate([x_sample[b], x_prompt[2 * b], x_prompt[2 * b + 1]], axis=0)
        cond = np.stack([c[b], c_ctx], axis=0)
        cond_pl = cond.reshape(2, 16, 128).transpose(2, 1, 0).reshape(128, 32)
        smallpl = np.ascontiguousarray(np.concatenate([cond_pl, adab_pl, nw_pl, mnw_pl, psc_pl], axis=1), dtype=f)
        in_maps.append({
            "xs": np.ascontiguousarray(xs), "sC": np.ascontiguousarray(state_C[b, 0]), "sn": np.ascontiguousarray(state_n[b, 0]),
            "sm": np.ascontiguousarray(state_m[b, 0].reshape(1, 16)), "w_in": w_in2, "gb": gb, "pool_w": pool_w2, "w_out": w_out2,
            "ada_w": ada_w2, "w1": w1_2, "w2": w2_2, "smallpl": smallpl, "cb16": cb16, "cf32": cf32, "invc_s": invc_s, "invc_p": invc_p,
        })
    if "nc" not in _CACHE:
        nc, es, P = build_program()
        _CACHE["nc"] = nc
        _CACHE["es"] = es
    nc = _CACHE["nc"]
    res = run_bass_kernel_spmd(nc, in_maps, core_ids=list(range(8)))
    R = res.results
    y_sample = np.stack([R[b]["ys"][0:2048] for b in range(8)], axis=0)
    y_prompt = np.stack([R[b]["ys"][2048 + 256 * j:2048 + 256 * (j + 1)] for b in range(8) for j in range(2)], axis=0)
    new_C = np.stack([R[b]["nC"][j] for b in range(8) for j in range(2)], axis=0)[:, None]
    new_n = np.stack([R[b]["nn"][j] for b in range(8) for j in range(2)], axis=0)[:, None]
    new_m = np.stack([R[b]["nm"][j].reshape(2, 8) for b in range(8) for j in range(2)], axis=0)[:, None]
    return (y_prompt.astype(f), y_sample.astype(f), new_C.astype(f), new_n.astype(f), new_m.astype(f))
```

```python
import os
from contextlib import ExitStack

import numpy as np
import ml_dtypes

import concourse.bass as bass
import concourse.mybir as mybir
from concourse.bass_utils import run_bass_kernel_spmd

F32 = mybir.dt.float32
BF16 = mybir.dt.bfloat16
AF = mybir.ActivationFunctionType
ALU = mybir.AluOpType
AX = mybir.AxisListType

D = 2048
NH = 8
DH = 128
M = 1024
DFF = 8192
EPS = 1e-6
NTOK = 2560
ENGS = ("pe", "act", "dve", "pool", "sp")


class _Rec:
    def __init__(self):
        self.call = None

    def __getattr__(self, name):
        def f(*a, **k):
            self.call = (name, a, k)
            return self
        return f


class Prog:
    def __init__(self, nc, es):
        self.nc = nc
        self.es = es
        self.ops = []
        self.last_writer = {}
        self.readers = {}
        self.chan_last = {}
        self.region_fence = {}

    @staticmethod
    def _psum_banks(ap):
        esz = 4 if ap.dtype == F32 else 2
        start = ap.offset * esz
        ext = sum((cnt - 1) * abs(st) for st, cnt in ap.ap[1:]) * esz + esz
        return range(start // 2048, (start + ext - 1) // 2048 + 1)

    def _add(self, eng, fn, reads, writes, kind="c", chan=None):
        idx = len(self.ops)
        deps = set()
        rec = _Rec()
        fn(rec)
        assert rec.call is not None
        reads = list(reads)
        writes = list(writes)
        name_, a_, k_ = rec.call
        operands = [("out" if i == 0 else "in", v) for i, v in enumerate(a_)] + list(k_.items())
        for key, v in operands:
            if hasattr(v, "space") and str(v.space) == "PSUM":
                for b in self._psum_banks(v):
                    if key == "out":
                        if f"ps{b}" not in writes:
                            print("WARN: undeclared psum write", eng, name_, b)
                            writes.append(f"ps{b}")
                    else:
                        if f"ps{b}" not in reads:
                            print("WARN: undeclared psum read", eng, name_, b)
                            reads.append(f"ps{b}")
                    if eng in ("act", "dve"):
                        if f"psx{b}" not in writes:
                            writes.append(f"psx{b}")

        def lastw(r):
            w = self.last_writer.get(r)
            if w is None and "/" in r:
                w = self.region_fence.get(r.split("/")[0])
            return w

        for r in reads:
            w = lastw(r)
            if w is not None:
                deps.add(w)
        for w_ in writes:
            w = lastw(w_)
            if w is not None:
                deps.add(w)
            for rd in self.readers.get(w_, ()):
                deps.add(rd)
        if kind == "d":
            prev = self.chan_last.get(chan)
            if prev is not None:
                deps.add(prev)
            self.chan_last[chan] = idx
        deps.discard(idx)
        for r in reads:
            self.readers.setdefault(r, []).append(idx)
        for w_ in writes:
            self.last_writer[w_] = idx
            self.readers[w_] = []
        self.ops.append(dict(eng=eng, call=rec.call, deps=deps, kind=kind, chan=chan))
        return idx

    def pe(self, fn, reads=(), writes=()):
        return self._add("pe", fn, reads, writes)

    def act(self, fn, reads=(), writes=()):
        return self._add("act", fn, reads, writes)

    def dve(self, fn, reads=(), writes=()):
        return self._add("dve", fn, reads, writes)

    def pool(self, fn, reads=(), writes=()):
        return self._add("pool", fn, reads, writes)

    def dma(self, queue, chan, fn, reads=(), writes=()):
        return self._add(queue, fn, reads, writes, kind="d", chan=chan)

    def fence(self, region, fn):
        pref = region + "/"
        names = [k for k in set(self.last_writer) | set(self.readers) if k.startswith(pref)]
        idx = self._add("dve", fn, reads=(), writes=names + ["SCRcell"])
        for k in names:
            self.last_writer.pop(k, None)
            self.readers.pop(k, None)
        self.region_fence[region] = idx
        return idx

    def emit(self):
        nc, es = self.nc, self.es
        ops = self.ops
        n = len(ops)
        needs_signal = [False] * n
        for i, op in enumerate(ops):
            keep = set()
            for d in op["deps"]:
                dop = ops[d]
                if dop["kind"] == "c" and op["kind"] == "c" and dop["eng"] == "pe" and op["eng"] == "pe":
                    continue
                keep.add(d)
            op["deps"] = keep
            for d in keep:
                needs_signal[d] = True
        eng_sem = {e: es.enter_context(nc.semaphore(f"sem_{e}")) for e in ENGS}
        chan_sem = {}
        for op in ops:
            if op["kind"] == "d" and op["chan"] not in chan_sem:
                chan_sem[op["chan"]] = es.enter_context(nc.semaphore(f"semd_{len(chan_sem)}"))
        eng_cnt = {e: 0 for e in ENGS}
        chan_cnt = {c: 0 for c in chan_sem}
        sig = [None] * n
        for i, op in enumerate(ops):
            if op["kind"] == "d":
                chan_cnt[op["chan"]] += 16
                sig[i] = (chan_sem[op["chan"]], chan_cnt[op["chan"]])
            elif needs_signal[i]:
                eng_cnt[op["eng"]] += 1
                sig[i] = (eng_sem[op["eng"]], eng_cnt[op["eng"]])
        per_eng = {e: [] for e in ENGS}
        for i, op in enumerate(ops):
            per_eng[op["eng"]].append(i)
        final_waits = [(chan_sem[c], chan_cnt[c]) for c in chan_sem]
        block = es.enter_context(nc.Block())

        def make(ename):
            def body(eng):
                known = {}
                for i in per_eng[ename]:
                    op = ops[i]
                    waits = {}
                    for d in op["deps"]:
                        s, v = sig[d]
                        key = id(s)
                        if known.get(key, 0) >= v:
                            continue
                        if key not in waits or waits[key][1] < v:
                            waits[key] = (s, v)
                    for key, (s, v) in waits.items():
                        eng.wait_ge(s, v)
                        known[key] = v
                    name_, a_, k_ = op["call"]
                    ins = getattr(eng, name_)(*a_, **k_)
                    if sig[i] is not None:
                        s, v = sig[i]
                        ins.then_inc(s, 16 if op["kind"] == "d" else 1)
                if ename == "sp":
                    for s, v in final_waits:
                        if v > 0:
                            eng.wait_ge(s, v)
            return body

        block.tensor(make("pe"))
        block.scalar(make("act"))
        block.vector(make("dve"))
        block.gpsimd(make("pool"))
        block.sync(make("sp"))
        self.stats = {e: len(per_eng[e]) for e in ENGS}


def build_program(debug=False):
    nc = bass.Bass("TRN2", target_bir_lowering=False)
    es = ExitStack()

    def din(name, shape, dt=F32):
        return nc.dram_tensor(name, list(shape), dt, kind="ExternalInput").ap()

    def dout(name, shape, dt=F32):
        return nc.dram_tensor(name, list(shape), dt, kind="ExternalOutput").ap()

    xs = din("xs", [NTOK, D])
    sC = din("sC", [2, NH, DH, DH])
    sn = din("sn", [2, NH, DH])
    sm = din("sm", [1, 16])
    w_in = din("w_in", [D, 5152])
    gb = din("gb", [1, 32])
    pool_w = din("pool_w", [4, 256, 256])
    w_out = din("w_out", [D, D])
    ada_w = din("ada_w", [D, 6 * D])
    w1 = din("w1", [D, DFF])
    w2 = din("w2", [DFF, D])
    smallpl = din("smallpl", [128, 208])
    cb16 = din("cb16", [128, 384], BF16)
    cf32 = din("cf32", [128, 512])
    invc_s = din("invc_s", [1, 256])
    invc_p = din("invc_p", [1, 1024])
    ys = dout("ys", [NTOK, D])
    nC = dout("nC", [2, 2, NH, DH, DH])
    nn = dout("nn", [2, 2, NH, DH])
    nm = dout("nm", [2, 16])

    scr = nc.dram_tensor("scr_w", [97, 128, 4096], BF16, kind="Internal").ap()

    def sb(name, shape, dt=F32):
        return es.enter_context(nc.sbuf_tensor(name, list(shape), dt))

    R1 = sb("R1", [128, 32768], BF16)
    R2 = sb("R2", [128, 16, 2048], BF16)
    RX = sb("RX", [128, 8192], F32)
    RING = [sb(f"ring{i}", [128, 4096], BF16) for i in range(2)]
    GG = sb("GG", [128, 4096], F32)
    TS = sb("TS", [128, 1536], F32)
    CB = sb("CB", [128, 384], BF16)
    CF = sb("CF", [128, 512], F32)
    SPL = sb("SPL", [128, 208], F32)
    GBI = sb("GBI", [128, 32], F32)
    SMB = sb("SMB", [128, 16], F32)
    SCB = sb("SCB", [128, 32], BF16)
    MOD = sb("MOD", [128, 192], F32)
    DER = sb("DER", [128, 192], F32)
    WG = sb("WG", [128, 512], BF16)
    PW = sb("PW", [128, 512], BF16)
    INVC = sb("INVC", [128, 256], F32)
    STAT = sb("STAT", [128, 64], F32)
    DIAG = sb("DIAG", [128, 256], F32)
    MFIN = sb("MFIN", [128, 32], F32)
    SCR = sb("SCR", [128, 8], F32)
    FST = sb("FST", [128, 24], F32)
    PS = es.enter_context(nc.psum_tensor("PS", [128, 4096], F32))
    PSb = PS.bitcast(BF16)

    identb = CB[:, 0:128]
    maskU = CB[:, 128:256]
    maskL = CB[:, 256:384]
    identf = CF[:, 0:128]
    triU = CF[:, 128:256]
    triL = CF[:, 256:384]
    onesf = CF[:, 384:512]

    P = Prog(nc, es)
    dbg = {}
    STOP = os.environ.get("MK_STOP", "")
    DUMPS = os.environ.get("MK_DUMP", "").split(",")

    class _Stop(Exception):
        pass

    def checkpoint(name):
        if STOP == name:
            raise _Stop()

    def dump(name, ap, shape, reads, dt=F32):
        if name not in DUMPS:
            return
        o = nc.dram_tensor("dbg_" + name, list(shape), dt, kind="ExternalOutput").ap()
        P.dma("sp", "dbg_" + name, lambda e: e.dma_start(out=o, in_=ap), reads=reads)

    def bank(b):
        return PS[:, b * 512:(b + 1) * 512]

    def bankb(b):
        return PSb[:, b * 1024:(b + 1) * 1024]

    rr = {"ev": 0, "stat": 0, "ps": 0, "chan": 0}

    def evac_eng():
        rr["ev"] ^= 1
        if os.environ.get("MK_EV"):
            return os.environ["MK_EV"]
        return "act" if rr["ev"] else "dve"

    def stat_col():
        c = rr["stat"]
        rr["stat"] = (c + 1) % 64
        return c

    tasks = []

    def add_task(load_fn, compute_fn):
        tasks.append((load_fn, compute_fn))

    def ring_w(slot, a, b):
        return RING[slot][:, 0:a * b].rearrange("p (a b) -> p a b", a=a)

    def ring_f(slot):
        return RING[slot].bitcast(F32)

    def run_tasks():
        nslot = [0]
        pending = []
        loaded = []

        def do_load(t):
            lf, cf = t
            if lf is None:
                loaded.append((None, cf))
            else:
                s = nslot[0] % 2
                nslot[0] += 1
                lf(s)
                loaded.append((s, cf))

        i = 0
        nt = len(tasks)
        outstanding = 0
        qi = 0
        while qi < nt or loaded:
            while qi < nt and (tasks[qi][0] is None or outstanding < 2):
                if tasks[qi][0] is not None:
                    outstanding += 1
                do_load(tasks[qi])
                qi += 1
                if len(loaded) >= 3:
                    break
            s, cf = loaded.pop(0)
            cf(s)
            if rr.get("conv_rate"):
                emit_conv(rr["conv_rate"])
            if s is not None:
                outstanding -= 1
        tasks.clear()

    def wchan():
        rr["chan"] = (rr["chan"] + 1) % 4
        return f"w{rr['chan']}"

    rr["hchan"] = 0
    rr["cchan"] = 0

    def hchan():
        rr["hchan"] = (rr["hchan"] + 1) % 4
        return f"h{rr['hchan']}"

    def cchan():
        rr["cchan"] = (rr["cchan"] + 1) % 3
        return f"cv{rr['cchan']}"

    w_outv_ = w_out.rearrange("(k p) n -> p k n", p=128)
    w1v_ = w1.rearrange("(k p) n -> p k n", p=128)
    w2v_ = w2.rearrange("(j p) n -> p j n", p=128)
    conv_jobs = []
    for hf in range(2):
        for u4 in range(4):
            i = hf * 4 + u4
            conv_jobs.append((i, scr[i].rearrange("p (a b) -> p a b", a=4), w_outv_[:, u4 * 4:(u4 + 1) * 4, hf * 1024:(hf + 1) * 1024]))
    for u in range(32):
        conv_jobs.append((8 + u, scr[8 + u].rearrange("p (a b) -> p a b", a=16), w1v_[:, :, u * 256:(u + 1) * 256]))
    for hf in range(2):
        for u in range(16):
            i = 40 + hf * 16 + u
            conv_jobs.append((i, scr[i].rearrange("p (a b) -> p a b", a=4), w2v_[:, u * 4:(u + 1) * 4, hf * 1024:(hf + 1) * 1024]))

    w_inv_ = w_in.rearrange("(k p) n -> p k n", p=128)
    win_jobs = []
    for h_ in range(NH):
        for kind in range(2):
            c0_ = (0 if kind == 0 else 2 * M) + h_ * 128
            c1_ = (M if kind == 0 else 3 * M) + h_ * 128
            i = 72 + 8 * kind + h_
            dv = scr[i].rearrange("p (a b) -> p a b", a=16)
            win_jobs.append((i, dv[:, :, 0:128], w_inv_[:, :, c0_:c0_ + 128]))
            win_jobs.append((i, dv[:, :, 128:256], w_inv_[:, :, c1_:c1_ + 128]))
    for g_ in range(4):
        i = 88 + g_
        win_jobs.append((i, scr[i].rearrange("p (a b) -> p a b", a=16), w_inv_[:, :, 4128 + g_ * 256:4128 + (g_ + 1) * 256]))
    for g_ in range(4):
        i = 93 + g_
        win_jobs.append((i, scr[i][:, 0:512].rearrange("p (c n) -> p c n", c=2), pool_w[g_].rearrange("(c p) n -> p c n", p=128)))
    win_jobs.sort(key=lambda t: (t[0] - 72) % 8 if t[0] < 88 else 100)

    def emit_conv(n):
        for _ in range(n):
            if not conv_jobs:
                return
            i, o_ap, i_ap = conv_jobs.pop(0)
            rr["cvn"] = (rr.get("cvn", 0) + 1) % 20
            P.dma("pool", f"cvu{rr['cvn']}", lambda e: e.dma_start(out=o_ap, in_=i_ap), writes=[f"scr{i}"])

    def scr_load(slot, i):
        P.dma("sp", hchan(), lambda e: e.dma_start(out=RING[slot][:], in_=scr[i]), reads=[f"scr{i}"], writes=[f"ring{slot}"])

    def scr_store(slot, i):
        P.dma("sp", hchan(), lambda e: e.dma_start(out=scr[i], in_=RING[slot][:]), reads=[f"ring{slot}"], writes=[f"scr{i}"])

    def rstd_from(ss_ap, ss_res, n, out_col):
        tmpc = stat_col()
        P.act(lambda e, a=ss_ap, t=tmpc: e.activation(out=STAT[:, t:t + 1], in_=a, func=AF.Sqrt, scale=1.0 / n, bias=EPS),
              reads=ss_res, writes=[f"st{tmpc}"])
        P.dve(lambda e, t=tmpc, o=out_col: e.reciprocal(out=STAT[:, o:o + 1], in_=STAT[:, t:t + 1]),
              reads=[f"st{tmpc}"], writes=[f"st{out_col}"])

    def _body():
        P.dma("sp", "c0", lambda e: e.dma_start(out=CB[:], in_=cb16[:, :]), writes=["CB"])
        P.dma("sp", "c1", lambda e: e.dma_start(out=CF[:], in_=cf32[:, :]), writes=["CF"])
        P.dma("sp", "c2", lambda e: e.dma_start(out=SPL[:], in_=smallpl[:, :]), writes=["SPL"])
        P.dma("sp", "c3", lambda e: e.dma_start(out=GBI[:], in_=gb.partition_broadcast(128)), writes=["GBI"])
        P.dma("sp", "c4", lambda e: e.dma_start(out=SMB[:], in_=sm.partition_broadcast(128)), writes=["SMB"])
        P.dma("pool", "c5", lambda e: e.dma_start(out=WG[:].rearrange("p (k n) -> p k n", k=16),
                                                  in_=w_in.rearrange("(k p) n -> p k n", p=128)[:, :, 4096:4128]),
              writes=["WG"])
        condv = SPL[:, 0:32]
        adab = SPL[:, 32:128]
        nwv = SPL[:, 128:192]
        mnw = SPL[:, 192:200]
        psc = SPL[:, 200:208]
        P.act(lambda e: e.activation(out=SCB[:], in_=condv, func=AF.Silu), reads=["SPL"], writes=["SCB"])
        SCBv = SCB[:].rearrange("p (k c) -> p k c", c=2)
        ada_v = ada_w.rearrange("(k p) n -> p k n", p=128)
        conv_jobs[0:0] = win_jobs
        emit_conv(len(win_jobs))
        MODB = 7
        for u in range(48):
            def lf(slot, u=u):
                P.dma("pool", wchan(), lambda e, s=slot, u=u: e.dma_start(out=ring_w(s, 16, 256), in_=ada_v[:, :, u * 256:(u + 1) * 256]),
                      writes=[f"ring{slot}"])

            def cf(slot, u=u):
                wv = ring_w(slot, 16, 256)
                for c2 in range(2):
                    ch = u * 2 + c2
                    for k in range(16):
                        P.pe(lambda e, wv=wv, c2=c2, k=k, ch=ch: e.matmul(bank(MODB)[:, ch * 2:ch * 2 + 2], lhsT=wv[:, k, c2 * 128:(c2 + 1) * 128],
                                                                     rhs=SCBv[:, k, :], start=(k == 0), stop=(k == 15)),
                             reads=[f"ring{slot}", "SCB"], writes=[f"ps{MODB}"])
            add_task(lf, cf)
        run_tasks()
        emit_conv(1000)
        P.dve(lambda e: e.tensor_tensor(out=MOD[:].rearrange("p (a c) -> p a c", c=2), in0=bank(MODB)[:, 0:192].rearrange("p (a c) -> p a c", c=2),
                                        in1=adab.unsqueeze(2).to_broadcast([128, 96, 2]), op=ALU.add),
              reads=[f"ps{MODB}", "SPL"], writes=["MOD"])
        MODv = MOD[:].rearrange("p (a c) -> p a c", c=2)
        DERv = DER[:].rearrange("p (c i k) -> p c i k", c=2, i=6)

        def modv(i, c):
            return MODv[:, i * 16:(i + 1) * 16, c]

        def nw(i):
            return nwv[:, i * 16:(i + 1) * 16]

        for c in range(2):
            P.dve(lambda e, c=c: e.scalar_tensor_tensor(out=DERv[:, c, 0, :], in0=modv(1, c), scalar=1.0, in1=nw(0), op0=ALU.add, op1=ALU.mult),
                  reads=["MOD", "SPL"], writes=["DER"])
            P.dve(lambda e, c=c: e.tensor_copy(out=DERv[:, c, 1, :], in_=modv(0, c)), reads=["MOD"], writes=["DER"])
            P.dve(lambda e, c=c: e.scalar_tensor_tensor(out=DERv[:, c, 2, :], in0=modv(4, c), scalar=1.0, in1=nw(2), op0=ALU.add, op1=ALU.mult),
                  reads=["MOD", "SPL"], writes=["DER"])
            P.dve(lambda e, c=c: e.tensor_copy(out=DERv[:, c, 3, :], in_=modv(3, c)), reads=["MOD"], writes=["DER"])
            P.dve(lambda e, c=c: e.tensor_tensor(out=DERv[:, c, 4, :], in0=modv(2, c), in1=nw(1), op=ALU.mult), reads=["MOD", "SPL"], writes=["DER"])
            P.dve(lambda e, c=c: e.tensor_tensor(out=DERv[:, c, 5, :], in0=modv(5, c), in1=nw(3), op=ALU.mult), reads=["MOD", "SPL"], writes=["DER"])

        dump("mod", MOD[:], [128, 192], ["MOD"])
        dump("der", DER[:], [128, 192], ["DER"])
        checkpoint("setup")
        xnb = TS.bitcast(BF16)[:, 0:2048]
        xnf = TS[:, 0:1024]
        rtmp = [TS.bitcast(BF16)[:, 2048 + i * 512:2048 + (i + 1) * 512] for i in range(2)]

        def norm_stats(src_ap, src_res, junk=None):
            ssc = stat_col()
            jap, jres = junk if junk is not None else (xnb, ["xnp0", "xnp1"])
            P.act(lambda e, c=ssc: e.activation(out=jap, in_=src_ap, func=AF.Square, accum_out=STAT[:, c:c + 1]),
                  reads=src_res, writes=list(jres) + [f"st{ssc}"])
            rc = stat_col()
            rstd_from(STAT[:, ssc:ssc + 1], [f"st{ssc}"], D, rc)
            return rc

        def norm_transpose(src_ap, src_res, cond, gi, si, dst_fn, dst_res_fn, tbanks, xn_on_act=False, junk=None, rc=None):
            if rc is None:
                rc = norm_stats(src_ap, src_res, junk)
            for hb in range(2):
                xh = xnb[:, hb * 1024:(hb + 1) * 1024]
                sh_ = src_ap[:, hb * 1024:(hb + 1) * 1024]
                xr = f"xnp{hb}"
                if xn_on_act:
                    P.act(lambda e, c=rc: e.activation(out=xh, in_=sh_, func=AF.Identity, scale=STAT[:, c:c + 1]),
                          reads=src_res + [f"st{rc}"], writes=[xr])
                else:
                    P.dve(lambda e, c=rc: e.tensor_scalar(out=xh, in0=sh_, scalar1=STAT[:, c:c + 1], scalar2=None, op0=ALU.mult),
                          reads=src_res + [f"st{rc}"], writes=[xr])
                b = tbanks[hb]
                for kk in range(8):
                    k = hb * 8 + kk
                    P.pe(lambda e, b=b, kk=kk, k=k: e.transpose(out=bankb(b)[:, kk * 128:(kk + 1) * 128], in_=xnb[:, k * 128:(k + 1) * 128], identity=identb),
                         reads=[xr, "CB"], writes=[f"ps{b}"])
                for kk in range(8):
                    k = hb * 8 + kk
                    en = "act" if hb == 1 else "dve"
                    src = bankb(b)[:, kk * 128:(kk + 1) * 128]
                    g_ap = DERv[:, cond, gi, k:k + 1]
                    s_ap = DERv[:, cond, si, k:k + 1]
                    if en == "act":
                        P.act(lambda e, src=src, k=k, g=g_ap, s=s_ap: e.activation(out=dst_fn(k), in_=src, func=AF.Identity, scale=g, bias=s),
                              reads=[f"ps{b}", "DER"], writes=[dst_res_fn(k)])
                    else:
                        P.dve(lambda e, src=src, k=k, g=g_ap, s=s_ap: e.tensor_scalar(out=dst_fn(k), in0=src, scalar1=g, scalar2=s, op0=ALU.mult, op1=ALU.add),
                              reads=[f"ps{b}", "DER"], writes=[dst_res_fn(k)])

        groups = [
            dict(name="P", base=2048, NT=512, nch=4, cond=1, seqs=[(0, 2, 0), (2, 2, 1)], L=256, invc=invc_p),
            dict(name="S", base=0, NT=2048, nch=16, cond=0, seqs=[(0, 16, None)], L=64, invc=invc_s),
        ]
        if os.environ.get("MK_ONLY_P"):
            groups = groups[:1]
        RXb = RX.bitcast(BF16)
        GGb = None

        for gi_, G in enumerate(groups):
            base, NT, nch, cond, L = G["base"], G["NT"], G["nch"], G["cond"], G["L"]
            ntt = NT // 512
            hT = R1[:, 0:16 * NT].rearrange("p (k t) -> p k t", k=16)

            def hres(k, c):
                return f"R1/h{k}c{c}"

            def mres(k, c):
                return f"R2/k{k}c{c}"

            P.fence("R1", lambda e: e.memset(SCR[:, 0:1], 0.0))
            rr["conv_rate"] = 0
            first_pass = False


            R2f32 = R2.bitcast(F32)
            NXS = 4
            xslots = [R2f32[:, 2 * i:2 * i + 2, :] for i in range(NXS)]

            def a1_load(c):
                sl = c % NXS
                P.dma("sp", f"xa{sl}", lambda e: e.dma_start(out=xslots[sl], in_=xs[base + c * 128: base + (c + 1) * 128, :].rearrange("p (a b) -> p a b", a=2)),
                      writes=[f"R2/xs{sl}"])

            for c in range(min(NXS - 1, nch)):
                a1_load(c)
            a1junk = [(R2[:, 14 + i, 0:2048], [mres(14 + i, c_) for c_ in range(16)]) for i in range(2)]
            xflat = [xslots[i].rearrange("p a b -> p (a b)") for i in range(NXS)]
            rcs_ = {0: norm_stats(xflat[0], ["R2/xs0"], a1junk[0])}
            for c in range(nch):
                if c + NXS - 1 < nch:
                    a1_load(c + NXS - 1)
                if c + 1 < nch:
                    rcs_[c + 1] = norm_stats(xflat[(c + 1) % NXS], [f"R2/xs{(c + 1) % NXS}"], a1junk[(c + 1) % 2])
                sl = c % NXS
                norm_transpose(xflat[sl], [f"R2/xs{sl}"], cond, 0, 1,
                               lambda k, c=c: hT[:, k, c * 128:(c + 1) * 128], lambda k, c=c: hres(k, c), (0, 1), rc=rcs_[c])

            if G["name"] == "P":
                dump("hT", R1[:, 0:8192], [128, 8192], [hres(k, c) for k in range(16) for c in range(4)], BF16)
                checkpoint("A1")
            P.fence("R2", lambda e: e.memset(SCR[:, 1:2], 0.0))
            GB = 2
            Gt = GG[:, 0:nch * 32].rearrange("p (c x) -> p c x", x=32)
            G4 = GG[:, 0:nch * 32].rearrange("p (c d t h) -> p c d t h", d=2, t=2, h=8)
            li = G4[:, :, :, 0, :]
            lfp = G4[:, :, :, 1, :]

            def g16(off):
                return GG[:, off:off + nch * 16]

            def g4(off):
                return GG[:, off:off + nch * 16].rearrange("p (c d h) -> p c d h", d=2, h=8)

            def g3(off):
                return GG[:, off:off + nch * 16].rearrange("p (c x) -> p c x", x=16)

            O_LF, O_T1, O_T2, O_B, O_TOT, O_A, O_AM, O_MC, O_MP, O_W, O_GC, O_DL = [512 + 256 * i for i in range(12)]
            O_AMT = 3600
            for c in range(nch):
                for k in range(16):
                    P.pe(lambda e, c=c, k=k: e.matmul(bank(GB)[:, c * 32:(c + 1) * 32], lhsT=hT[:, k, c * 128:(c + 1) * 128],
                                                      rhs=WG[:].rearrange("p (k n) -> p k n", k=16)[:, k, :], start=(k == 0), stop=(k == 15)),
                         reads=[hres(k, c), "WG"], writes=[f"ps{GB}"])
            P.dve(lambda e: e.tensor_tensor(out=Gt, in0=bank(GB)[:, 0:nch * 32].rearrange("p (c x) -> p c x", x=32),
                                            in1=GBI[:].unsqueeze(1).to_broadcast([128, nch, 32]), op=ALU.add),
                  reads=[f"ps{GB}", "GBI"], writes=["GG/G"])
            checkpoint("g1")
            P.act(lambda e: e.activation(out=g4(O_T1), in_=lfp, func=AF.Abs), reads=["GG/G"], writes=["GG/T1"])
            P.act(lambda e: e.activation(out=g16(O_T2), in_=g16(O_T1), func=AF.Exp, scale=-1.0), reads=["GG/T1"], writes=["GG/T2"])
            P.act(lambda e: e.activation(out=g16(O_T1), in_=g16(O_T2), func=AF.Ln, bias=1.0), reads=["GG/T2"], writes=["GG/T1"])
            P.dve(lambda e: e.scalar_tensor_tensor(out=g4(O_LF), in0=lfp, scalar=0.0, in1=g4(O_T1), op0=ALU.min, op1=ALU.subtract),
                  reads=["GG/G", "GG/T1"], writes=["GG/LF"])
            checkpoint("g2")
            LF4 = g4(O_LF)
            for c in range(nch):
                P.pe(lambda e, c=c: e.matmul(bank(GB)[:, c * 32:c * 32 + 8], lhsT=triU, rhs=LF4[:, c, 0, :], start=True, stop=True),
                     reads=["GG/LF", "CF", "GG/G"], writes=[f"ps{GB}"])
                P.pe(lambda e, c=c: e.matmul(bank(GB)[:, c * 32 + 8:c * 32 + 16], lhsT=triL, rhs=LF4[:, c, 1, :], start=True, stop=True),
                     reads=["GG/LF", "CF"], writes=[f"ps{GB}"])
                P.pe(lambda e, c=c: e.matmul(bank(GB)[:, c * 32 + 16:c * 32 + 32], lhsT=onesf, rhs=g3(O_LF)[:, c, :], start=True, stop=True),
                     reads=["GG/LF", "CF"], writes=[f"ps{GB}"])
            pg = bank(GB)[:, 0:nch * 32].rearrange("p (c x) -> p c x", x=32)
            checkpoint("g3a")
            P.dve(lambda e: e.tensor_copy(out=g3(O_B), in_=pg[:, :, 0:16]), reads=[f"ps{GB}"], writes=["GG/B"])
            if os.environ.get("MK_X") == "a":
                P.dve(lambda e: e.tensor_copy(out=g3(O_TOT), in_=pg[:, :, 16:32]), reads=[f"ps{GB}"], writes=["GG/TOT"])
            elif os.environ.get("MK_X") == "c":
                P.act(lambda e: e.activation(out=g3(O_TOT), in_=pg[:, :, 16:32], func=AF.Identity), reads=[f"ps{GB}"], writes=["GG/TOT"])
            elif os.environ.get("MK_X") == "s":
                P.act(lambda e: e.copy(out=g3(O_TOT), in_=pg[:, :, 16:32]), reads=[f"ps{GB}", "GG/B"], writes=["GG/TOT"])
            else:
                P.act(lambda e: e.copy(out=g3(O_TOT), in_=pg[:, :, 16:32]), reads=[f"ps{GB}"], writes=["GG/TOT"])
            checkpoint("g3b")
            P.dve(lambda e: e.tensor_tensor(out=g4(O_A), in0=li, in1=g4(O_B), op=ALU.subtract), reads=["GG/G", "GG/B"], writes=["GG/A"])
            checkpoint("g3")
            ncol = nch * 16
            nhalf = (ncol + 127) // 128
            AMB = 3
            for hf in range(nhalf):
                w_ = min(128, ncol - hf * 128)
                P.pe(lambda e, hf=hf, w_=w_: e.transpose(out=bank(AMB)[0:w_, hf * 128:hf * 128 + 128], in_=g16(O_A)[:, hf * 128:hf * 128 + w_], identity=identf),
                     reads=["GG/A", "CF"], writes=[f"ps{AMB}"])
                P.dve(lambda e, hf=hf, w_=w_: e.reduce_max(out=GG[0:w_, O_AMT + hf:O_AMT + hf + 1], in_=bank(AMB)[0:w_, hf * 128:hf * 128 + 128], axis=AX.X),
                      reads=[f"ps{AMB}"], writes=["GG/AMT"])
                P.dve(lambda e, hf=hf, w_=w_: e.tensor_scalar(out=DIAG[0:w_, hf * 128:hf * 128 + w_], in0=identf[0:w_, 0:w_], scalar1=GG[0:w_, O_AMT + hf:O_AMT + hf + 1],
                                                            scalar2=None, op0=ALU.mult),
                      reads=["GG/AMT", "CF"], writes=["DIAG"])
                P.pe(lambda e, hf=hf, w_=w_: e.matmul(bank(AMB)[:, 256 + hf * 128:256 + hf * 128 + w_], lhsT=onesf[0:w_, :], rhs=DIAG[0:w_, hf * 128:hf * 128 + w_],
                                                      start=True, stop=True),
                     reads=["DIAG", "CF"], writes=[f"ps{AMB}"])
            P.dve(lambda e: e.tensor_copy(out=g16(O_AM), in_=bank(AMB)[:, 256:256 + ncol]), reads=[f"ps{AMB}"], writes=["GG/AM"])
            checkpoint("g4")
            MFv = MFIN[:].rearrange("p (s l) -> p s l", s=2)
            for (c0, n, pidx) in G["seqs"]:
                for d in range(2):
                    order = list(range(c0, c0 + n)) if d == 0 else list(range(c0 + n - 1, c0 - 1, -1))
                    ls = slice(d * 8, d * 8 + 8)
                    first = order[0]
                    if pidx is None:
                        P.dve(lambda e, first=first, ls=ls: e.tensor_copy(out=g3(O_MP)[:, first, ls], in_=SMB[:, ls]), reads=["SMB"], writes=["GG/MP"])
                    else:
                        P.dve(lambda e, first=first, ls=ls: e.memset(g3(O_MP)[:, first, ls], 0.0), writes=["GG/MP"])
                    for i, c in enumerate(order):
                        P.dve(lambda e, c=c, ls=ls: e.tensor_tensor(out=g3(O_MC)[:, c, ls], in0=g3(O_MP)[:, c, ls], in1=g3(O_AM)[:, c, ls], op=ALU.max),
                              reads=["GG/MP", "GG/AM"], writes=["GG/MC"])
                        if i + 1 < n:
                            dst = g3(O_MP)[:, order[i + 1], ls]
                            dres = "GG/MP"
                        else:
                            dst = MFv[:, (pidx or 0), ls]
                            dres = "MFIN"
                        P.dve(lambda e, c=c, ls=ls, dst=dst: e.tensor_tensor(out=dst, in0=g3(O_TOT)[:, c, ls], in1=g3(O_MC)[:, c, ls], op=ALU.add),
                              reads=["GG/TOT", "GG/MC"], writes=[dres])
                if pidx is not None:
                    P.dma("sp", "om", lambda e, pidx=pidx: e.dma_start(out=nm[pidx:pidx + 1, :], in_=MFv[0:1, pidx, :]), reads=["MFIN"])
            checkpoint("g5")
            P.dve(lambda e: e.tensor_tensor(out=g16(O_T1), in0=g16(O_A), in1=g16(O_MC), op=ALU.subtract), reads=["GG/A", "GG/MC"], writes=["GG/T1"])
            P.act(lambda e: e.activation(out=g16(O_W), in_=g16(O_T1), func=AF.Exp), reads=["GG/T1"], writes=["GG/W"])
            P.dve(lambda e: e.tensor_tensor(out=g16(O_T2), in0=g16(O_MP), in1=g16(O_MC), op=ALU.subtract), reads=["GG/MP", "GG/MC"], writes=["GG/T2"])
            P.act(lambda e: e.activation(out=g16(O_GC), in_=g16(O_T2), func=AF.Exp), reads=["GG/T2"], writes=["GG/GC"])
            P.dve(lambda e: e.tensor_tensor(out=g16(O_T1), in0=g16(O_B), in1=g16(O_MC), op=ALU.add), reads=["GG/B", "GG/MC", "GG/W"], writes=["GG/T1"])
            P.act(lambda e: e.activation(out=g16(O_DL), in_=g16(O_T1), func=AF.Exp, scale=-1.0), reads=["GG/T1"], writes=["GG/DL"])
            Wv, GCv, DLv = g3(O_W), g3(O_GC), g3(O_DL)
            if G["name"] == "P":
                dump("gates", GG[:], [128, 4096], ["GG/W", "GG/GC", "GG/DL", "GG/MC", "GG/B", "GG/A", "GG/G", "GG/AM", "GG/MP", "GG/TOT", "GG/LF"])
                checkpoint("gates")

            R2flat = R2[:].rearrange("p k t -> p (k t)")
            SETB = [RXb, R2flat[:, 16384:32768]]
            SETP = ["RX/s0", "R2/s1"]

            def sets(si):
                Bf = SETB[si]
                return dict(
                    qT=Bf[:, 0:NT], kT=Bf[:, 2048:2048 + NT],
                    ktok=Bf[:, 4096:4096 + nch * 128].rearrange("p (c d) -> p c d", d=128),
                    vtok=Bf[:, 6144:6144 + nch * 130].rearrange("p (c d) -> p c d", d=130),
                    sigo=Bf[:, 8256:8256 + nch * 128].rearrange("p (c d) -> p c d", d=128),
                    pre=SETP[si])
            T0 = 10304
            sTm = [RXb[:, T0 + i * 128:T0 + (i + 1) * 128] for i in range(4)]
            vwb = [RXb[:, T0 + 512 + i * 130:T0 + 512 + (i + 1) * 130] for i in range(6)]
            hnb = [RXb[:, T0 + 1292 + i * 128:T0 + 1292 + (i + 1) * 128] for i in range(2)]
            F0 = (T0 + 1548) // 2
            hs8 = [RX[:, F0 + i * 128:F0 + (i + 1) * 128] for i in range(8)]
            Cst = [[RX[:, F0 + 1024 + (d * 2 + j) * 130:F0 + 1024 + (d * 2 + j + 1) * 130] for j in range(2)] for d in range(2)]
            assert F0 + 1024 + 4 * 130 <= 8192
            CSB = R2flat[:, 16384 + 10304:16384 + 10304 + 32 * 130].rearrange("p (d c x) -> p d c x", d=2, x=130)
            w_inv = w_in.rearrange("(k p) n -> p k n", p=128)
            heads = list(range(NH))

            def load_unit(slot, h, kind):
                wv = ring_w(slot, 16, 256)
                si = 72 + 8 * kind + h
                if not first_pass:
                    scr_load(slot, si)
                    return
                c0_ = (0 if kind == 0 else 2 * M) + h * 128
                c1_ = (M if kind == 0 else 3 * M) + h * 128
                P.dma("pool", wchan(), lambda e: e.dma_start(out=wv[:, :, 0:128], in_=w_inv[:, :, c0_:c0_ + 128]), writes=[f"ring{slot}"])
                P.dma("pool", wchan(), lambda e: e.dma_start(out=wv[:, :, 128:256], in_=w_inv[:, :, c1_:c1_ + 128]), writes=[f"ring{slot}"])
                scr_store(slot, si)

            def gen_proj(h):
                S = sets(h % 2)
                pre = S["pre"]
                wv = ring_w(0, 16, 256)
                for tt in range(ntt):
                    for which in range(2):
                        b = which
                        for k in range(16):
                            P.pe(lambda e: e.matmul(bank(b), lhsT=wv[:, k, which * 128:(which + 1) * 128], rhs=hT[:, k, tt * 512:(tt + 1) * 512],
                                                    start=(k == 0), stop=(k == 15)),
                                 reads=["ring0"] + [hres(k, tt * 4 + j) for j in range(4)], writes=[f"ps{b}"])
                        if which == 0:
                            P.act(lambda e: e.activation(out=S["qT"][:, tt * 512:(tt + 1) * 512], in_=bank(b), func=AF.Identity, scale=float(DH ** -0.5)),
                                  reads=[f"ps{b}"], writes=[f"{pre}q{tt}"])
                        else:
                            P.dve(lambda e: e.tensor_copy(out=S["kT"][:, tt * 512:(tt + 1) * 512], in_=bank(b)),
                                  reads=[f"ps{b}"], writes=[f"{pre}k{tt}"])
                        yield
                if h + 1 < NH:
                    load_unit(0, h + 1, 0)
                for c in range(nch):
                    b = c % 2
                    P.pe(lambda e: e.transpose(out=bankb(b)[:, 0:128], in_=S["kT"][:, c * 128:(c + 1) * 128], identity=identb),
                         reads=[f"{pre}k{c // 4}", "CB"], writes=[f"ps{b}"])
                    P.dve(lambda e: e.tensor_copy(out=S["ktok"][:, c, :], in_=bankb(b)[:, 0:128]), reads=[f"ps{b}"], writes=[f"{pre}kt{c}"])
                    if c % 2 == 1:
                        yield
                wv1 = ring_w(1, 16, 256)
                P.dve(lambda e: e.memset(S["vtok"][:, :, 128:130], 1.0), writes=[f"{pre}vones"])
                for c in range(nch):
                    b = c % 2
                    for k in range(16):
                        P.pe(lambda e: e.matmul(bank(b)[:, 0:256], lhsT=hT[:, k, c * 128:(c + 1) * 128], rhs=wv1[:, k, :], start=(k == 0), stop=(k == 15)),
                             reads=["ring1", hres(k, c)], writes=[f"ps{b}"])
                    P.dve(lambda e: e.tensor_copy(out=S["vtok"][:, c, 0:128], in_=bank(b)[:, 0:128]), reads=[f"ps{b}"], writes=[f"{pre}v{c}"])
                    P.act(lambda e: e.activation(out=S["sigo"][:, c, :], in_=bank(b)[:, 128:256], func=AF.Sigmoid), reads=[f"ps{b}"], writes=[f"{pre}o{c}"])
                    yield
                if h + 1 < NH:
                    load_unit(1, h + 1, 1)

            def run_scan(h, filler):
                S = sets(h % 2)
                pre = S["pre"]
                qT, kT, ktok, vtok, sigo = S["qT"], S["kT"], S["ktok"], S["vtok"], S["sigo"]
                for (c0, n, pidx) in G["seqs"]:
                    for d in range(2):
                        if pidx is None:
                            P.dma("sp", f"st{d}", lambda e: e.dma_start(out=Cst[d][0][:, 0:128], in_=sC[d, h, :, :]), writes=[f"RX/C{d}0"])
                            P.dma("sp", f"st{d}", lambda e: e.dma_start(out=Cst[d][0][:, 128:129], in_=sn[d, h, :].rearrange("(p o) -> p o", o=1)),
                                  writes=[f"RX/C{d}0"])
                        else:
                            P.dve(lambda e: e.memset(Cst[d][0][:, 0:130], 0.0), writes=[f"RX/C{d}0"])

                    def chunk_of(d, i):
                        return c0 + i if d == 0 else c0 + n - 1 - i

                    def prefetch(d, i):
                        c = chunk_of(d, i)
                        lane = d * 8 + h
                        vi = d * 2 + (i % 2)
                        cb_ = 2 + d * 2 + (i % 2)
                        P.act(lambda e: e.activation(out=vwb[vi][:, 0:129], in_=vtok[:, c, 0:129], func=AF.Identity, scale=Wv[:, c, lane:lane + 1]),
                              reads=[f"{pre}v{c}", f"{pre}vones", "GG/W"], writes=[f"RX/vw{vi}"])
                        P.pe(lambda e: e.matmul(bank(cb_)[:, 0:129], lhsT=ktok[:, c, :], rhs=vwb[vi][:, 0:129], start=True, stop=True),
                             reads=[f"{pre}kt{c}", f"RX/vw{vi}"], writes=[f"ps{cb_}"])

                    for i in range(min(2, n)):
                        for d in range(2):
                            prefetch(d, i)
                    for i in range(n):
                        for d in range(2):
                            c = chunk_of(d, i)
                            lane = d * 8 + h
                            cur, nx = i % 2, (i + 1) % 2
                            cb_ = 2 + d * 2 + (i % 2)
                            P.act(lambda e: e.activation(out=CSB[:, d, c, 0:129], in_=Cst[d][cur][:, 0:129], func=AF.Identity, scale=GCv[:, c, lane:lane + 1]),
                                  reads=[f"RX/C{d}{cur}", "GG/GC"], writes=[f"R2/cs{d}_{c}"])
                            P.dve(lambda e: e.scalar_tensor_tensor(out=Cst[d][nx][:, 0:129], in0=Cst[d][cur][:, 0:129], scalar=GCv[:, c, lane:lane + 1],
                                                                   in1=bank(cb_)[:, 0:129], op0=ALU.mult, op1=ALU.add),
                                  reads=[f"RX/C{d}{cur}", f"ps{cb_}", "GG/GC"], writes=[f"RX/C{d}{nx}"])
                        if i + 2 < n:
                            for d in range(2):
                                prefetch(d, i + 2)

                    if pidx is not None:
                        fin = n % 2
                        for d in range(2):
                            P.dma("sp", f"oc{d}", lambda e: e.dma_start(out=nC[pidx, d, h, :, :], in_=Cst[d][fin][:, 0:128]), reads=[f"RX/C{d}{fin}"])
                            P.dma("sp", f"on{d}", lambda e: e.dma_start(out=nn[pidx, d, h, :].rearrange("(p o) -> p o", o=1), in_=Cst[d][fin][:, 128:129]),
                                  reads=[f"RX/C{d}{fin}"])
                    batch = []

                    def flush():
                        nb = len(batch)
                        if nb == 0:
                            return
                        P.act(lambda e: e.activation(out=FST[:, 8:8 + nb], in_=FST[:, 0:nb], func=AF.Sqrt, scale=1.0 / DH, bias=EPS),
                              reads=[f"fs{j}" for j in range(nb)], writes=["fstd"])
                        P.dve(lambda e: e.reciprocal(out=FST[:, 16:16 + nb], in_=FST[:, 8:8 + nb]), reads=["fstd"], writes=["frs"])
                        def stt_(j, c):
                            hb_ = j % 2
                            P.dve(lambda e: e.scalar_tensor_tensor(out=hnb[hb_], in0=hs8[j], scalar=FST[:, 16 + j:17 + j], in1=sigo[:, c, :], op0=ALU.mult, op1=ALU.mult),
                                  reads=[f"RX/hs{j}", "frs", f"{pre}o{c}"], writes=[f"RX/hn{hb_}"])
                        stt_(0, batch[0])
                        for j, c in enumerate(batch):
                            hb_ = j % 2
                            if j % 2 == 1:
                                filler(1)
                            tb = 0 + (j % 2)
                            P.pe(lambda e: e.transpose(out=bankb(tb)[:, 0:128], in_=hnb[hb_], identity=identb), reads=[f"RX/hn{hb_}", "CB"], writes=[f"ps{tb}"])
                            if j + 1 < nb:
                                stt_(j + 1, batch[j + 1])
                            P.dve(lambda e: e.tensor_scalar(out=R2[:, h, c * 128:(c + 1) * 128], in0=bankb(tb)[:, 0:128], scalar1=mnw[:, h:h + 1], scalar2=None, op0=ALU.mult),
                                  reads=[f"ps{tb}", "SPL"], writes=[mres(h, c)])
                        batch.clear()

                    def emit_sT(c):
                        sb_ = 2 + (c % 2)
                        P.pe(lambda e: e.matmul(bank(sb_)[:, 0:128], lhsT=kT[:, c * 128:(c + 1) * 128], rhs=qT[:, c * 128:(c + 1) * 128], start=True, stop=True),
                             reads=[f"{pre}k{c // 4}", f"{pre}q{c // 4}"], writes=[f"ps{sb_}"])

                    def emit_masks(c):
                        sb_ = 2 + (c % 2)
                        for d in range(2):
                            lane = d * 8 + h
                            msk = maskU if d == 0 else maskL
                            si_ = (c % 2) * 2 + d
                            P.dve(lambda e: e.scalar_tensor_tensor(out=sTm[si_], in0=bank(sb_)[:, 0:128], scalar=Wv[:, c, lane:lane + 1], in1=msk, op0=ALU.mult, op1=ALU.mult),
                                  reads=[f"ps{sb_}", "CB", "GG/W"], writes=[f"RX/sTm{si_}"])

                    emit_sT(c0)
                    emit_masks(c0)
                    for c in range(c0, c0 + n):
                        j = len(batch)
                        batch.append(c)
                        if c + 1 < c0 + n:
                            emit_sT(c + 1)
                            emit_masks(c + 1)
                        filler(1)
                        nbase = 4 + 2 * (c % 2)
                        for d in range(2):
                            si_ = (c % 2) * 2 + d
                            nb_ = nbase + d
                            P.pe(lambda e: e.matmul(bank(nb_)[:, 0:129], lhsT=sTm[si_], rhs=vtok[:, c, 0:129], start=True, stop=False),
                                 reads=[f"RX/sTm{si_}", f"{pre}v{c}", f"{pre}vones"], writes=[f"ps{nb_}"])
                            P.pe(lambda e: e.matmul(bank(nb_)[:, 0:129], lhsT=qT[:, c * 128:(c + 1) * 128], rhs=CSB[:, d, c, 0:129], start=False, stop=True),
                                 reads=[f"{pre}q{c // 4}", f"R2/cs{d}_{c}"], writes=[f"ps{nb_}"])
                        dc_ = stat_col()
                        if dc_ == 63:
                            dc_ = stat_col()
                        stat_col()
                        den2 = PS[:, nbase * 512:(nbase + 2) * 512].rearrange("p (b x) -> p b x", b=2)[:, :, 128:129]
                        P.act(lambda e: e.activation(out=STAT[:, dc_:dc_ + 2].unsqueeze(2), in_=den2, func=AF.Abs),
                              reads=[f"ps{nbase}", f"ps{nbase + 1}"], writes=[f"st{dc_}", f"st{dc_ + 1}"])
                        P.dve(lambda e: e.tensor_tensor(out=STAT[:, dc_:dc_ + 2], in0=STAT[:, dc_:dc_ + 2], in1=DLv[:, c, h:16:8], op=ALU.max),
                              reads=[f"st{dc_}", f"st{dc_ + 1}", "GG/DL"], writes=[f"st{dc_}", f"st{dc_ + 1}"])
                        P.dve(lambda e: e.reciprocal(out=STAT[:, dc_:dc_ + 2], in_=STAT[:, dc_:dc_ + 2]), reads=[f"st{dc_}", f"st{dc_ + 1}"], writes=[f"st{dc_}", f"st{dc_ + 1}"])
                        P.act(lambda e: e.activation(out=hs8[j], in_=bank(nbase)[:, 0:128], func=AF.Identity, scale=STAT[:, dc_:dc_ + 1]),
                              reads=[f"ps{nbase}", f"st{dc_}"], writes=[f"RX/hs{j}"])
                        P.dve(lambda e: e.scalar_tensor_tensor(out=hs8[j], in0=bank(nbase + 1)[:, 0:128], scalar=STAT[:, dc_ + 1:dc_ + 2], in1=hs8[j], op0=ALU.mult, op1=ALU.add),
                              reads=[f"ps{nbase + 1}", f"st{dc_ + 1}", f"RX/hs{j}"], writes=[f"RX/hs{j}"])
                        P.act(lambda e: e.activation(out=hnb[j % 2], in_=hs8[j], func=AF.Square, accum_out=FST[:, j:j + 1]),
                              reads=[f"RX/hs{j}"], writes=[f"RX/hn{j % 2}", f"fs{j}"])
                        filler(1)
                        if len(batch) == 8:
                            flush()
                    flush()

            def drain(g_, nsteps=None):
                k_ = 0
                for _ in g_:
                    k_ += 1
                    if nsteps is not None and k_ >= nsteps:
                        return False
                return True

            load_unit(0, heads[0], 0)
            load_unit(1, heads[0], 1)
            drain(gen_proj(heads[0]))
            for h in heads:
                gp = gen_proj(h + 1) if h + 1 < NH else iter(())
                run_scan(h, lambda k, gp=gp: drain(gp, k))
                drain(gp)
                if rr.get("conv_rate"):
                    emit_conv(rr["conv_rate"] * 3)
            P.fence("R2", lambda e: e.memset(SCR[:, 1:2], 0.0))

            if G["name"] == "P":
                dump("mixh", R2[:, 0:8, 0:512], [128, 8, 512], [mres(k, c) for k in range(8) for c in range(4)], BF16)
                checkpoint("heads")
            P.fence("RX", lambda e: e.memset(SCR[:, 1:2], 0.0))
            rows = 512 // L
            LP = L + 16
            PBUFS = [[RX[:, (s_ * 3 + i) * 640:(s_ * 3 + i) * 640 + rows * LP].rearrange("p (r l) -> p r l", l=LP) for i in range(3)] for s_ in range(2)]
            ppT4 = [[RXb[:, 7680 + (p_ * 2 + i) * 512:7680 + (p_ * 2 + i + 1) * 512] for i in range(2)] for p_ in range(2)]
            for i in range(6):
                P.dve(lambda e, i=i: e.memset(RX[:, i * 640:(i + 1) * 640], 0.0), writes=[f"RX/pb{i // 3}_{i % 3}"])
            for g in range(4):
                win = (2, 4, 8, 16)[g]
                half = win // 2

                def lf_u(slot, g=g):
                    if not first_pass:
                        scr_load(slot, 88 + g)
                        return
                    P.dma("pool", wchan(), lambda e, s=slot: e.dma_start(out=ring_w(s, 16, 256), in_=w_inv[:, :, 4128 + g * 256:4128 + (g + 1) * 256]),
                          writes=[f"ring{slot}"])
                    scr_store(slot, 88 + g)

                def cf_u(slot, g=g, win=win, half=half):
                    wv = ring_w(slot, 16, 256)
                    P.dma("sp", "pw", lambda e: e.dma_start(out=PW[:], in_=scr[93 + g][:, 0:512]), reads=[f"scr{93 + g}"], writes=["PW"])
                    P.dma("sp", "ic", lambda e: e.dma_start(out=INVC[:, 0:L], in_=G["invc"][:, g * L:(g + 1) * L].partition_broadcast(128)), writes=["INVC"])
                    PWv = PW[:].rearrange("p (c n) -> p c n", c=2)
                    def emit_hp(tt):
                        ppT = ppT4[tt % 2]
                        for dc in range(2):
                            b = 2 + dc
                            for cc in range(2):
                                P.pe(lambda e, b=b, dc=dc, cc=cc: e.matmul(bank(b), lhsT=PWv[:, cc, dc * 128:(dc + 1) * 128], rhs=ppT[cc], start=(cc == 0), stop=(cc == 1)),
                                     reads=["PW", f"RX/pp{tt % 2}_{cc}"], writes=[f"ps{b}"])
                            kk = 8 + g * 2 + dc
                            P.act(lambda e, b=b, kk=kk, tt=tt: e.activation(out=R2[:, kk, tt * 512:(tt + 1) * 512], in_=bank(b), func=AF.Identity, scale=psc[:, kk - 8:kk - 7]),
                                  reads=[f"ps{b}", "SPL"], writes=[mres(kk, tt * 4 + j) for j in range(4)])

                    for tt in range(ntt):
                        ppT = ppT4[tt % 2]
                        for cc in range(2):
                            b = (tt * 2 + cc) % 2
                            for k in range(16):
                                P.pe(lambda e, b=b, k=k, cc=cc, tt=tt: e.matmul(bank(b), lhsT=wv[:, k, cc * 128:(cc + 1) * 128], rhs=hT[:, k, tt * 512:(tt + 1) * 512],
                                                                            start=(k == 0), stop=(k == 15)),
                                     reads=[f"ring{slot}"] + [hres(k, tt * 4 + j) for j in range(4)], writes=[f"ps{b}"])
                            PBUF = PBUFS[cc]
                            pbn = lambda i_: f"RX/pb{cc}_{i_}"
                            P.act(lambda e, b=b: e.copy(out=PBUF[0][:, :, 8:8 + L], in_=bank(b).rearrange("p (r l) -> p r l", l=L)),
                                  reads=[f"ps{b}"], writes=[pbn(0)])
                            cur = 0
                            sh = 1
                            nxt = 1
                            while sh < win:
                                src = PBUF[cur]
                                dst = PBUF[nxt]
                                P.dve(lambda e, src=src, dst=dst, sh=sh: e.tensor_tensor(out=dst[:, :, 8:LP], in0=src[:, :, 8:LP], in1=src[:, :, 8 - sh:LP - sh], op=ALU.add),
                                      reads=[pbn(cur)], writes=[pbn(nxt)])
                                cur = nxt
                                nxt = 2 if cur == 1 else 1
                                sh *= 2
                            src = PBUF[cur]
                            dst = PBUF[nxt]
                            P.dve(lambda e, src=src, dst=dst, half=half: e.tensor_tensor(out=dst[:, :, 8:8 + L], in0=src[:, :, 8 + half - 1:8 + half - 1 + L],
                                                                                     in1=INVC[:, 0:L].unsqueeze(1).to_broadcast([128, rows, L]), op=ALU.mult),
                                  reads=[pbn(cur), "INVC"], writes=[pbn(nxt)])
                            P.dve(lambda e, dst=dst, cc=cc: e.tensor_tensor(out=ppT[cc].rearrange("p (r l) -> p r l", l=L), in0=dst[:, :, 8:8 + L], in1=PBUF[0][:, :, 8:8 + L], op=ALU.subtract),
                                  reads=[pbn(nxt), pbn(0)], writes=[f"RX/pp{tt % 2}_{cc}"])
                        if tt >= 1:
                            emit_hp(tt - 1)
                    emit_hp(ntt - 1)
                add_task(lf_u, cf_u)
            run_tasks()
            if G["name"] == "P":
                dump("mixp", R2[:, 8:16, 0:512], [128, 8, 512], [mres(k, c) for k in range(8, 16) for c in range(4)], BF16)
                checkpoint("pool")

            rr["conv_rate"] = 0
            P.fence("R1", lambda e: e.memset(SCR[:, 2:3], 0.0))
            P.fence("RX", lambda e: e.memset(SCR[:, 3:4], 0.0))
            P.fence("GG", lambda e: e.memset(SCR[:, 6:7], 0.0))
            for which in range(2):
                for k in range(16):
                    dsl = (which * 16 + k) % 2
                    P.dve(lambda e, which=which, k=k, dsl=dsl: e.tensor_scalar(out=DIAG[:, dsl * 128:(dsl + 1) * 128], in0=identf, scalar1=DERv[:, cond, 4 + which, k:k + 1],
                                                                             scalar2=None, op0=ALU.mult),
                          reads=["DER", "CF"], writes=[f"DIAG{dsl}"])
                    b = dsl
                    P.pe(lambda e, dsl=dsl, b=b: e.matmul(bank(b)[:, 0:128], lhsT=onesf, rhs=DIAG[:, dsl * 128:(dsl + 1) * 128], start=True, stop=True),
                         reads=[f"DIAG{dsl}", "CF"], writes=[f"ps{b}"])
                    P.act(lambda e, which=which, k=k, b=b: e.copy(out=GG[:, which * 2048 + k * 128:which * 2048 + (k + 1) * 128], in_=bank(b)[:, 0:128]),
                          reads=[f"ps{b}"], writes=["GG/GA" if which == 0 else "GG/GF"])
            GA = GG[:, 0:2048]
            GF = GG[:, 2048:4096]
            X1 = RX[:].rearrange("p (s d) -> p s d", s=4)
            R2f = R2.bitcast(F32)
            aT = R1[:].rearrange("p (j t) -> p j t", j=64)
            w_outv = w_out.rearrange("(k p) n -> p k n", p=128)
            w1v = w1.rearrange("(k p) n -> p k n", p=128)
            w2v = w2.rearrange("(j p) n -> p j n", p=128)
            for tt in range(ntt):
                r0 = base + tt * 512
                ssB = {}

                def tokc(sub, tt=tt):
                    return tt * 4 + sub

                for hf in range(2):
                    for u4 in range(4):
                        def lf_wo(slot, hf=hf, u4=u4):
                            if True:
                                scr_load(slot, hf * 4 + u4)
                            else:
                                P.dma("pool", wchan(), lambda e, s=slot: e.dma_start(out=ring_w(s, 4, 1024), in_=w_outv[:, u4 * 4:(u4 + 1) * 4, hf * 1024:(hf + 1) * 1024]),
                                      writes=[f"ring{slot}"])

                        def cf_wo(slot, hf=hf, u4=u4, tt=tt):
                            wv = ring_w(slot, 4, 1024)
                            for sub in range(4):
                                for kk in range(4):
                                    k = u4 * 4 + kk
                                    for cbb in range(2):
                                        b = sub * 2 + cbb
                                        P.pe(lambda e, b=b, k=k, kk=kk, cbb=cbb, sub=sub: e.matmul(bank(b), lhsT=R2[:, k, (tt * 4 + sub) * 128:(tt * 4 + sub + 1) * 128],
                                                                                            rhs=wv[:, kk, cbb * 512:(cbb + 1) * 512], start=(k == 0), stop=(k == 15)),
                                             reads=[f"ring{slot}", mres(k, tt * 4 + sub)], writes=[f"ps{b}"])
                            if u4 == 3:
                                for sub in range(4):
                                    pss = PS[:, sub * 1024:(sub + 1) * 1024]
                                    sc_ = stat_col()
                                    ssB[(sub, hf)] = sc_
                                    P.act(lambda e, pss=pss, sc_=sc_: e.activation(out=TS.bitcast(BF16)[:, 2048:3072], in_=pss, func=AF.Square, accum_out=STAT[:, sc_:sc_ + 1]),
                                          reads=[f"ps{sub * 2}", f"ps{sub * 2 + 1}"], writes=["rt0", "rt1", f"st{sc_}"])
                                    if hf == 0:
                                        P.dve(lambda e, pss=pss, sub=sub: e.tensor_copy(out=X1[:, sub, 0:1024], in_=pss),
                                              reads=[f"ps{sub * 2}", f"ps{sub * 2 + 1}"], writes=[f"RX/x{sub}a"])
                        add_task(lf_wo, cf_wo)
                for sub in range(4):
                    def lf_x(slot, sub=sub, r0=r0):
                        P.dma("sp", f"x{slot}", lambda e, s=slot: e.dma_start(out=ring_f(s)[:, 0:2048], in_=xs[r0 + sub * 128:r0 + (sub + 1) * 128, :]),
                              writes=[f"ring{slot}"])

                    def cf_x(slot, sub=sub, tt=tt):
                        a0, a1 = ssB[(sub, 0)], ssB[(sub, 1)]
                        P.dve(lambda e: e.tensor_tensor(out=STAT[:, a0:a0 + 1], in0=STAT[:, a0:a0 + 1], in1=STAT[:, a1:a1 + 1], op=ALU.add),
                              reads=[f"st{a0}", f"st{a1}"], writes=[f"st{a0}"])
                        rc_ = stat_col()
                        rstd_from(STAT[:, a0:a0 + 1], [f"st{a0}"], D, rc_)
                        pss = PS[:, sub * 1024:(sub + 1) * 1024]
                        P.dve(lambda e: e.scalar_tensor_tensor(out=X1[:, sub, 0:1024], in0=X1[:, sub, 0:1024], scalar=STAT[:, rc_:rc_ + 1], in1=GA[:, 0:1024],
                                                               op0=ALU.mult, op1=ALU.mult),
                              reads=[f"RX/x{sub}a", f"st{rc_}", "GG/GA"], writes=[f"RX/x{sub}a"])
                        P.dve(lambda e: e.scalar_tensor_tensor(out=X1[:, sub, 1024:2048], in0=pss, scalar=STAT[:, rc_:rc_ + 1], in1=GA[:, 1024:2048],
                                                               op0=ALU.mult, op1=ALU.mult),
                              reads=[f"ps{sub * 2}", f"ps{sub * 2 + 1}", f"st{rc_}", "GG/GA"], writes=[f"RX/x{sub}b"])
                        P.dve(lambda e: e.tensor_tensor(out=X1[:, sub, 0:1024], in0=X1[:, sub, 0:1024], in1=ring_f(slot)[:, 0:1024], op=ALU.add),
                              reads=[f"RX/x{sub}a", f"ring{slot}"], writes=[f"RX/x{sub}a"])
                        P.pool(lambda e: e.tensor_tensor(out=X1[:, sub, 1024:2048], in0=X1[:, sub, 1024:2048], in1=ring_f(slot)[:, 1024:2048], op=ALU.add),
                               reads=[f"RX/x{sub}b", f"ring{slot}"], writes=[f"RX/x{sub}b"])

                        def back(sb2):
                            c = tt * 4 + sb2
                            norm_transpose(X1[:, sb2, :], [f"RX/x{sb2}a", f"RX/x{sb2}b"], cond, 2, 3,
                                           lambda k, c=c: R2[:, k, c * 128:(c + 1) * 128], lambda k, c=c: mres(k, c), (sb2 * 2, sb2 * 2 + 1), xn_on_act=True,
                                           junk=(R1[:, 0:2048], [f"R1/a{j_}" for j_ in range(4)]))
                        if sub >= 1:
                            back(sub - 1)
                        if sub == 3:
                            back(3)
                    add_task(lf_x, cf_x)
                for u in range(32):
                    def lf_w1(slot, u=u):
                        if True:
                            scr_load(slot, 8 + u)
                        else:
                            P.dma("pool", wchan(), lambda e, s=slot: e.dma_start(out=ring_w(s, 16, 256), in_=w1v[:, :, u * 256:(u + 1) * 256]), writes=[f"ring{slot}"])

                    def cf_w1(slot, u=u, tt=tt):
                        wv = ring_w(slot, 16, 256)
                        for jj in range(2):
                            j = u * 2 + jj
                            b = j % 4
                            for k in range(16):
                                P.pe(lambda e, b=b, k=k, jj=jj: e.matmul(bank(b), lhsT=wv[:, k, jj * 128:(jj + 1) * 128], rhs=R2[:, k, tt * 512:(tt + 1) * 512],
                                                                         start=(k == 0), stop=(k == 15)),
                                     reads=[f"ring{slot}"] + [mres(k, tt * 4 + s_) for s_ in range(4)], writes=[f"ps{b}"])
                            rt = j % 2
                            P.act(lambda e, b=b, rt=rt: e.activation(out=rtmp[rt], in_=bank(b), func=AF.Relu), reads=[f"ps{b}"], writes=[f"rt{rt}"])
                            P.dve(lambda e, j=j, rt=rt: e.tensor_tensor(out=aT[:, j, :], in0=rtmp[rt], in1=rtmp[rt], op=ALU.mult), reads=[f"rt{rt}"], writes=[f"R1/a{j}"])
                    add_task(lf_w1, cf_w1)
                ss3 = {}
                for hf in range(2):
                    for u in range(16):
                        def lf_w2(slot, hf=hf, u=u):
                            if True:
                                scr_load(slot, 40 + hf * 16 + u)
                            else:
                                P.dma("pool", wchan(), lambda e, s=slot: e.dma_start(out=ring_w(s, 4, 1024), in_=w2v[:, u * 4:(u + 1) * 4, hf * 1024:(hf + 1) * 1024]),
                                      writes=[f"ring{slot}"])

                        def cf_w2(slot, hf=hf, u=u, tt=tt, r0=r0):
                            wv = ring_w(slot, 4, 1024)
                            for sub in range(4):
                                for jj in range(4):
                                    j = u * 4 + jj
                                    for cbb in range(2):
                                        b = sub * 2 + cbb
                                        P.pe(lambda e, b=b, j=j, jj=jj, cbb=cbb, sub=sub: e.matmul(bank(b), lhsT=aT[:, j, sub * 128:(sub + 1) * 128], rhs=wv[:, jj, cbb * 512:(cbb + 1) * 512],
                                                                                            start=(j == 0), stop=(j == 63)),
                                             reads=[f"ring{slot}", f"R1/a{j}"], writes=[f"ps{b}"])
                            if u == 15:
                                tmpp = [TS[:, 0:512], TS[:, 512:1024]]
                                YR1 = R1.bitcast(F32)[:, 0:4096].rearrange("p (s d) -> p s d", s=4)
                                if hf == 1:
                                    for sub in range(4):
                                        pss = PS[:, sub * 1024:(sub + 1) * 1024]
                                        P.act(lambda e: e.copy(out=YR1[:, sub, :], in_=pss), reads=[f"ps{sub * 2}", f"ps{sub * 2 + 1}"],
                                              writes=[f"R1/a{j_}" for j_ in range(sub * 4, sub * 4 + 4)])
                                for sub in range(4):
                                    pss = PS[:, sub * 1024:(sub + 1) * 1024] if hf == 0 else YR1[:, sub, :]
                                    sres = [f"ps{sub * 2}", f"ps{sub * 2 + 1}"] if hf == 0 else [f"R1/a{j_}" for j_ in range(sub * 4, sub * 4 + 4)]
                                    sc_ = stat_col()
                                    ss3[(sub, hf)] = sc_
                                    P.act(lambda e: e.activation(out=TS.bitcast(BF16)[:, 2048:3072], in_=pss, func=AF.Square, accum_out=STAT[:, sc_:sc_ + 1]),
                                          reads=sres, writes=["rt0", "rt1", f"st{sc_}"])
                                    if hf == 0:
                                        yr = R2f[:, sub * 4:(sub + 1) * 4, tt * 256:(tt + 1) * 256]
                                        yres = [mres(k, tt * 4 + s_) for k in range(sub * 4, sub * 4 + 4) for s_ in range(4)]
                                        P.dve(lambda e: e.tensor_copy(out=yr, in_=pss.rearrange("p (a b) -> p a b", a=4)),
                                              reads=[f"ps{sub * 2}", f"ps{sub * 2 + 1}"], writes=yres)
                                if hf == 1:
                                    rcs = {}
                                    for sub in range(4):
                                        a0, a1 = ss3[(sub, 0)], ss3[(sub, 1)]
                                        P.dve(lambda e: e.tensor_tensor(out=STAT[:, a0:a0 + 1], in0=STAT[:, a0:a0 + 1], in1=STAT[:, a1:a1 + 1], op=ALU.add),
                                              reads=[f"st{a0}", f"st{a1}"], writes=[f"st{a0}"])
                                        rcs[sub] = stat_col()
                                        rstd_from(STAT[:, a0:a0 + 1], [f"st{a0}"], D, rcs[sub])
                                    for sub in range(4):
                                        rc_ = rcs[sub]
                                        y1 = YR1[:, sub, :]
                                        y1res = [f"R1/a{j_}" for j_ in range(sub * 4, sub * 4 + 4)]
                                        yr = R2f[:, sub * 4:(sub + 1) * 4, tt * 256:(tt + 1) * 256]
                                        yres = [mres(k, tt * 4 + s_) for k in range(sub * 4, sub * 4 + 4) for s_ in range(4)]
                                        P.dve(lambda e: e.scalar_tensor_tensor(out=y1, in0=y1, scalar=STAT[:, rc_:rc_ + 1], in1=GF[:, 1024:2048], op0=ALU.mult, op1=ALU.mult),
                                              reads=y1res + [f"st{rc_}", "GG/GF"], writes=y1res)
                                        P.dve(lambda e: e.tensor_tensor(out=X1[:, sub, 1024:2048], in0=X1[:, sub, 1024:2048], in1=y1, op=ALU.add),
                                              reads=y1res + [f"RX/x{sub}b"], writes=[f"RX/x{sub}b"])
                                        P.dve(lambda e: e.scalar_tensor_tensor(out=yr, in0=yr, scalar=STAT[:, rc_:rc_ + 1], in1=GF[:, 0:1024].rearrange("p (a b) -> p a b", a=4),
                                                                               op0=ALU.mult, op1=ALU.mult),
                                              reads=yres + [f"st{rc_}", "GG/GF"], writes=yres)
                                        P.pool(lambda e: e.tensor_tensor(out=X1[:, sub, 0:1024].rearrange("p (a b) -> p a b", a=4), in0=X1[:, sub, 0:1024].rearrange("p (a b) -> p a b", a=4),
                                                                         in1=yr, op=ALU.add),
                                               reads=yres + [f"RX/x{sub}a"], writes=[f"RX/x{sub}a"])
                                        P.dma("pool", f"oy{sub % 2}", lambda e: e.dma_start(out=ys[r0 + sub * 128:r0 + (sub + 1) * 128, :], in_=X1[:, sub, :]),
                                              reads=[f"RX/x{sub}a", f"RX/x{sub}b"])
                        add_task(lf_w2, cf_w2)
                run_tasks()
            P.fence("RX", lambda e: e.memset(SCR[:, 4:5], 0.0))
            P.fence("R2", lambda e: e.memset(SCR[:, 5:6], 0.0))
            if G["name"] == "P":
                checkpoint("BP")
            P.fence("GG", lambda e: e.memset(SCR[:, 7:8], 0.0))


    try:
        _body()
    except _Stop:
        pass
    P.emit()
    return nc, es, P


def _invcnt(L):
    out = np.zeros((4, L), np.float32)
    t = np.arange(L)
    for g, w in enumerate((2, 4, 8, 16)):
        lo = np.maximum(t - w // 2, 0)
        hi = np.minimum(t + w // 2, L)
        out[g] = 1.0 / (hi - lo).astype(np.float32)
    return out.reshape(1, 4 * L)


_CACHE = {}


def kernel(x_prompt, x_sample, c, state_C, state_n, state_m, c_ctx, w_in, gate_bias, mlstm_norm_w,
           pool_w, pool_scale, w_out, ada_w, ada_b, norm_w, w1, w2):
    f = np.float32
    x_prompt = np.asarray(x_prompt, f); x_sample = np.asarray(x_sample, f)
    c = np.asarray(c, f); c_ctx = np.asarray(c_ctx, f)
    state_C = np.asarray(state_C, f); state_n = np.asarray(state_n, f); state_m = np.asarray(state_m, f)
    w_in2 = np.ascontiguousarray(np.asarray(w_in, f)[0])
    w_out2 = np.ascontiguousarray(np.asarray(w_out, f)[0])
    ada_w2 = np.ascontiguousarray(np.asarray(ada_w, f)[0])
    w1_2 = np.ascontiguousarray(np.asarray(w1, f)[0])
    w2_2 = np.ascontiguousarray(np.asarray(w2, f)[0])
    pool_w2 = np.ascontiguousarray(np.asarray(pool_w, f)[0])
    gb = np.ascontiguousarray(np.asarray(gate_bias, f)[0].reshape(1, 32))
    adab_pl = np.asarray(ada_b, f)[0].reshape(96, 128).T
    nw_pl = np.asarray(norm_w, f)[0].reshape(4, 16, 128).transpose(2, 0, 1).reshape(128, 64)
    mnw_pl = np.asarray(mlstm_norm_w, f)[0].reshape(8, 128).T
    psc_pl = np.asarray(pool_scale, f)[0].reshape(8, 128).T
    eye = np.eye(128, dtype=f)
    triU = np.triu(np.ones((128, 128), f))
    triL = np.tril(np.ones((128, 128), f))
    cb16 = np.concatenate([eye, triU, triL], axis=1).astype(ml_dtypes.bfloat16)
    cf32 = np.concatenate([eye, triU, triL, np.ones((128, 128), f)], axis=1)
    invc_s = _invcnt(64)
    invc_p = _invcnt(256)
    in_maps = []
    for b in range(8):
        xs = np.concatenate([x_sample[b], x_prompt[2 * b], x_prompt[2 * b + 1]], axis=0)
        cond = np.stack([c[b], c_ctx], axis=0)
        cond_pl = cond.reshape(2, 16, 128).transpose(2, 1, 0).reshape(128, 32)
        smallpl = np.ascontiguousarray(np.concatenate([cond_pl, adab_pl, nw_pl, mnw_pl, psc_pl], axis=1), dtype=f)
        in_maps.append({
            "xs": np.ascontiguousarray(xs), "sC": np.ascontiguousarray(state_C[b, 0]), "sn": np.ascontiguousarray(state_n[b, 0]),
            "sm": np.ascontiguousarray(state_m[b, 0].reshape(1, 16)), "w_in": w_in2, "gb": gb, "pool_w": pool_w2, "w_out": w_out2,
            "ada_w": ada_w2, "w1": w1_2, "w2": w2_2, "smallpl": smallpl, "cb16": cb16, "cf32": cf32, "invc_s": invc_s, "invc_p": invc_p,
        })
    if "nc" not in _CACHE:
        nc, es, P = build_program()
        _CACHE["nc"] = nc
        _CACHE["es"] = es
    nc = _CACHE["nc"]
    res = run_bass_kernel_spmd(nc, in_maps, core_ids=list(range(8)))
    R = res.results
    y_sample = np.stack([R[b]["ys"][0:2048] for b in range(8)], axis=0)
    y_prompt = np.stack([R[b]["ys"][2048 + 256 * j:2048 + 256 * (j + 1)] for b in range(8) for j in range(2)], axis=0)
    new_C = np.stack([R[b]["nC"][j] for b in range(8) for j in range(2)], axis=0)[:, None]
    new_n = np.stack([R[b]["nn"][j] for b in range(8) for j in range(2)], axis=0)[:, None]
    new_m = np.stack([R[b]["nm"][j].reshape(2, 8) for b in range(8) for j in range(2)], axis=0)[:, None]
    return (y_prompt.astype(f), y_sample.astype(f), new_C.astype(f), new_n.astype(f), new_m.astype(f))
```

```python
import os
from contextlib import ExitStack

import numpy as np
import ml_dtypes

import concourse.bass as bass
import concourse.mybir as mybir
from concourse.bass_utils import run_bass_kernel_spmd

F32 = mybir.dt.float32
BF16 = mybir.dt.bfloat16
AF = mybir.ActivationFunctionType
ALU = mybir.AluOpType
AX = mybir.AxisListType

D = 2048
NH = 8
DH = 128
M = 1024
DFF = 8192
EPS = 1e-6
NTOK = 2560
ENGS = ("pe", "act", "dve", "pool", "sp")


class _Rec:
    def __init__(self):
        self.call = None

    def __getattr__(self, name):
        def f(*a, **k):
            self.call = (name, a, k)
            return self
        return f


class Prog:
    def __init__(self, nc, es):
        self.nc = nc
        self.es = es
        self.ops = []
        self.last_writer = {}
        self.readers = {}
        self.chan_last = {}
        self.region_fence = {}

    @staticmethod
    def _psum_banks(ap):
        esz = 4 if ap.dtype == F32 else 2
        start = ap.offset * esz
        ext = sum((cnt - 1) * abs(st) for st, cnt in ap.ap[1:]) * esz + esz
        return range(start // 2048, (start + ext - 1) // 2048 + 1)

    def _add(self, eng, fn, reads, writes, kind="c", chan=None):
        idx = len(self.ops)
        deps = set()
        rec = _Rec()
        fn(rec)
        assert rec.call is not None
        reads = list(reads)
        writes = list(writes)
        name_, a_, k_ = rec.call
        operands = [("out" if i == 0 else "in", v) for i, v in enumerate(a_)] + list(k_.items())
        for key, v in operands:
            if hasattr(v, "space") and str(v.space) == "PSUM":
                for b in self._psum_banks(v):
                    if key == "out":
                        if f"ps{b}" not in writes:
                            print("WARN: undeclared psum write", eng, name_, b)
                            writes.append(f"ps{b}")
                    else:
                        if f"ps{b}" not in reads:
                            print("WARN: undeclared psum read", eng, name_, b)
                            reads.append(f"ps{b}")
                    if eng in ("act", "dve"):
                        if f"psx{b}" not in writes:
                            writes.append(f"psx{b}")

        def lastw(r):
            w = self.last_writer.get(r)
            if w is None and "/" in r:
                w = self.region_fence.get(r.split("/")[0])
            return w

        for r in reads:
            w = lastw(r)
            if w is not None:
                deps.add(w)
        for w_ in writes:
            w = lastw(w_)
            if w is not None:
                deps.add(w)
            for rd in self.readers.get(w_, ()):
                deps.add(rd)
        if kind == "d":
            prev = self.chan_last.get(chan)
            if prev is not None:
                deps.add(prev)
            self.chan_last[chan] = idx
        deps.discard(idx)
        for r in reads:
            self.readers.setdefault(r, []).append(idx)
        for w_ in writes:
            self.last_writer[w_] = idx
            self.readers[w_] = []
        self.ops.append(dict(eng=eng, call=rec.call, deps=deps, kind=kind, chan=chan))
        return idx

    def pe(self, fn, reads=(), writes=()):
        return self._add("pe", fn, reads, writes)

    def act(self, fn, reads=(), writes=()):
        return self._add("act", fn, reads, writes)

    def dve(self, fn, reads=(), writes=()):
        return self._add("dve", fn, reads, writes)

    def pool(self, fn, reads=(), writes=()):
        return self._add("pool", fn, reads, writes)

    def dma(self, queue, chan, fn, reads=(), writes=()):
        return self._add(queue, fn, reads, writes, kind="d", chan=chan)

    def fence(self, region, fn):
        pref = region + "/"
        names = [k for k in set(self.last_writer) | set(self.readers) if k.startswith(pref)]
        idx = self._add("dve", fn, reads=(), writes=names + ["SCRcell"])
        for k in names:
            self.last_writer.pop(k, None)
            self.readers.pop(k, None)
        self.region_fence[region] = idx
        return idx

    def emit(self):
        nc, es = self.nc, self.es
        ops = self.ops
        n = len(ops)
        needs_signal = [False] * n
        for i, op in enumerate(ops):
            keep = set()
            for d in op["deps"]:
                dop = ops[d]
                if dop["kind"] == "c" and op["kind"] == "c" and dop["eng"] == "pe" and op["eng"] == "pe":
                    continue
                keep.add(d)
            op["deps"] = keep
            for d in keep:
                needs_signal[d] = True
        eng_sem = {e: es.enter_context(nc.semaphore(f"sem_{e}")) for e in ENGS}
        chan_sem = {}
        for op in ops:
            if op["kind"] == "d" and op["chan"] not in chan_sem:
                chan_sem[op["chan"]] = es.enter_context(nc.semaphore(f"semd_{len(chan_sem)}"))
        eng_cnt = {e: 0 for e in ENGS}
        chan_cnt = {c: 0 for c in chan_sem}
        sig = [None] * n
        for i, op in enumerate(ops):
            if op["kind"] == "d":
                chan_cnt[op["chan"]] += 16
                sig[i] = (chan_sem[op["chan"]], chan_cnt[op["chan"]])
            elif needs_signal[i]:
                eng_cnt[op["eng"]] += 1
                sig[i] = (eng_sem[op["eng"]], eng_cnt[op["eng"]])
        per_eng = {e: [] for e in ENGS}
        for i, op in enumerate(ops):
            per_eng[op["eng"]].append(i)
        final_waits = [(chan_sem[c], chan_cnt[c]) for c in chan_sem]
        block = es.enter_context(nc.Block())

        def make(ename):
            def body(eng):
                known = {}
                for i in per_eng[ename]:
                    op = ops[i]
                    waits = {}
                    for d in op["deps"]:
                        s, v = sig[d]
                        key = id(s)
                        if known.get(key, 0) >= v:
                            continue
                        if key not in waits or waits[key][1] < v:
                            waits[key] = (s, v)
                    for key, (s, v) in waits.items():
                        eng.wait_ge(s, v)
                        known[key] = v
                    name_, a_, k_ = op["call"]
                    ins = getattr(eng, name_)(*a_, **k_)
                    if sig[i] is not None:
                        s, v = sig[i]
                        ins.then_inc(s, 16 if op["kind"] == "d" else 1)
                if ename == "sp":
                    for s, v in final_waits:
                        if v > 0:
                            eng.wait_ge(s, v)
            return body

        block.tensor(make("pe"))
        block.scalar(make("act"))
        block.vector(make("dve"))
        block.gpsimd(make("pool"))
        block.sync(make("sp"))
        self.stats = {e: len(per_eng[e]) for e in ENGS}


def build_program(debug=False):
    nc = bass.Bass("TRN2", target_bir_lowering=False)
    es = ExitStack()

    def din(name, shape, dt=F32):
        return nc.dram_tensor(name, list(shape), dt, kind="ExternalInput").ap()

    def dout(name, shape, dt=F32):
        return nc.dram_tensor(name, list(shape), dt, kind="ExternalOutput").ap()

    xs = din("xs", [NTOK, D])
    sC = din("sC", [2, NH, DH, DH])
    sn = din("sn", [2, NH, DH])
    sm = din("sm", [1, 16])
    w_in = din("w_in", [D, 5152])
    gb = din("gb", [1, 32])
    pool_w = din("pool_w", [4, 256, 256])
    w_out = din("w_out", [D, D])
    ada_w = din("ada_w", [D, 6 * D])
    w1 = din("w1", [D, DFF])
    w2 = din("w2", [DFF, D])
    smallpl = din("smallpl", [128, 208])
    cb16 = din("cb16", [128, 384], BF16)
    cf32 = din("cf32", [128, 512])
    invc_s = din("invc_s", [1, 256])
    invc_p = din("invc_p", [1, 1024])
    ys = dout("ys", [NTOK, D])
    nC = dout("nC", [2, 2, NH, DH, DH])
    nn = dout("nn", [2, 2, NH, DH])
    nm = dout("nm", [2, 16])

    scr = nc.dram_tensor("scr_w", [97, 128, 4096], BF16, kind="Internal").ap()

    def sb(name, shape, dt=F32):
        return es.enter_context(nc.sbuf_tensor(name, list(shape), dt))

    R1 = sb("R1", [128, 32768], BF16)
    R2 = sb("R2", [128, 16, 2048], BF16)
    RX = sb("RX", [128, 8192], F32)
    RING = [sb(f"ring{i}", [128, 4096], BF16) for i in range(2)]
    GG = sb("GG", [128, 4096], F32)
    TS = sb("TS", [128, 1536], F32)
    CB = sb("CB", [128, 384], BF16)
    CF = sb("CF", [128, 512], F32)
    SPL = sb("SPL", [128, 208], F32)
    GBI = sb("GBI", [128, 32], F32)
    SMB = sb("SMB", [128, 16], F32)
    SCB = sb("SCB", [128, 32], BF16)
    MOD = sb("MOD", [128, 192], F32)
    DER = sb("DER", [128, 192], F32)
    WG = sb("WG", [128, 512], BF16)
    PW = sb("PW", [128, 512], BF16)
    INVC = sb("INVC", [128, 256], F32)
    STAT = sb("STAT", [128, 64], F32)
    DIAG = sb("DIAG", [128, 256], F32)
    MFIN = sb("MFIN", [128, 32], F32)
    SCR = sb("SCR", [128, 8], F32)
    FST = sb("FST", [128, 24], F32)
    PS = es.enter_context(nc.psum_tensor("PS", [128, 4096], F32))
    PSb = PS.bitcast(BF16)

    identb = CB[:, 0:128]
    maskU = CB[:, 128:256]
    maskL = CB[:, 256:384]
    identf = CF[:, 0:128]
    triU = CF[:, 128:256]
    triL = CF[:, 256:384]
    onesf = CF[:, 384:512]

    P = Prog(nc, es)
    dbg = {}
    STOP = os.environ.get("MK_STOP", "")
    DUMPS = os.environ.get("MK_DUMP", "").split(",")

    class _Stop(Exception):
        pass

    def checkpoint(name):
        if STOP == name:
            raise _Stop()

    def dump(name, ap, shape, reads, dt=F32):
        if name not in DUMPS:
            return
        o = nc.dram_tensor("dbg_" + name, list(shape), dt, kind="ExternalOutput").ap()
        P.dma("sp", "dbg_" + name, lambda e: e.dma_start(out=o, in_=ap), reads=reads)

    def bank(b):
        return PS[:, b * 512:(b + 1) * 512]

    def bankb(b):
        return PSb[:, b * 1024:(b + 1) * 1024]

    rr = {"ev": 0, "stat": 0, "ps": 0, "chan": 0}

    def evac_eng():
        rr["ev"] ^= 1
        if os.environ.get("MK_EV"):
            return os.environ["MK_EV"]
        return "act" if rr["ev"] else "dve"

    def stat_col():
        c = rr["stat"]
        rr["stat"] = (c + 1) % 64
        return c

    tasks = []

    def add_task(load_fn, compute_fn):
        tasks.append((load_fn, compute_fn))

    def ring_w(slot, a, b):
        return RING[slot][:, 0:a * b].rearrange("p (a b) -> p a b", a=a)

    def ring_f(slot):
        return RING[slot].bitcast(F32)

    def run_tasks():
        nslot = [0]
        pending = []
        loaded = []

        def do_load(t):
            lf, cf = t
            if lf is None:
                loaded.append((None, cf))
            else:
                s = nslot[0] % 2
                nslot[0] += 1
                lf(s)
                loaded.append((s, cf))

        i = 0
        nt = len(tasks)
        outstanding = 0
        qi = 0
        while qi < nt or loaded:
            while qi < nt and (tasks[qi][0] is None or outstanding < 2):
                if tasks[qi][0] is not None:
                    outstanding += 1
                do_load(tasks[qi])
                qi += 1
                if len(loaded) >= 3:
                    break
            s, cf = loaded.pop(0)
            cf(s)
            if rr.get("conv_rate"):
                emit_conv(rr["conv_rate"])
            if s is not None:
                outstanding -= 1
        tasks.clear()

    def wchan():
        rr["chan"] = (rr["chan"] + 1) % 4
        return f"w{rr['chan']}"

    rr["hchan"] = 0
    rr["cchan"] = 0

    def hchan():
        rr["hchan"] = (rr["hchan"] + 1) % 4
        return f"h{rr['hchan']}"

    def cchan():
        rr["cchan"] = (rr["cchan"] + 1) % 3
        return f"cv{rr['cchan']}"

    w_outv_ = w_out.rearrange("(k p) n -> p k n", p=128)
    w1v_ = w1.rearrange("(k p) n -> p k n", p=128)
    w2v_ = w2.rearrange("(j p) n -> p j n", p=128)
    conv_jobs = []
    for hf in range(2):
        for u4 in range(4):
            i = hf * 4 + u4
            conv_jobs.append((i, scr[i].rearrange("p (a b) -> p a b", a=4), w_outv_[:, u4 * 4:(u4 + 1) * 4, hf * 1024:(hf + 1) * 1024]))
    for u in range(32):
        conv_jobs.append((8 + u, scr[8 + u].rearrange("p (a b) -> p a b", a=16), w1v_[:, :, u * 256:(u + 1) * 256]))
    for hf in range(2):
        for u in range(16):
            i = 40 + hf * 16 + u
            conv_jobs.append((i, scr[i].rearrange("p (a b) -> p a b", a=4), w2v_[:, u * 4:(u + 1) * 4, hf * 1024:(hf + 1) * 1024]))

    w_inv_ = w_in.rearrange("(k p) n -> p k n", p=128)
    win_jobs = []
    for h_ in range(NH):
        for kind in range(2):
            c0_ = (0 if kind == 0 else 2 * M) + h_ * 128
            c1_ = (M if kind == 0 else 3 * M) + h_ * 128
            i = 72 + 8 * kind + h_
            dv = scr[i].rearrange("p (a b) -> p a b", a=16)
            win_jobs.append((i, dv[:, :, 0:128], w_inv_[:, :, c0_:c0_ + 128]))
            win_jobs.append((i, dv[:, :, 128:256], w_inv_[:, :, c1_:c1_ + 128]))
    for g_ in range(4):
        i = 88 + g_
        win_jobs.append((i, scr[i].rearrange("p (a b) -> p a b", a=16), w_inv_[:, :, 4128 + g_ * 256:4128 + (g_ + 1) * 256]))
    for g_ in range(4):
        i = 93 + g_
        win_jobs.append((i, scr[i][:, 0:512].rearrange("p (c n) -> p c n", c=2), pool_w[g_].rearrange("(c p) n -> p c n", p=128)))
    win_jobs.sort(key=lambda t: (t[0] - 72) % 8 if t[0] < 88 else 100)

    def emit_conv(n):
        for _ in range(n):
            if not conv_jobs:
                return
            i, o_ap, i_ap = conv_jobs.pop(0)
            rr["cvn"] = (rr.get("cvn", 0) + 1) % 20
            P.dma("pool", f"cvu{rr['cvn']}", lambda e: e.dma_start(out=o_ap, in_=i_ap), writes=[f"scr{i}"])

    def scr_load(slot, i):
        P.dma("sp", hchan(), lambda e: e.dma_start(out=RING[slot][:], in_=scr[i]), reads=[f"scr{i}"], writes=[f"ring{slot}"])

    def scr_store(slot, i):
        P.dma("sp", hchan(), lambda e: e.dma_start(out=scr[i], in_=RING[slot][:]), reads=[f"ring{slot}"], writes=[f"scr{i}"])

    def rstd_from(ss_ap, ss_res, n, out_col):
        tmpc = stat_col()
        P.act(lambda e, a=ss_ap, t=tmpc: e.activation(out=STAT[:, t:t + 1], in_=a, func=AF.Sqrt, scale=1.0 / n, bias=EPS),
              reads=ss_res, writes=[f"st{tmpc}"])
        P.dve(lambda e, t=tmpc, o=out_col: e.reciprocal(out=STAT[:, o:o + 1], in_=STAT[:, t:t + 1]),
              reads=[f"st{tmpc}"], writes=[f"st{out_col}"])

    def _body():
        P.dma("sp", "c0", lambda e: e.dma_start(out=CB[:], in_=cb16[:, :]), writes=["CB"])
        P.dma("sp", "c1", lambda e: e.dma_start(out=CF[:], in_=cf32[:, :]), writes=["CF"])
        P.dma("sp", "c2", lambda e: e.dma_start(out=SPL[:], in_=smallpl[:, :]), writes=["SPL"])
        P.dma("sp", "c3", lambda e: e.dma_start(out=GBI[:], in_=gb.partition_broadcast(128)), writes=["GBI"])
        P.dma("sp", "c4", lambda e: e.dma_start(out=SMB[:], in_=sm.partition_broadcast(128)), writes=["SMB"])
        P.dma("pool", "c5", lambda e: e.dma_start(out=WG[:].rearrange("p (k n) -> p k n", k=16),
                                                  in_=w_in.rearrange("(k p) n -> p k n", p=128)[:, :, 4096:4128]),
              writes=["WG"])
        condv = SPL[:, 0:32]
        adab = SPL[:, 32:128]
        nwv = SPL[:, 128:192]
        mnw = SPL[:, 192:200]
        psc = SPL[:, 200:208]
        P.act(lambda e: e.activation(out=SCB[:], in_=condv, func=AF.Silu), reads=["SPL"], writes=["SCB"])
        SCBv = SCB[:].rearrange("p (k c) -> p k c", c=2)
        ada_v = ada_w.rearrange("(k p) n -> p k n", p=128)
        conv_jobs[0:0] = win_jobs
        emit_conv(len(win_jobs))
        MODB = 7
        for u in range(48):
            def lf(slot, u=u):
                P.dma("pool", wchan(), lambda e, s=slot, u=u: e.dma_start(out=ring_w(s, 16, 256), in_=ada_v[:, :, u * 256:(u + 1) * 256]),
                      writes=[f"ring{slot}"])

            def cf(slot, u=u):
                wv = ring_w(slot, 16, 256)
                for c2 in range(2):
                    ch = u * 2 + c2
                    for k in range(16):
                        P.pe(lambda e, wv=wv, c2=c2, k=k, ch=ch: e.matmul(bank(MODB)[:, ch * 2:ch * 2 + 2], lhsT=wv[:, k, c2 * 128:(c2 + 1) * 128],
                                                                     rhs=SCBv[:, k, :], start=(k == 0), stop=(k == 15)),
                             reads=[f"ring{slot}", "SCB"], writes=[f"ps{MODB}"])
            add_task(lf, cf)
        run_tasks()
        emit_conv(1000)
        P.dve(lambda e: e.tensor_tensor(out=MOD[:].rearrange("p (a c) -> p a c", c=2), in0=bank(MODB)[:, 0:192].rearrange("p (a c) -> p a c", c=2),
                                        in1=adab.unsqueeze(2).to_broadcast([128, 96, 2]), op=ALU.add),
              reads=[f"ps{MODB}", "SPL"], writes=["MOD"])
        MODv = MOD[:].rearrange("p (a c) -> p a c", c=2)
        DERv = DER[:].rearrange("p (c i k) -> p c i k", c=2, i=6)

        def modv(i, c):
            return MODv[:, i * 16:(i + 1) * 16, c]

        def nw(i):
            return nwv[:, i * 16:(i + 1) * 16]

        for c in range(2):
            P.dve(lambda e, c=c: e.scalar_tensor_tensor(out=DERv[:, c, 0, :], in0=modv(1, c), scalar=1.0, in1=nw(0), op0=ALU.add, op1=ALU.mult),
                  reads=["MOD", "SPL"], writes=["DER"])
            P.dve(lambda e, c=c: e.tensor_copy(out=DERv[:, c, 1, :], in_=modv(0, c)), reads=["MOD"], writes=["DER"])
            P.dve(lambda e, c=c: e.scalar_tensor_tensor(out=DERv[:, c, 2, :], in0=modv(4, c), scalar=1.0, in1=nw(2), op0=ALU.add, op1=ALU.mult),
                  reads=["MOD", "SPL"], writes=["DER"])
            P.dve(lambda e, c=c: e.tensor_copy(out=DERv[:, c, 3, :], in_=modv(3, c)), reads=["MOD"], writes=["DER"])
            P.dve(lambda e, c=c: e.tensor_tensor(out=DERv[:, c, 4, :], in0=modv(2, c), in1=nw(1), op=ALU.mult), reads=["MOD", "SPL"], writes=["DER"])
            P.dve(lambda e, c=c: e.tensor_tensor(out=DERv[:, c, 5, :], in0=modv(5, c), in1=nw(3), op=ALU.mult), reads=["MOD", "SPL"], writes=["DER"])

        dump("mod", MOD[:], [128, 192], ["MOD"])
        dump("der", DER[:], [128, 192], ["DER"])
        checkpoint("setup")
        xnb = TS.bitcast(BF16)[:, 0:2048]
        xnf = TS[:, 0:1024]
        rtmp = [TS.bitcast(BF16)[:, 2048 + i * 512:2048 + (i + 1) * 512] for i in range(2)]

        def norm_stats(src_ap, src_res, junk=None):
            ssc = stat_col()
            jap, jres = junk if junk is not None else (xnb, ["xnp0", "xnp1"])
            P.act(lambda e, c=ssc: e.activation(out=jap, in_=src_ap, func=AF.Square, accum_out=STAT[:, c:c + 1]),
                  reads=src_res, writes=list(jres) + [f"st{ssc}"])
            rc = stat_col()
            rstd_from(STAT[:, ssc:ssc + 1], [f"st{ssc}"], D, rc)
            return rc

        def norm_transpose(src_ap, src_res, cond, gi, si, dst_fn, dst_res_fn, tbanks, xn_on_act=False, junk=None, rc=None):
            if rc is None:
                rc = norm_stats(src_ap, src_res, junk)
            for hb in range(2):
                xh = xnb[:, hb * 1024:(hb + 1) * 1024]
                sh_ = src_ap[:, hb * 1024:(hb + 1) * 1024]
                xr = f"xnp{hb}"
                if xn_on_act:
                    P.act(lambda e, c=rc: e.activation(out=xh, in_=sh_, func=AF.Identity, scale=STAT[:, c:c + 1]),
                          reads=src_res + [f"st{rc}"], writes=[xr])
                else:
                    P.dve(lambda e, c=rc: e.tensor_scalar(out=xh, in0=sh_, scalar1=STAT[:, c:c + 1], scalar2=None, op0=ALU.mult),
                          reads=src_res + [f"st{rc}"], writes=[xr])
            for hb in range(2):
                b = tbanks[hb]
                xr = f"xnp{hb}"
                for kk in range(8):
                    k = hb * 8 + kk
                    P.pe(lambda e, b=b, kk=kk, k=k: e.transpose(out=bankb(b)[:, kk * 128:(kk + 1) * 128], in_=xnb[:, k * 128:(k + 1) * 128], identity=identb),
                         reads=[xr, "CB"], writes=[f"ps{b}"])
            for kk in range(8):
                for hb in range(2):
                    b = tbanks[hb]
                    k = hb * 8 + kk
                    en = "act" if hb == 1 else "dve"
                    src = bankb(b)[:, kk * 128:(kk + 1) * 128]
                    g_ap = DERv[:, cond, gi, k:k + 1]
                    s_ap = DERv[:, cond, si, k:k + 1]
                    if en == "act":
                        P.act(lambda e, src=src, k=k, g=g_ap, s=s_ap: e.activation(out=dst_fn(k), in_=src, func=AF.Identity, scale=g, bias=s),
                              reads=[f"ps{b}", "DER"], writes=[dst_res_fn(k)])
                    else:
                        P.dve(lambda e, src=src, k=k, g=g_ap, s=s_ap: e.tensor_scalar(out=dst_fn(k), in0=src, scalar1=g, scalar2=s, op0=ALU.mult, op1=ALU.add),
                              reads=[f"ps{b}", "DER"], writes=[dst_res_fn(k)])

        groups = [
            dict(name="P", base=2048, NT=512, nch=4, cond=1, seqs=[(0, 2, 0), (2, 2, 1)], L=256, invc=invc_p),
            dict(name="S", base=0, NT=2048, nch=16, cond=0, seqs=[(0, 16, None)], L=64, invc=invc_s),
        ]
        if os.environ.get("MK_ONLY_P"):
            groups = groups[:1]
        RXb = RX.bitcast(BF16)
        GGb = None

        for gi_, G in enumerate(groups):
            base, NT, nch, cond, L = G["base"], G["NT"], G["nch"], G["cond"], G["L"]
            ntt = NT // 512
            hT = R1[:, 0:16 * NT].rearrange("p (k t) -> p k t", k=16)

            def hres(k, c):
                return f"R1/h{k}c{c}"

            def mres(k, c):
                return f"R2/k{k}c{c}"

            P.fence("R1", lambda e: e.memset(SCR[:, 0:1], 0.0))
            rr["conv_rate"] = 0
            first_pass = False


            R2f32 = R2.bitcast(F32)
            NXS = 4
            xslots = [R2f32[:, 2 * i:2 * i + 2, :] for i in range(NXS)]

            def a1_load(c):
                sl = c % NXS
                P.dma("sp", f"xa{sl}", lambda e: e.dma_start(out=xslots[sl], in_=xs[base + c * 128: base + (c + 1) * 128, :].rearrange("p (a b) -> p a b", a=2)),
                      writes=[f"R2/xs{sl}"])

            for c in range(min(NXS - 1, nch)):
                a1_load(c)
            a1junk = [(R2[:, 14 + i, 0:2048], [mres(14 + i, c_) for c_ in range(16)]) for i in range(2)]
            xflat = [xslots[i].rearrange("p a b -> p (a b)") for i in range(NXS)]
            rcs_ = {0: norm_stats(xflat[0], ["R2/xs0"], a1junk[0])}
            for c in range(nch):
                if c + NXS - 1 < nch:
                    a1_load(c + NXS - 1)
                if c + 1 < nch:
                    rcs_[c + 1] = norm_stats(xflat[(c + 1) % NXS], [f"R2/xs{(c + 1) % NXS}"], a1junk[(c + 1) % 2])
                sl = c % NXS
                norm_transpose(xflat[sl], [f"R2/xs{sl}"], cond, 0, 1,
                               lambda k, c=c: hT[:, k, c * 128:(c + 1) * 128], lambda k, c=c: hres(k, c), (0, 1), rc=rcs_[c])

            if G["name"] == "P":
                dump("hT", R1[:, 0:8192], [128, 8192], [hres(k, c) for k in range(16) for c in range(4)], BF16)
                checkpoint("A1")
            P.fence("R2", lambda e: e.memset(SCR[:, 1:2], 0.0))
            GB = 2
            Gt = GG[:, 0:nch * 32].rearrange("p (c x) -> p c x", x=32)
            G4 = GG[:, 0:nch * 32].rearrange("p (c d t h) -> p c d t h", d=2, t=2, h=8)
            li = G4[:, :, :, 0, :]
            lfp = G4[:, :, :, 1, :]

            def g16(off):
                return GG[:, off:off + nch * 16]

            def g4(off):
                return GG[:, off:off + nch * 16].rearrange("p (c d h) -> p c d h", d=2, h=8)

            def g3(off):
                return GG[:, off:off + nch * 16].rearrange("p (c x) -> p c x", x=16)

            O_LF, O_T1, O_T2, O_B, O_TOT, O_A, O_AM, O_MC, O_MP, O_W, O_GC, O_DL = [512 + 256 * i for i in range(12)]
            O_AMT = 3600
            for c in range(nch):
                for k in range(16):
                    P.pe(lambda e, c=c, k=k: e.matmul(bank(GB)[:, c * 32:(c + 1) * 32], lhsT=hT[:, k, c * 128:(c + 1) * 128],
                                                      rhs=WG[:].rearrange("p (k n) -> p k n", k=16)[:, k, :], start=(k == 0), stop=(k == 15)),
                         reads=[hres(k, c), "WG"], writes=[f"ps{GB}"])
            P.dve(lambda e: e.tensor_tensor(out=Gt, in0=bank(GB)[:, 0:nch * 32].rearrange("p (c x) -> p c x", x=32),
                                            in1=GBI[:].unsqueeze(1).to_broadcast([128, nch, 32]), op=ALU.add),
                  reads=[f"ps{GB}", "GBI"], writes=["GG/G"])
            checkpoint("g1")
            P.act(lambda e: e.activation(out=g4(O_T1), in_=lfp, func=AF.Abs), reads=["GG/G"], writes=["GG/T1"])
            P.act(lambda e: e.activation(out=g16(O_T2), in_=g16(O_T1), func=AF.Exp, scale=-1.0), reads=["GG/T1"], writes=["GG/T2"])
            P.act(lambda e: e.activation(out=g16(O_T1), in_=g16(O_T2), func=AF.Ln, bias=1.0), reads=["GG/T2"], writes=["GG/T1"])
            P.dve(lambda e: e.scalar_tensor_tensor(out=g4(O_LF), in0=lfp, scalar=0.0, in1=g4(O_T1), op0=ALU.min, op1=ALU.subtract),
                  reads=["GG/G", "GG/T1"], writes=["GG/LF"])
            checkpoint("g2")
            LF4 = g4(O_LF)
            for c in range(nch):
                P.pe(lambda e, c=c: e.matmul(bank(GB)[:, c * 32:c * 32 + 8], lhsT=triU, rhs=LF4[:, c, 0, :], start=True, stop=True),
                     reads=["GG/LF", "CF", "GG/G"], writes=[f"ps{GB}"])
                P.pe(lambda e, c=c: e.matmul(bank(GB)[:, c * 32 + 8:c * 32 + 16], lhsT=triL, rhs=LF4[:, c, 1, :], start=True, stop=True),
                     reads=["GG/LF", "CF"], writes=[f"ps{GB}"])
                P.pe(lambda e, c=c: e.matmul(bank(GB)[:, c * 32 + 16:c * 32 + 32], lhsT=onesf, rhs=g3(O_LF)[:, c, :], start=True, stop=True),
                     reads=["GG/LF", "CF"], writes=[f"ps{GB}"])
            pg = bank(GB)[:, 0:nch * 32].rearrange("p (c x) -> p c x", x=32)
            checkpoint("g3a")
            P.dve(lambda e: e.tensor_copy(out=g3(O_B), in_=pg[:, :, 0:16]), reads=[f"ps{GB}"], writes=["GG/B"])
            if os.environ.get("MK_X") == "a":
                P.dve(lambda e: e.tensor_copy(out=g3(O_TOT), in_=pg[:, :, 16:32]), reads=[f"ps{GB}"], writes=["GG/TOT"])
            elif os.environ.get("MK_X") == "c":
                P.act(lambda e: e.activation(out=g3(O_TOT), in_=pg[:, :, 16:32], func=AF.Identity), reads=[f"ps{GB}"], writes=["GG/TOT"])
            elif os.environ.get("MK_X") == "s":
                P.act(lambda e: e.copy(out=g3(O_TOT), in_=pg[:, :, 16:32]), reads=[f"ps{GB}", "GG/B"], writes=["GG/TOT"])
            else:
                P.act(lambda e: e.copy(out=g3(O_TOT), in_=pg[:, :, 16:32]), reads=[f"ps{GB}"], writes=["GG/TOT"])
            checkpoint("g3b")
            P.dve(lambda e: e.tensor_tensor(out=g4(O_A), in0=li, in1=g4(O_B), op=ALU.subtract), reads=["GG/G", "GG/B"], writes=["GG/A"])
            checkpoint("g3")
            ncol = nch * 16
            nhalf = (ncol + 127) // 128
            AMB = 3
            for hf in range(nhalf):
                w_ = min(128, ncol - hf * 128)
                P.pe(lambda e, hf=hf, w_=w_: e.transpose(out=bank(AMB)[0:w_, hf * 128:hf * 128 + 128], in_=g16(O_A)[:, hf * 128:hf * 128 + w_], identity=identf),
                     reads=["GG/A", "CF"], writes=[f"ps{AMB}"])
                P.dve(lambda e, hf=hf, w_=w_: e.reduce_max(out=GG[0:w_, O_AMT + hf:O_AMT + hf + 1], in_=bank(AMB)[0:w_, hf * 128:hf * 128 + 128], axis=AX.X),
                      reads=[f"ps{AMB}"], writes=["GG/AMT"])
                P.dve(lambda e, hf=hf, w_=w_: e.tensor_scalar(out=DIAG[0:w_, hf * 128:hf * 128 + w_], in0=identf[0:w_, 0:w_], scalar1=GG[0:w_, O_AMT + hf:O_AMT + hf + 1],
                                                            scalar2=None, op0=ALU.mult),
                      reads=["GG/AMT", "CF"], writes=["DIAG"])
                P.pe(lambda e, hf=hf, w_=w_: e.matmul(bank(AMB)[:, 256 + hf * 128:256 + hf * 128 + w_], lhsT=onesf[0:w_, :], rhs=DIAG[0:w_, hf * 128:hf * 128 + w_],
                                                      start=True, stop=True),
                     reads=["DIAG", "CF"], writes=[f"ps{AMB}"])
            P.dve(lambda e: e.tensor_copy(out=g16(O_AM), in_=bank(AMB)[:, 256:256 + ncol]), reads=[f"ps{AMB}"], writes=["GG/AM"])
            checkpoint("g4")
            MFv = MFIN[:].rearrange("p (s l) -> p s l", s=2)
            for (c0, n, pidx) in G["seqs"]:
                for d in range(2):
                    order = list(range(c0, c0 + n)) if d == 0 else list(range(c0 + n - 1, c0 - 1, -1))
                    ls = slice(d * 8, d * 8 + 8)
                    first = order[0]
                    if pidx is None:
                        P.dve(lambda e, first=first, ls=ls: e.tensor_copy(out=g3(O_MP)[:, first, ls], in_=SMB[:, ls]), reads=["SMB"], writes=["GG/MP"])
                    else:
                        P.dve(lambda e, first=first, ls=ls: e.memset(g3(O_MP)[:, first, ls], 0.0), writes=["GG/MP"])
                    for i, c in enumerate(order):
                        P.dve(lambda e, c=c, ls=ls: e.tensor_tensor(out=g3(O_MC)[:, c, ls], in0=g3(O_MP)[:, c, ls], in1=g3(O_AM)[:, c, ls], op=ALU.max),
                              reads=["GG/MP", "GG/AM"], writes=["GG/MC"])
                        if i + 1 < n:
                            dst = g3(O_MP)[:, order[i + 1], ls]
                            dres = "GG/MP"
                        else:
                            dst = MFv[:, (pidx or 0), ls]
                            dres = "MFIN"
                        P.dve(lambda e, c=c, ls=ls, dst=dst: e.tensor_tensor(out=dst, in0=g3(O_TOT)[:, c, ls], in1=g3(O_MC)[:, c, ls], op=ALU.add),
                              reads=["GG/TOT", "GG/MC"], writes=[dres])
                if pidx is not None:
                    P.dma("sp", "om", lambda e, pidx=pidx: e.dma_start(out=nm[pidx:pidx + 1, :], in_=MFv[0:1, pidx, :]), reads=["MFIN"])
            checkpoint("g5")
            P.dve(lambda e: e.tensor_tensor(out=g16(O_T1), in0=g16(O_A), in1=g16(O_MC), op=ALU.subtract), reads=["GG/A", "GG/MC"], writes=["GG/T1"])
            P.act(lambda e: e.activation(out=g16(O_W), in_=g16(O_T1), func=AF.Exp), reads=["GG/T1"], writes=["GG/W"])
            P.dve(lambda e: e.tensor_tensor(out=g16(O_T2), in0=g16(O_MP), in1=g16(O_MC), op=ALU.subtract), reads=["GG/MP", "GG/MC"], writes=["GG/T2"])
            P.act(lambda e: e.activation(out=g16(O_GC), in_=g16(O_T2), func=AF.Exp), reads=["GG/T2"], writes=["GG/GC"])
            P.dve(lambda e: e.tensor_tensor(out=g16(O_T1), in0=g16(O_B), in1=g16(O_MC), op=ALU.add), reads=["GG/B", "GG/MC", "GG/W"], writes=["GG/T1"])
            P.act(lambda e: e.activation(out=g16(O_DL), in_=g16(O_T1), func=AF.Exp, scale=-1.0), reads=["GG/T1"], writes=["GG/DL"])
            Wv, GCv, DLv = g3(O_W), g3(O_GC), g3(O_DL)
            if G["name"] == "P":
                dump("gates", GG[:], [128, 4096], ["GG/W", "GG/GC", "GG/DL", "GG/MC", "GG/B", "GG/A", "GG/G", "GG/AM", "GG/MP", "GG/TOT", "GG/LF"])
                checkpoint("gates")

            R2flat = R2[:].rearrange("p k t -> p (k t)")
            SETB = [RXb, R2flat[:, 16384:32768]]
            SETP = ["RX/s0", "R2/s1"]

            def sets(si):
                Bf = SETB[si]
                return dict(
                    qT=Bf[:, 0:NT], kT=Bf[:, 2048:2048 + NT],
                    ktok=Bf[:, 4096:4096 + nch * 128].rearrange("p (c d) -> p c d", d=128),
                    vtok=Bf[:, 6144:6144 + nch * 130].rearrange("p (c d) -> p c d", d=130),
                    sigo=Bf[:, 8256:8256 + nch * 128].rearrange("p (c d) -> p c d", d=128),
                    pre=SETP[si])
            T0 = 10304
            sTm = [RXb[:, T0 + i * 128:T0 + (i + 1) * 128] for i in range(4)]
            vwb = [RXb[:, T0 + 512 + i * 130:T0 + 512 + (i + 1) * 130] for i in range(6)]
            hnb = [RXb[:, T0 + 1292 + i * 128:T0 + 1292 + (i + 1) * 128] for i in range(2)]
            F0 = (T0 + 1548) // 2
            hs8 = [RX[:, F0 + i * 128:F0 + (i + 1) * 128] for i in range(8)]
            Cst = [[RX[:, F0 + 1024 + (d * 2 + j) * 130:F0 + 1024 + (d * 2 + j + 1) * 130] for j in range(2)] for d in range(2)]
            assert F0 + 1024 + 4 * 130 <= 8192
            CSB = R2flat[:, 16384 + 10304:16384 + 10304 + 32 * 130].rearrange("p (d c x) -> p d c x", d=2, x=130)
            w_inv = w_in.rearrange("(k p) n -> p k n", p=128)
            heads = list(range(NH))

            def load_unit(slot, h, kind):
                wv = ring_w(slot, 16, 256)
                si = 72 + 8 * kind + h
                if not first_pass:
                    scr_load(slot, si)
                    return
                c0_ = (0 if kind == 0 else 2 * M) + h * 128
                c1_ = (M if kind == 0 else 3 * M) + h * 128
                P.dma("pool", wchan(), lambda e: e.dma_start(out=wv[:, :, 0:128], in_=w_inv[:, :, c0_:c0_ + 128]), writes=[f"ring{slot}"])
                P.dma("pool", wchan(), lambda e: e.dma_start(out=wv[:, :, 128:256], in_=w_inv[:, :, c1_:c1_ + 128]), writes=[f"ring{slot}"])
                scr_store(slot, si)

            def gen_proj(h):
                S = sets(h % 2)
                pre = S["pre"]
                wv = ring_w(0, 16, 256)
                for tt in range(ntt):
                    for which in range(2):
                        b = which
                        for k in range(16):
                            P.pe(lambda e: e.matmul(bank(b), lhsT=wv[:, k, which * 128:(which + 1) * 128], rhs=hT[:, k, tt * 512:(tt + 1) * 512],
                                                    start=(k == 0), stop=(k == 15)),
                                 reads=["ring0"] + [hres(k, tt * 4 + j) for j in range(4)], writes=[f"ps{b}"])
                        if which == 0:
                            P.act(lambda e: e.activation(out=S["qT"][:, tt * 512:(tt + 1) * 512], in_=bank(b), func=AF.Identity, scale=float(DH ** -0.5)),
                                  reads=[f"ps{b}"], writes=[f"{pre}q{tt}"])
                        else:
                            P.dve(lambda e: e.tensor_copy(out=S["kT"][:, tt * 512:(tt + 1) * 512], in_=bank(b)),
                                  reads=[f"ps{b}"], writes=[f"{pre}k{tt}"])
                        yield
                if h + 1 < NH:
                    load_unit(0, h + 1, 0)
                for c in range(nch):
                    b = c % 2
                    P.pe(lambda e: e.transpose(out=bankb(b)[:, 0:128], in_=S["kT"][:, c * 128:(c + 1) * 128], identity=identb),
                         reads=[f"{pre}k{c // 4}", "CB"], writes=[f"ps{b}"])
                    P.dve(lambda e: e.tensor_copy(out=S["ktok"][:, c, :], in_=bankb(b)[:, 0:128]), reads=[f"ps{b}"], writes=[f"{pre}kt{c}"])
                    if c % 2 == 1:
                        yield
                wv1 = ring_w(1, 16, 256)
                P.dve(lambda e: e.memset(S["vtok"][:, :, 128:130], 1.0), writes=[f"{pre}vones"])
                for c in range(nch):
                    b = c % 2
                    for k in range(16):
                        P.pe(lambda e: e.matmul(bank(b)[:, 0:256], lhsT=hT[:, k, c * 128:(c + 1) * 128], rhs=wv1[:, k, :], start=(k == 0), stop=(k == 15)),
                             reads=["ring1", hres(k, c)], writes=[f"ps{b}"])
                    P.dve(lambda e: e.tensor_copy(out=S["vtok"][:, c, 0:128], in_=bank(b)[:, 0:128]), reads=[f"ps{b}"], writes=[f"{pre}v{c}"])
                    P.act(lambda e: e.activation(out=S["sigo"][:, c, :], in_=bank(b)[:, 128:256], func=AF.Sigmoid), reads=[f"ps{b}"], writes=[f"{pre}o{c}"])
                    yield
                if h + 1 < NH:
                    load_unit(1, h + 1, 1)

            def run_scan(h, filler):
                S = sets(h % 2)
                pre = S["pre"]
                qT, kT, ktok, vtok, sigo = S["qT"], S["kT"], S["ktok"], S["vtok"], S["sigo"]
                for (c0, n, pidx) in G["seqs"]:
                    for d in range(2):
                        if pidx is None:
                            P.dma("sp", f"st{d}", lambda e: e.dma_start(out=Cst[d][0][:, 0:128], in_=sC[d, h, :, :]), writes=[f"RX/C{d}0"])
                            P.dma("sp", f"st{d}", lambda e: e.dma_start(out=Cst[d][0][:, 128:129], in_=sn[d, h, :].rearrange("(p o) -> p o", o=1)),
                                  writes=[f"RX/C{d}0"])
                        else:
                            P.dve(lambda e: e.memset(Cst[d][0][:, 0:130], 0.0), writes=[f"RX/C{d}0"])

                    def chunk_of(d, i):
                        return c0 + i if d == 0 else c0 + n - 1 - i

                    def prefetch(d, i):
                        c = chunk_of(d, i)
                        lane = d * 8 + h
                        vi = d * 2 + (i % 2)
                        cb_ = 2 + d * 2 + (i % 2)
                        P.act(lambda e: e.activation(out=vwb[vi][:, 0:129], in_=vtok[:, c, 0:129], func=AF.Identity, scale=Wv[:, c, lane:lane + 1]),
                              reads=[f"{pre}v{c}", f"{pre}vones", "GG/W"], writes=[f"RX/vw{vi}"])
                        P.pe(lambda e: e.matmul(bank(cb_)[:, 0:129], lhsT=ktok[:, c, :], rhs=vwb[vi][:, 0:129], start=True, stop=True),
                             reads=[f"{pre}kt{c}", f"RX/vw{vi}"], writes=[f"ps{cb_}"])

                    for i in range(min(2, n)):
                        for d in range(2):
                            prefetch(d, i)
                    for i in range(n):
                        for d in range(2):
                            c = chunk_of(d, i)
                            lane = d * 8 + h
                            cur, nx = i % 2, (i + 1) % 2
                            cb_ = 2 + d * 2 + (i % 2)
                            P.act(lambda e: e.activation(out=CSB[:, d, c, 0:129], in_=Cst[d][cur][:, 0:129], func=AF.Identity, scale=GCv[:, c, lane:lane + 1]),
                                  reads=[f"RX/C{d}{cur}", "GG/GC"], writes=[f"R2/cs{d}_{c}"])
                            P.dve(lambda e: e.scalar_tensor_tensor(out=Cst[d][nx][:, 0:129], in0=Cst[d][cur][:, 0:129], scalar=GCv[:, c, lane:lane + 1],
                                                                   in1=bank(cb_)[:, 0:129], op0=ALU.mult, op1=ALU.add),
                                  reads=[f"RX/C{d}{cur}", f"ps{cb_}", "GG/GC"], writes=[f"RX/C{d}{nx}"])
                        if i + 2 < n:
                            for d in range(2):
                                prefetch(d, i + 2)

                    if pidx is not None:
                        fin = n % 2
                        for d in range(2):
                            P.dma("sp", f"oc{d}", lambda e: e.dma_start(out=nC[pidx, d, h, :, :], in_=Cst[d][fin][:, 0:128]), reads=[f"RX/C{d}{fin}"])
                            P.dma("sp", f"on{d}", lambda e: e.dma_start(out=nn[pidx, d, h, :].rearrange("(p o) -> p o", o=1), in_=Cst[d][fin][:, 128:129]),
                                  reads=[f"RX/C{d}{fin}"])
                    batch = []

                    def flush():
                        nb = len(batch)
                        if nb == 0:
                            return
                        P.act(lambda e: e.activation(out=FST[:, 8:8 + nb], in_=FST[:, 0:nb], func=AF.Sqrt, scale=1.0 / DH, bias=EPS),
                              reads=[f"fs{j}" for j in range(nb)], writes=["fstd"])
                        P.dve(lambda e: e.reciprocal(out=FST[:, 16:16 + nb], in_=FST[:, 8:8 + nb]), reads=["fstd"], writes=["frs"])
                        def stt_(j, c):
                            hb_ = j % 2
                            P.dve(lambda e: e.scalar_tensor_tensor(out=hnb[hb_], in0=hs8[j], scalar=FST[:, 16 + j:17 + j], in1=sigo[:, c, :], op0=ALU.mult, op1=ALU.mult),
                                  reads=[f"RX/hs{j}", "frs", f"{pre}o{c}"], writes=[f"RX/hn{hb_}"])
                        stt_(0, batch[0])
                        for j, c in enumerate(batch):
                            hb_ = j % 2
                            if j % 2 == 1:
                                filler(1)
                            tb = 0 + (j % 2)
                            P.pe(lambda e: e.transpose(out=bankb(tb)[:, 0:128], in_=hnb[hb_], identity=identb), reads=[f"RX/hn{hb_}", "CB"], writes=[f"ps{tb}"])
                            if j + 1 < nb:
                                stt_(j + 1, batch[j + 1])
                            P.dve(lambda e: e.tensor_scalar(out=R2[:, h, c * 128:(c + 1) * 128], in0=bankb(tb)[:, 0:128], scalar1=mnw[:, h:h + 1], scalar2=None, op0=ALU.mult),
                                  reads=[f"ps{tb}", "SPL"], writes=[mres(h, c)])
                        batch.clear()

                    def emit_sT(c):
                        sb_ = 2 + (c % 2)
                        P.pe(lambda e: e.matmul(bank(sb_)[:, 0:128], lhsT=kT[:, c * 128:(c + 1) * 128], rhs=qT[:, c * 128:(c + 1) * 128], start=True, stop=True),
                             reads=[f"{pre}k{c // 4}", f"{pre}q{c // 4}"], writes=[f"ps{sb_}"])

                    def emit_masks(c):
                        sb_ = 2 + (c % 2)
                        for d in range(2):
                            lane = d * 8 + h
                            msk = maskU if d == 0 else maskL
                            si_ = (c % 2) * 2 + d
                            P.dve(lambda e: e.scalar_tensor_tensor(out=sTm[si_], in0=bank(sb_)[:, 0:128], scalar=Wv[:, c, lane:lane + 1], in1=msk, op0=ALU.mult, op1=ALU.mult),
                                  reads=[f"ps{sb_}", "CB", "GG/W"], writes=[f"RX/sTm{si_}"])

                    emit_sT(c0)
                    emit_masks(c0)
                    for c in range(c0, c0 + n):
                        j = len(batch)
                        batch.append(c)
                        if c + 1 < c0 + n:
                            emit_sT(c + 1)
                            emit_masks(c + 1)
                        filler(1)
                        nbase = 4 + 2 * (c % 2)
                        for d in range(2):
                            si_ = (c % 2) * 2 + d
                            nb_ = nbase + d
                            P.pe(lambda e: e.matmul(bank(nb_)[:, 0:129], lhsT=sTm[si_], rhs=vtok[:, c, 0:129], start=True, stop=False),
                                 reads=[f"RX/sTm{si_}", f"{pre}v{c}", f"{pre}vones"], writes=[f"ps{nb_}"])
                            P.pe(lambda e: e.matmul(bank(nb_)[:, 0:129], lhsT=qT[:, c * 128:(c + 1) * 128], rhs=CSB[:, d, c, 0:129], start=False, stop=True),
                                 reads=[f"{pre}q{c // 4}", f"R2/cs{d}_{c}"], writes=[f"ps{nb_}"])
                        dc_ = stat_col()
                        if dc_ == 63:
                            dc_ = stat_col()
                        stat_col()
                        den2 = PS[:, nbase * 512:(nbase + 2) * 512].rearrange("p (b x) -> p b x", b=2)[:, :, 128:129]
                        P.act(lambda e: e.activation(out=STAT[:, dc_:dc_ + 2].unsqueeze(2), in_=den2, func=AF.Abs),
                              reads=[f"ps{nbase}", f"ps{nbase + 1}"], writes=[f"st{dc_}", f"st{dc_ + 1}"])
                        P.dve(lambda e: e.tensor_tensor(out=STAT[:, dc_:dc_ + 2], in0=STAT[:, dc_:dc_ + 2], in1=DLv[:, c, h:16:8], op=ALU.max),
                              reads=[f"st{dc_}", f"st{dc_ + 1}", "GG/DL"], writes=[f"st{dc_}", f"st{dc_ + 1}"])
                        P.dve(lambda e: e.reciprocal(out=STAT[:, dc_:dc_ + 2], in_=STAT[:, dc_:dc_ + 2]), reads=[f"st{dc_}", f"st{dc_ + 1}"], writes=[f"st{dc_}", f"st{dc_ + 1}"])
                        P.act(lambda e: e.activation(out=hs8[j], in_=bank(nbase)[:, 0:128], func=AF.Identity, scale=STAT[:, dc_:dc_ + 1]),
                              reads=[f"ps{nbase}", f"st{dc_}"], writes=[f"RX/hs{j}"])
                        P.dve(lambda e: e.scalar_tensor_tensor(out=hs8[j], in0=bank(nbase + 1)[:, 0:128], scalar=STAT[:, dc_ + 1:dc_ + 2], in1=hs8[j], op0=ALU.mult, op1=ALU.add),
                              reads=[f"ps{nbase + 1}", f"st{dc_ + 1}", f"RX/hs{j}"], writes=[f"RX/hs{j}"])
                        P.act(lambda e: e.activation(out=hnb[j % 2], in_=hs8[j], func=AF.Square, accum_out=FST[:, j:j + 1]),
                              reads=[f"RX/hs{j}"], writes=[f"RX/hn{j % 2}", f"fs{j}"])
                        filler(1)
                        if len(batch) == 8:
                            flush()
                    flush()

            def drain(g_, nsteps=None):
                k_ = 0
                for _ in g_:
                    k_ += 1
                    if nsteps is not None and k_ >= nsteps:
                        return False
                return True

            load_unit(0, heads[0], 0)
            load_unit(1, heads[0], 1)
            drain(gen_proj(heads[0]))
            for h in heads:
                gp = gen_proj(h + 1) if h + 1 < NH else iter(())
                run_scan(h, lambda k, gp=gp: drain(gp, k))
                drain(gp)
                if rr.get("conv_rate"):
                    emit_conv(rr["conv_rate"] * 3)
            P.fence("R2", lambda e: e.memset(SCR[:, 1:2], 0.0))

            if G["name"] == "P":
                dump("mixh", R2[:, 0:8, 0:512], [128, 8, 512], [mres(k, c) for k in range(8) for c in range(4)], BF16)
                checkpoint("heads")
            P.fence("RX", lambda e: e.memset(SCR[:, 1:2], 0.0))
            rows = 512 // L
            LP = L + 16
            PBUFS = [[RX[:, (s_ * 3 + i) * 640:(s_ * 3 + i) * 640 + rows * LP].rearrange("p (r l) -> p r l", l=LP) for i in range(3)] for s_ in range(2)]
            ppT4 = [[RXb[:, 7680 + (p_ * 2 + i) * 512:7680 + (p_ * 2 + i + 1) * 512] for i in range(2)] for p_ in range(2)]
            for i in range(6):
                P.dve(lambda e, i=i: e.memset(RX[:, i * 640:(i + 1) * 640], 0.0), writes=[f"RX/pb{i // 3}_{i % 3}"])
            for g in range(4):
                win = (2, 4, 8, 16)[g]
                half = win // 2

                def lf_u(slot, g=g):
                    if not first_pass:
                        scr_load(slot, 88 + g)
                        return
                    P.dma("pool", wchan(), lambda e, s=slot: e.dma_start(out=ring_w(s, 16, 256), in_=w_inv[:, :, 4128 + g * 256:4128 + (g + 1) * 256]),
                          writes=[f"ring{slot}"])
                    scr_store(slot, 88 + g)

                def cf_u(slot, g=g, win=win, half=half):
                    wv = ring_w(slot, 16, 256)
                    P.dma("sp", "pw", lambda e: e.dma_start(out=PW[:], in_=scr[93 + g][:, 0:512]), reads=[f"scr{93 + g}"], writes=["PW"])
                    P.dma("sp", "ic", lambda e: e.dma_start(out=INVC[:, 0:L], in_=G["invc"][:, g * L:(g + 1) * L].partition_broadcast(128)), writes=["INVC"])
                    PWv = PW[:].rearrange("p (c n) -> p c n", c=2)
                    def emit_hp(tt):
                        ppT = ppT4[tt % 2]
                        for dc in range(2):
                            b = 2 + dc
                            for cc in range(2):
                                P.pe(lambda e, b=b, dc=dc, cc=cc: e.matmul(bank(b), lhsT=PWv[:, cc, dc * 128:(dc + 1) * 128], rhs=ppT[cc], start=(cc == 0), stop=(cc == 1)),
                                     reads=["PW", f"RX/pp{tt % 2}_{cc}"], writes=[f"ps{b}"])
                            kk = 8 + g * 2 + dc
                            P.act(lambda e, b=b, kk=kk, tt=tt: e.activation(out=R2[:, kk, tt * 512:(tt + 1) * 512], in_=bank(b), func=AF.Identity, scale=psc[:, kk - 8:kk - 7]),
                                  reads=[f"ps{b}", "SPL"], writes=[mres(kk, tt * 4 + j) for j in range(4)])

                    for tt in range(ntt):
                        ppT = ppT4[tt % 2]
                        for cc in range(2):
                            b = (tt * 2 + cc) % 2
                            for k in range(16):
                                P.pe(lambda e, b=b, k=k, cc=cc, tt=tt: e.matmul(bank(b), lhsT=wv[:, k, cc * 128:(cc + 1) * 128], rhs=hT[:, k, tt * 512:(tt + 1) * 512],
                                                                            start=(k == 0), stop=(k == 15)),
                                     reads=[f"ring{slot}"] + [hres(k, tt * 4 + j) for j in range(4)], writes=[f"ps{b}"])
                            PBUF = PBUFS[cc]
                            pbn = lambda i_: f"RX/pb{cc}_{i_}"
                            P.act(lambda e, b=b: e.copy(out=PBUF[0][:, :, 8:8 + L], in_=bank(b).rearrange("p (r l) -> p r l", l=L)),
                                  reads=[f"ps{b}"], writes=[pbn(0)])
                            cur = 0
                            sh = 1
                            nxt = 1
                            while sh < win:
                                src = PBUF[cur]
                                dst = PBUF[nxt]
                                P.dve(lambda e, src=src, dst=dst, sh=sh: e.tensor_tensor(out=dst[:, :, 8:LP], in0=src[:, :, 8:LP], in1=src[:, :, 8 - sh:LP - sh], op=ALU.add),
                                      reads=[pbn(cur)], writes=[pbn(nxt)])
                                cur = nxt
                                nxt = 2 if cur == 1 else 1
                                sh *= 2
                            src = PBUF[cur]
                            dst = PBUF[nxt]
                            P.dve(lambda e, src=src, dst=dst, half=half: e.tensor_tensor(out=dst[:, :, 8:8 + L], in0=src[:, :, 8 + half - 1:8 + half - 1 + L],
                                                                                     in1=INVC[:, 0:L].unsqueeze(1).to_broadcast([128, rows, L]), op=ALU.mult),
                                  reads=[pbn(cur), "INVC"], writes=[pbn(nxt)])
                            P.dve(lambda e, dst=dst, cc=cc: e.tensor_tensor(out=ppT[cc].rearrange("p (r l) -> p r l", l=L), in0=dst[:, :, 8:8 + L], in1=PBUF[0][:, :, 8:8 + L], op=ALU.subtract),
                                  reads=[pbn(nxt), pbn(0)], writes=[f"RX/pp{tt % 2}_{cc}"])
                        if tt >= 1:
                            emit_hp(tt - 1)
                    emit_hp(ntt - 1)
                add_task(lf_u, cf_u)
            run_tasks()
            if G["name"] == "P":
                dump("mixp", R2[:, 8:16, 0:512], [128, 8, 512], [mres(k, c) for k in range(8, 16) for c in range(4)], BF16)
                checkpoint("pool")

            rr["conv_rate"] = 0
            P.fence("R1", lambda e: e.memset(SCR[:, 2:3], 0.0))
            P.fence("RX", lambda e: e.memset(SCR[:, 3:4], 0.0))
            P.fence("GG", lambda e: e.memset(SCR[:, 6:7], 0.0))
            for which in range(2):
                for k in range(16):
                    dsl = (which * 16 + k) % 2
                    P.dve(lambda e, which=which, k=k, dsl=dsl: e.tensor_scalar(out=DIAG[:, dsl * 128:(dsl + 1) * 128], in0=identf, scalar1=DERv[:, cond, 4 + which, k:k + 1],
                                                                             scalar2=None, op0=ALU.mult),
                          reads=["DER", "CF"], writes=[f"DIAG{dsl}"])
                    b = dsl
                    P.pe(lambda e, dsl=dsl, b=b: e.matmul(bank(b)[:, 0:128], lhsT=onesf, rhs=DIAG[:, dsl * 128:(dsl + 1) * 128], start=True, stop=True),
                         reads=[f"DIAG{dsl}", "CF"], writes=[f"ps{b}"])
                    P.act(lambda e, which=which, k=k, b=b: e.copy(out=GG[:, which * 2048 + k * 128:which * 2048 + (k + 1) * 128], in_=bank(b)[:, 0:128]),
                          reads=[f"ps{b}"], writes=["GG/GA" if which == 0 else "GG/GF"])
            GA = GG[:, 0:2048]
            GF = GG[:, 2048:4096]
            X1 = RX[:].rearrange("p (s d) -> p s d", s=4)
            R2f = R2.bitcast(F32)
            aT = R1[:].rearrange("p (j t) -> p j t", j=64)
            w_outv = w_out.rearrange("(k p) n -> p k n", p=128)
            w1v = w1.rearrange("(k p) n -> p k n", p=128)
            w2v = w2.rearrange("(j p) n -> p j n", p=128)
            for tt in range(ntt):
                r0 = base + tt * 512
                ssB = {}

                def tokc(sub, tt=tt):
                    return tt * 4 + sub

                for hf in range(2):
                    for u4 in range(4):
                        def lf_wo(slot, hf=hf, u4=u4):
                            if True:
                                scr_load(slot, hf * 4 + u4)
                            else:
                                P.dma("pool", wchan(), lambda e, s=slot: e.dma_start(out=ring_w(s, 4, 1024), in_=w_outv[:, u4 * 4:(u4 + 1) * 4, hf * 1024:(hf + 1) * 1024]),
                                      writes=[f"ring{slot}"])

                        def cf_wo(slot, hf=hf, u4=u4, tt=tt):
                            wv = ring_w(slot, 4, 1024)
                            for sub in range(4):
                                for kk in range(4):
                                    k = u4 * 4 + kk
                                    for cbb in range(2):
                                        b = sub * 2 + cbb
                                        P.pe(lambda e, b=b, k=k, kk=kk, cbb=cbb, sub=sub: e.matmul(bank(b), lhsT=R2[:, k, (tt * 4 + sub) * 128:(tt * 4 + sub + 1) * 128],
                                                                                            rhs=wv[:, kk, cbb * 512:(cbb + 1) * 512], start=(k == 0), stop=(k == 15)),
                                             reads=[f"ring{slot}", mres(k, tt * 4 + sub)], writes=[f"ps{b}"])
                            if u4 == 3:
                                for sub in range(4):
                                    pss = PS[:, sub * 1024:(sub + 1) * 1024]
                                    sc_ = stat_col()
                                    ssB[(sub, hf)] = sc_
                                    P.act(lambda e, pss=pss, sc_=sc_: e.activation(out=TS.bitcast(BF16)[:, 2048:3072], in_=pss, func=AF.Square, accum_out=STAT[:, sc_:sc_ + 1]),
                                          reads=[f"ps{sub * 2}", f"ps{sub * 2 + 1}"], writes=["rt0", "rt1", f"st{sc_}"])
                                    if hf == 0:
                                        P.dve(lambda e, pss=pss, sub=sub: e.tensor_copy(out=X1[:, sub, 0:1024], in_=pss),
                                              reads=[f"ps{sub * 2}", f"ps{sub * 2 + 1}"], writes=[f"RX/x{sub}a"])
                        add_task(lf_wo, cf_wo)
                for sub in range(4):
                    def lf_x(slot, sub=sub, r0=r0):
                        P.dma("sp", f"x{slot}", lambda e, s=slot: e.dma_start(out=ring_f(s)[:, 0:2048], in_=xs[r0 + sub * 128:r0 + (sub + 1) * 128, :]),
                              writes=[f"ring{slot}"])

                    def cf_x(slot, sub=sub, tt=tt):
                        a0, a1 = ssB[(sub, 0)], ssB[(sub, 1)]
                        P.dve(lambda e: e.tensor_tensor(out=STAT[:, a0:a0 + 1], in0=STAT[:, a0:a0 + 1], in1=STAT[:, a1:a1 + 1], op=ALU.add),
                              reads=[f"st{a0}", f"st{a1}"], writes=[f"st{a0}"])
                        rc_ = stat_col()
                        rstd_from(STAT[:, a0:a0 + 1], [f"st{a0}"], D, rc_)
                        pss = PS[:, sub * 1024:(sub + 1) * 1024]
                        P.dve(lambda e: e.scalar_tensor_tensor(out=X1[:, sub, 0:1024], in0=X1[:, sub, 0:1024], scalar=STAT[:, rc_:rc_ + 1], in1=GA[:, 0:1024],
                                                               op0=ALU.mult, op1=ALU.mult),
                              reads=[f"RX/x{sub}a", f"st{rc_}", "GG/GA"], writes=[f"RX/x{sub}a"])
                        P.dve(lambda e: e.scalar_tensor_tensor(out=X1[:, sub, 1024:2048], in0=pss, scalar=STAT[:, rc_:rc_ + 1], in1=GA[:, 1024:2048],
                                                               op0=ALU.mult, op1=ALU.mult),
                              reads=[f"ps{sub * 2}", f"ps{sub * 2 + 1}", f"st{rc_}", "GG/GA"], writes=[f"RX/x{sub}b"])
                        P.dve(lambda e: e.tensor_tensor(out=X1[:, sub, 0:1024], in0=X1[:, sub, 0:1024], in1=ring_f(slot)[:, 0:1024], op=ALU.add),
                              reads=[f"RX/x{sub}a", f"ring{slot}"], writes=[f"RX/x{sub}a"])
                        P.pool(lambda e: e.tensor_tensor(out=X1[:, sub, 1024:2048], in0=X1[:, sub, 1024:2048], in1=ring_f(slot)[:, 1024:2048], op=ALU.add),
                               reads=[f"RX/x{sub}b", f"ring{slot}"], writes=[f"RX/x{sub}b"])

                        def back(sb2):
                            c = tt * 4 + sb2
                            norm_transpose(X1[:, sb2, :], [f"RX/x{sb2}a", f"RX/x{sb2}b"], cond, 2, 3,
                                           lambda k, c=c: R2[:, k, c * 128:(c + 1) * 128], lambda k, c=c: mres(k, c), (sb2 * 2, sb2 * 2 + 1), xn_on_act=True,
                                           junk=(R1[:, 0:2048], [f"R1/a{j_}" for j_ in range(4)]))
                        if sub >= 1:
                            back(sub - 1)
                        if sub == 3:
                            back(3)
                    add_task(lf_x, cf_x)
                for u in range(32):
                    def lf_w1(slot, u=u):
                        if True:
                            scr_load(slot, 8 + u)
                        else:
                            P.dma("pool", wchan(), lambda e, s=slot: e.dma_start(out=ring_w(s, 16, 256), in_=w1v[:, :, u * 256:(u + 1) * 256]), writes=[f"ring{slot}"])

                    def cf_w1(slot, u=u, tt=tt):
                        wv = ring_w(slot, 16, 256)
                        for jj in range(2):
                            j = u * 2 + jj
                            b = j % 4
                            for k in range(16):
                                P.pe(lambda e, b=b, k=k, jj=jj: e.matmul(bank(b), lhsT=wv[:, k, jj * 128:(jj + 1) * 128], rhs=R2[:, k, tt * 512:(tt + 1) * 512],
                                                                         start=(k == 0), stop=(k == 15)),
                                     reads=[f"ring{slot}"] + [mres(k, tt * 4 + s_) for s_ in range(4)], writes=[f"ps{b}"])
                            rt = j % 2
                            P.act(lambda e, b=b, rt=rt: e.activation(out=rtmp[rt], in_=bank(b), func=AF.Relu), reads=[f"ps{b}"], writes=[f"rt{rt}"])
                            P.dve(lambda e, j=j, rt=rt: e.tensor_tensor(out=aT[:, j, :], in0=rtmp[rt], in1=rtmp[rt], op=ALU.mult), reads=[f"rt{rt}"], writes=[f"R1/a{j}"])
                    add_task(lf_w1, cf_w1)
                ss3 = {}
                for hf in range(2):
                    for u in range(16):
                        def lf_w2(slot, hf=hf, u=u):
                            if True:
                                scr_load(slot, 40 + hf * 16 + u)
                            else:
                                P.dma("pool", wchan(), lambda e, s=slot: e.dma_start(out=ring_w(s, 4, 1024), in_=w2v[:, u * 4:(u + 1) * 4, hf * 1024:(hf + 1) * 1024]),
                                      writes=[f"ring{slot}"])

                        def cf_w2(slot, hf=hf, u=u, tt=tt, r0=r0):
                            wv = ring_w(slot, 4, 1024)
                            for sub in range(4):
                                for jj in range(4):
                                    j = u * 4 + jj
                                    for cbb in range(2):
                                        b = sub * 2 + cbb
                                        P.pe(lambda e, b=b, j=j, jj=jj, cbb=cbb, sub=sub: e.matmul(bank(b), lhsT=aT[:, j, sub * 128:(sub + 1) * 128], rhs=wv[:, jj, cbb * 512:(cbb + 1) * 512],
                                                                                            start=(j == 0), stop=(j == 63)),
                                             reads=[f"ring{slot}", f"R1/a{j}"], writes=[f"ps{b}"])
                            if u == 15:
                                tmpp = [TS[:, 0:512], TS[:, 512:1024]]
                                YR1 = R1.bitcast(F32)[:, 0:4096].rearrange("p (s d) -> p s d", s=4)
                                if hf == 1:
                                    for sub in range(4):
                                        pss = PS[:, sub * 1024:(sub + 1) * 1024]
                                        P.act(lambda e: e.copy(out=YR1[:, sub, :], in_=pss), reads=[f"ps{sub * 2}", f"ps{sub * 2 + 1}"],
                                              writes=[f"R1/a{j_}" for j_ in range(sub * 4, sub * 4 + 4)])
                                for sub in range(4):
                                    pss = PS[:, sub * 1024:(sub + 1) * 1024] if hf == 0 else YR1[:, sub, :]
                                    sres = [f"ps{sub * 2}", f"ps{sub * 2 + 1}"] if hf == 0 else [f"R1/a{j_}" for j_ in range(sub * 4, sub * 4 + 4)]
                                    sc_ = stat_col()
                                    ss3[(sub, hf)] = sc_
                                    P.act(lambda e: e.activation(out=TS.bitcast(BF16)[:, 2048:3072], in_=pss, func=AF.Square, accum_out=STAT[:, sc_:sc_ + 1]),
                                          reads=sres, writes=["rt0", "rt1", f"st{sc_}"])
                                    if hf == 0:
                                        yr = R2f[:, sub * 4:(sub + 1) * 4, tt * 256:(tt + 1) * 256]
                                        yres = [mres(k, tt * 4 + s_) for k in range(sub * 4, sub * 4 + 4) for s_ in range(4)]
                                        P.dve(lambda e: e.tensor_copy(out=yr, in_=pss.rearrange("p (a b) -> p a b", a=4)),
                                              reads=[f"ps{sub * 2}", f"ps{sub * 2 + 1}"], writes=yres)
                                if hf == 1:
                                    rcs = {}
                                    for sub in range(4):
                                        a0, a1 = ss3[(sub, 0)], ss3[(sub, 1)]
                                        P.dve(lambda e: e.tensor_tensor(out=STAT[:, a0:a0 + 1], in0=STAT[:, a0:a0 + 1], in1=STAT[:, a1:a1 + 1], op=ALU.add),
                                              reads=[f"st{a0}", f"st{a1}"], writes=[f"st{a0}"])
                                        rcs[sub] = stat_col()
                                        rstd_from(STAT[:, a0:a0 + 1], [f"st{a0}"], D, rcs[sub])
                                    for sub in range(4):
                                        rc_ = rcs[sub]
                                        y1 = YR1[:, sub, :]
                                        y1res = [f"R1/a{j_}" for j_ in range(sub * 4, sub * 4 + 4)]
                                        yr = R2f[:, sub * 4:(sub + 1) * 4, tt * 256:(tt + 1) * 256]
                                        yres = [mres(k, tt * 4 + s_) for k in range(sub * 4, sub * 4 + 4) for s_ in range(4)]
                                        P.dve(lambda e: e.scalar_tensor_tensor(out=y1, in0=y1, scalar=STAT[:, rc_:rc_ + 1], in1=GF[:, 1024:2048], op0=ALU.mult, op1=ALU.mult),
                                              reads=y1res + [f"st{rc_}", "GG/GF"], writes=y1res)
                                        P.dve(lambda e: e.tensor_tensor(out=X1[:, sub, 1024:2048], in0=X1[:, sub, 1024:2048], in1=y1, op=ALU.add),
                                              reads=y1res + [f"RX/x{sub}b"], writes=[f"RX/x{sub}b"])
                                        P.dve(lambda e: e.scalar_tensor_tensor(out=yr, in0=yr, scalar=STAT[:, rc_:rc_ + 1], in1=GF[:, 0:1024].rearrange("p (a b) -> p a b", a=4),
                                                                               op0=ALU.mult, op1=ALU.mult),
                                              reads=yres + [f"st{rc_}", "GG/GF"], writes=yres)
                                        P.pool(lambda e: e.tensor_tensor(out=X1[:, sub, 0:1024].rearrange("p (a b) -> p a b", a=4), in0=X1[:, sub, 0:1024].rearrange("p (a b) -> p a b", a=4),
                                                                         in1=yr, op=ALU.add),
                                               reads=yres + [f"RX/x{sub}a"], writes=[f"RX/x{sub}a"])
                                        P.dma("pool", f"oy{sub % 2}", lambda e: e.dma_start(out=ys[r0 + sub * 128:r0 + (sub + 1) * 128, :], in_=X1[:, sub, :]),
                                              reads=[f"RX/x{sub}a", f"RX/x{sub}b"])
                        add_task(lf_w2, cf_w2)
                run_tasks()
            P.fence("RX", lambda e: e.memset(SCR[:, 4:5], 0.0))
            P.fence("R2", lambda e: e.memset(SCR[:, 5:6], 0.0))
            if G["name"] == "P":
                checkpoint("BP")
            P.fence("GG", lambda e: e.memset(SCR[:, 7:8], 0.0))


    try:
        _body()
    except _Stop:
        pass
    P.emit()
    return nc, es, P


def _invcnt(L):
    out = np.zeros((4, L), np.float32)
    t = np.arange(L)
    for g, w in enumerate((2, 4, 8, 16)):
        lo = np.maximum(t - w // 2, 0)
        hi = np.minimum(t + w // 2, L)
        out[g] = 1.0 / (hi - lo).astype(np.float32)
    return out.reshape(1, 4 * L)


_CACHE = {}


def kernel(x_prompt, x_sample, c, state_C, state_n, state_m, c_ctx, w_in, gate_bias, mlstm_norm_w,
           pool_w, pool_scale, w_out, ada_w, ada_b, norm_w, w1, w2):
    f = np.float32
    x_prompt = np.asarray(x_prompt, f); x_sample = np.asarray(x_sample, f)
    c = np.asarray(c, f); c_ctx = np.asarray(c_ctx, f)
    state_C = np.asarray(state_C, f); state_n = np.asarray(state_n, f); state_m = np.asarray(state_m, f)
    w_in2 = np.ascontiguousarray(np.asarray(w_in, f)[0])
    w_out2 = np.ascontiguousarray(np.asarray(w_out, f)[0])
    ada_w2 = np.ascontiguousarray(np.asarray(ada_w, f)[0])
    w1_2 = np.ascontiguousarray(np.asarray(w1, f)[0])
    w2_2 = np.ascontiguousarray(np.asarray(w2, f)[0])
    pool_w2 = np.ascontiguousarray(np.asarray(pool_w, f)[0])
    gb = np.ascontiguousarray(np.asarray(gate_bias, f)[0].reshape(1, 32))
    adab_pl = np.asarray(ada_b, f)[0].reshape(96, 128).T
    nw_pl = np.asarray(norm_w, f)[0].reshape(4, 16, 128).transpose(2, 0, 1).reshape(128, 64)
    mnw_pl = np.asarray(mlstm_norm_w, f)[0].reshape(8, 128).T
    psc_pl = np.asarray(pool_scale, f)[0].reshape(8, 128).T
    eye = np.eye(128, dtype=f)
    triU = np.triu(np.ones((128, 128), f))
    triL = np.tril(np.ones((128, 128), f))
    cb16 = np.concatenate([eye, triU, triL], axis=1).astype(ml_dtypes.bfloat16)
    cf32 = np.concatenate([eye, triU, triL, np.ones((128, 128), f)], axis=1)
    invc_s = _invcnt(64)
    invc_p = _invcnt(256)
    in_maps = []
    for b in range(8):
        xs = np.concatenate([x_sample[b], x_prompt[2 * b], x_prompt[2 * b + 1]], axis=0)
        cond = np.stack([c[b], c_ctx], axis=0)
        cond_pl = cond.reshape(2, 16, 128).transpose(2, 1, 0).reshape(128, 32)
        smallpl = np.ascontiguousarray(np.concatenate([cond_pl, adab_pl, nw_pl, mnw_pl, psc_pl], axis=1), dtype=f)
        in_maps.append({
            "xs": np.ascontiguousarray(xs), "sC": np.ascontiguousarray(state_C[b, 0]), "sn": np.ascontiguousarray(state_n[b, 0]),
            "sm": np.ascontiguousarray(state_m[b, 0].reshape(1, 16)), "w_in": w_in2, "gb": gb, "pool_w": pool_w2, "w_out": w_out2,
            "ada_w": ada_w2, "w1": w1_2, "w2": w2_2, "smallpl": smallpl, "cb16": cb16, "cf32": cf32, "invc_s": invc_s, "invc_p": invc_p,
        })
    if "nc" not in _CACHE:
        nc, es, P = build_program()
        _CACHE["nc"] = nc
        _CACHE["es"] = es
    nc = _CACHE["nc"]
    res = run_bass_kernel_spmd(nc, in_maps, core_ids=list(range(8)))
    R = res.results
    y_sample = np.stack([R[b]["ys"][0:2048] for b in range(8)], axis=0)
    y_prompt = np.stack([R[b]["ys"][2048 + 256 * j:2048 + 256 * (j + 1)] for b in range(8) for j in range(2)], axis=0)
    new_C = np.stack([R[b]["nC"][j] for b in range(8) for j in range(2)], axis=0)[:, None]
    new_n = np.stack([R[b]["nn"][j] for b in range(8) for j in range(2)], axis=0)[:, None]
    new_m = np.stack([R[b]["nm"][j].reshape(2, 8) for b in range(8) for j in range(2)], axis=0)[:, None]
    return (y_prompt.astype(f), y_sample.astype(f), new_C.astype(f), new_n.astype(f), new_m.astype(f))
```
